# Optimizing a Trainium2 kernel written in Bass

```python
import math
import jax, jax.numpy as jnp
from jax import lax
import numpy as np

D_MODEL = 1024
BATCH = 16
SEQ = 2048
DEPTH = 2

N_META = 16
MIX_WIDTH = D_MODEL
FOX_WIDTH = MIX_WIDTH // 2
RET_WIDTH = MIX_WIDTH - FOX_WIDTH
FOX_HEAD_DIM = 64
FOX_HEADS = FOX_WIDTH // FOX_HEAD_DIM
RET_HEADS = 4
RET_HEAD_DIM = RET_WIDTH // RET_HEADS
D_FF = -(-8 * D_MODEL // (3 * 256)) * 256
BLOCK_Q = 128
RET_CHUNK = 128
ROPE_BASE = 10000.0
EPS = 1e-6
FGATE_BIAS_MEAN = 3.0
RET_LOG_GAMMA = tuple(math.log(1.0 - 2.0 ** (-5 - h)) for h in range(RET_HEADS))
IN_SIZES = (FOX_WIDTH, FOX_WIDTH, FOX_WIDTH, FOX_HEADS, RET_WIDTH, RET_WIDTH, RET_WIDTH, RET_WIDTH)
IN_COLS = sum(IN_SIZES)
SPLIT_POINTS = tuple(int(v) for v in np.cumsum(IN_SIZES)[:-1])

kernel_name = "hymba_fox_retnet_hybrid"


def rmsnorm(x, g):
    x32 = x.astype(jnp.float32)
    y = x32 * lax.rsqrt(jnp.mean(x32 * x32, axis=-1, keepdims=True) + EPS)
    return (y * g.astype(jnp.float32)).astype(x.dtype)


def rotate(x, cos, sin):
    half = x.shape[-1] // 2
    x1, x2 = x[..., :half], x[..., half:]
    c = cos[None, :, None, :].astype(x.dtype)
    s = sin[None, :, None, :].astype(x.dtype)
    return jnp.concatenate([x1 * c - x2 * s, x1 * s + x2 * c], axis=-1)


def fox_attention(q, k, v, logf):
    B, L, H, Dh = q.shape
    scale = Dh ** -0.5
    c = jnp.cumsum(logf, axis=1).transpose(0, 2, 1)
    qh = q.transpose(0, 2, 1, 3)
    kh = k.transpose(0, 2, 1, 3)
    vh = v.transpose(0, 2, 1, 3)
    pos = jnp.arange(L)

    def attend(q_blk, c_q, q_pos, k_, v_, c_k, k_pos):
        s = jnp.einsum('bhqd,bhkd->bhqk', q_blk, k_).astype(jnp.float32) * scale
        s = s + c_q[..., :, None] - c_k[..., None, :]
        s = jnp.where(k_pos[None, :] <= q_pos[:, None], s, -jnp.inf)
        p = jax.nn.softmax(s, axis=-1)
        return jnp.einsum('bhqk,bhkd->bhqd', p.astype(v_.dtype), v_)

    meta_out = attend(qh[:, :, :N_META], c[:, :, :N_META], pos[:N_META],
                      kh[:, :, :N_META], vh[:, :, :N_META], c[:, :, :N_META], pos[:N_META])
    n_blk = (L - N_META) // BLOCK_Q
    q_real = qh[:, :, N_META:].reshape(B, H, n_blk, BLOCK_Q, Dh).transpose(2, 0, 1, 3, 4)
    c_real = c[:, :, N_META:].reshape(B, H, n_blk, BLOCK_Q).transpose(2, 0, 1, 3)
    pos_real = pos[N_META:].reshape(n_blk, BLOCK_Q)
    real_out = lax.map(lambda a: attend(a[0], a[1], a[2], kh, vh, c, pos),
                       (q_real, c_real, pos_real))
    real_out = real_out.transpose(1, 2, 0, 3, 4).reshape(B, H, L - N_META, Dh)
    out = jnp.concatenate([meta_out, real_out], axis=2)
    return out.transpose(0, 2, 1, 3)


def retention_chunkwise(q, k, v):
    B, L, H, Dh = q.shape
    log_g = jnp.array(RET_LOG_GAMMA, dtype=jnp.float32)

    def intra(qc, kc, vc, n):
        idx = jnp.arange(n)
        diff = idx[:, None] - idx[None, :]
        dec = jnp.where(diff >= 0,
                        jnp.exp(jnp.maximum(diff, 0)[None].astype(jnp.float32) * log_g[:, None, None]),
                        0.0)
        s = jnp.einsum('bnhd,bmhd->bhnm', qc, kc).astype(jnp.float32) * dec
        return jnp.einsum('bhnm,bmhd->bnhd', s, vc.astype(jnp.float32))

    qm, km, vm = q[:, :N_META], k[:, :N_META], v[:, :N_META]
    o_meta = intra(qm, km, vm, N_META)
    m_idx = jnp.arange(N_META, dtype=jnp.float32)
    k_dec_meta = jnp.exp((N_META - 1 - m_idx)[:, None] * log_g[None, :])
    state0 = jnp.einsum('bmhd,bmhe->bhde', km.astype(jnp.float32) * k_dec_meta[None, :, :, None],
                        vm.astype(jnp.float32))

    C = RET_CHUNK
    n_chunks = (L - N_META) // C
    j_idx = jnp.arange(C, dtype=jnp.float32)
    q_dec = jnp.exp((j_idx + 1.0)[:, None] * log_g[None, :])[None, :, :, None]
    k_dec = jnp.exp((C - 1.0 - j_idx)[:, None] * log_g[None, :])[None, :, :, None]
    chunk_dec = jnp.exp(C * log_g)[None, :, None, None]

    def to_chunks(t):
        return t[:, N_META:].reshape(B, n_chunks, C, H, Dh).transpose(1, 0, 2, 3, 4)

    def step(state, xs):
        qc, kc, vc = xs
        o = intra(qc, kc, vc, C) + jnp.einsum('bjhd,bhde->bjhe', qc.astype(jnp.float32) * q_dec, state)
        state = chunk_dec * state + jnp.einsum('bjhd,bjhe->bhde', kc.astype(jnp.float32) * k_dec,
                                               vc.astype(jnp.float32))
        return state, o

    _, o_real = lax.scan(step, state0, (to_chunks(q), to_chunks(k), to_chunks(v)))
    o_real = o_real.transpose(1, 0, 2, 3, 4).reshape(B, L - N_META, H, Dh)
    return jnp.concatenate([o_meta, o_real], axis=1)


def head_groupnorm(o, g):
    mu = jnp.mean(o, axis=-1, keepdims=True)
    var = jnp.mean(jnp.square(o - mu), axis=-1, keepdims=True)
    y = (o - mu) * lax.rsqrt(var + EPS)
    B, L, H, Dh = o.shape
    return y.reshape(B, L, H * Dh) * g.astype(jnp.float32)


def hybrid_layer(h, attn_norm_g, w_in, b_fgate, ret_gn_g, w_out,
                 ffn_norm_g, w_gate, w_up, w_down, cos, sin):
    B, L, _ = h.shape
    xn = rmsnorm(h, attn_norm_g)
    proj = jnp.einsum('bld,dc->blc', xn, w_in)
    fq, fk, fv, flog, rq, rk, rv, rg = jnp.split(proj, SPLIT_POINTS, axis=-1)

    logf = jax.nn.log_sigmoid(flog.astype(jnp.float32) + b_fgate.astype(jnp.float32))
    fshape = (B, L, FOX_HEADS, FOX_HEAD_DIM)
    fox_out = fox_attention(fq.reshape(fshape), fk.reshape(fshape), fv.reshape(fshape), logf)
    fox_out = fox_out.reshape(B, L, FOX_WIDTH)

    rshape = (B, L, RET_HEADS, RET_HEAD_DIM)
    rq_ = rotate(rq.reshape(rshape), cos, sin)
    rk_ = rotate(rk.reshape(rshape), cos, sin) * (RET_HEAD_DIM ** -0.5)
    ret = retention_chunkwise(rq_, rk_, rv.reshape(rshape))
    ret = head_groupnorm(ret, ret_gn_g).astype(h.dtype)
    ret_out = jax.nn.silu(rg) * ret

    mix = jnp.concatenate([fox_out, ret_out], axis=-1)
    h = h + jnp.einsum('blc,cd->bld', mix, w_out)

    hn = rmsnorm(h, ffn_norm_g)
    ff = jax.nn.silu(jnp.einsum('bld,df->blf', hn, w_gate)) * jnp.einsum('bld,df->blf', hn, w_up)
    return h + jnp.einsum('blf,fd->bld', ff, w_down)


def setup_inputs(seed: int = 0) -> dict:
    key = jax.random.key(seed)
    ks = jax.random.split(key, 12)
    f32 = jnp.float32
    x = jax.random.normal(ks[0], (BATCH, SEQ, D_MODEL), f32)
    meta_tokens = jax.random.normal(ks[1], (N_META, D_MODEL), f32)
    attn_norm = 1.0 + 0.02 * jax.random.normal(ks[2], (DEPTH, D_MODEL), f32)
    w_in = jax.random.normal(ks[3], (DEPTH, D_MODEL, IN_COLS), f32) * D_MODEL ** -0.5
    b_fgate = FGATE_BIAS_MEAN + 0.5 * jax.random.normal(ks[4], (DEPTH, FOX_HEADS), f32)
    ret_gn = 1.0 + 0.02 * jax.random.normal(ks[5], (DEPTH, RET_WIDTH), f32)
    w_out = jax.random.normal(ks[6], (DEPTH, MIX_WIDTH, D_MODEL), f32) * MIX_WIDTH ** -0.5
    ffn_norm = 1.0 + 0.02 * jax.random.normal(ks[7], (DEPTH, D_MODEL), f32)
    w_gate = jax.random.normal(ks[8], (DEPTH, D_MODEL, D_FF), f32) * D_MODEL ** -0.5
    w_up = jax.random.normal(ks[9], (DEPTH, D_MODEL, D_FF), f32) * D_MODEL ** -0.5
    w_down = jax.random.normal(ks[10], (DEPTH, D_FF, D_MODEL), f32) * D_FF ** -0.5
    final_norm = 1.0 + 0.02 * jax.random.normal(ks[11], (D_MODEL,), f32)
    return {"x": x, "meta_tokens": meta_tokens, "attn_norm": attn_norm, "w_in": w_in,
            "b_fgate": b_fgate, "ret_gn": ret_gn, "w_out": w_out, "ffn_norm": ffn_norm,
            "w_gate": w_gate, "w_up": w_up, "w_down": w_down, "final_norm": final_norm}


def reference(x, meta_tokens, attn_norm, w_in, b_fgate, ret_gn, w_out, ffn_norm,
              w_gate, w_up, w_down, final_norm):
    B = x.shape[0]
    meta = jnp.broadcast_to(meta_tokens[None].astype(x.dtype), (B, N_META, D_MODEL))
    h = jnp.concatenate([meta, x], axis=1)
    L = h.shape[1]
    inv_freq = ROPE_BASE ** (-jnp.arange(0, RET_HEAD_DIM, 2, dtype=jnp.float32) / RET_HEAD_DIM)
    ang = jnp.arange(L, dtype=jnp.float32)[:, None] * inv_freq[None, :]
    cos, sin = jnp.cos(ang), jnp.sin(ang)
    for i in range(DEPTH):
        h = hybrid_layer(h, attn_norm[i], w_in[i], b_fgate[i], ret_gn[i], w_out[i],
                         ffn_norm[i], w_gate[i], w_up[i], w_down[i], cos, sin)
    h = rmsnorm(h, final_norm)
    return h[:, N_META:]
```

```python
import math
from contextlib import ExitStack

import numpy as np
import concourse.bass as bass
import concourse.mybir as mybir
from concourse.bass_utils import run_bass_kernel_spmd

F32 = mybir.dt.float32
BF16 = mybir.dt.bfloat16
AF = mybir.ActivationFunctionType
ALU = mybir.AluOpType
AX = mybir.AxisListType

NCORES = 8
D = 1024
SEQ = 2048
NMETA = 16
L = SEQ + NMETA
NT = 17
DFF = 2816
INC = 3592
EPS = 1e-6
NTOK = [16] + [128] * 16
POS0 = [0] + [16 + 128 * i for i in range(16)]
GROUPS = [[0, 1, 2, 3, 4], [5, 6, 7, 8], [9, 10, 11, 12], [13, 14, 15, 16]]
GAMMA = [1.0 - 2.0 ** (-5 - h) for h in range(4)]
NSLOT = 4
SLOT_EL = 4096

MODE = "fused"


class Sched:
    ENG = ("pe", "act", "dve", "pool", "sp")

    def __init__(self, nc):
        self.nc = nc
        self.ops = {e: [] for e in self.ENG}
        self.state = {}
        self.waited = {e: {} for e in self.ENG}
        self.dma_cnt = {}

    def _filter(self, eng, deps):
        out = []
        wd = self.waited[eng]
        for t in deps:
            if t[0] == "eng":
                _, e, idx = t
                if e == eng and e == "pe":
                    continue
                if wd.get(e, -1) >= idx:
                    continue
                wd[e] = idx
                self.ops[e][idx]["signal"] = True
                out.append(t)
            else:
                _, name, val = t
                if wd.get(name, -1) >= val:
                    continue
                wd[name] = val
                out.append(t)
        return out

    def _deps(self, eng, reads, writes):
        deps = []
        for k in reads:
            st = self.state.get(k)
            if st and st["w"] is not None:
                deps.append(st["w"])
        for k in writes:
            st = self.state.get(k)
            if st:
                if st["w"] is not None:
                    deps.append(st["w"])
                deps.extend(st["r"])
        return self._filter(eng, deps)

    def _commit(self, tok, reads, writes):
        for k in reads:
            st = self.state.setdefault(k, {"w": None, "r": []})
            if tok[0] == "eng":
                st["r"] = [t for t in st["r"] if not (t[0] == "eng" and t[1] == tok[1])]
            st["r"].append(tok)
        for k in writes:
            self.state[k] = {"w": tok, "r": []}

    def op(self, eng, fn, reads=(), writes=()):
        waits = self._deps(eng, reads, writes)
        idx = len(self.ops[eng])
        self.ops[eng].append({"fn": fn, "waits": waits, "signal": False, "dma": None})
        self._commit(("eng", eng, idx), reads, writes)

    def call(self, eng, name, *args, R=(), W=(), **kwargs):
        self.op(eng, (lambda e: getattr(e, name)(*args, **kwargs)), reads=R, writes=W)

    def dma(self, q, sem, out_ap, in_ap, reads=(), writes=()):
        waits = self._deps(q, reads, writes)
        self.dma_cnt[sem] = self.dma_cnt.get(sem, 0) + 16
        val = self.dma_cnt[sem]
        self.ops[q].append({"fn": (lambda e: e.dma_start(out=out_ap, in_=in_ap)),
                            "waits": waits, "signal": False, "dma": sem})
        self._commit(("sem", sem, val), reads, writes)

    def barrier(self):
        toks = []
        for e in self.ENG:
            for idx in range(len(self.ops[e]) - 1, -1, -1):
                o = self.ops[e][idx]
                if o["dma"] is None and o["fn"] is not None:
                    toks.append(("eng", e, idx))
                    break
        for s, v in self.dma_cnt.items():
            toks.append(("sem", s, v))
        for e in self.ENG:
            waits = self._filter(e, [t for t in toks if not (t[0] == "eng" and t[1] == e)])
            self.ops[e].append({"fn": None, "waits": waits, "signal": False, "dma": None})

    def final_wait(self, eng, sems):
        waits = [("sem", s, self.dma_cnt[s]) for s in sems if s in self.dma_cnt]
        self.ops[eng].append({"fn": None, "waits": waits, "signal": False, "dma": None})

    def emit(self, stack):
        nc = self.nc
        esem = {e: stack.enter_context(nc.semaphore("s_" + e)) for e in self.ENG}
        dsem = {s: stack.enter_context(nc.semaphore("d_" + s)) for s in self.dma_cnt}
        block = stack.enter_context(nc.Block())
        sigval = {}
        for e in self.ENG:
            c = 0
            for i, o in enumerate(self.ops[e]):
                if o["signal"]:
                    c += 1
                    sigval[(e, i)] = c

        def run(e, engobj):
            for i, o in enumerate(self.ops[e]):
                for t in o["waits"]:
                    if t[0] == "eng":
                        engobj.wait_ge(esem[t[1]], sigval[(t[1], t[2])])
                    else:
                        engobj.wait_ge(dsem[t[1]], t[2])
                if o["fn"] is None:
                    continue
                ins = o["fn"](engobj)
                if o["dma"] is not None:
                    ins.then_inc(dsem[o["dma"]], 16)
                elif o["signal"]:
                    ins.then_inc(esem[e], 1)

        @block.tensor
        def _(pe):
            run("pe", pe)

        @block.scalar
        def _(act):
            run("act", act)

        @block.vector
        def _(dve):
            run("dve", dve)

        @block.gpsimd
        def _(pool):
            run("pool", pool)

        @block.sync
        def _(sp):
            run("sp", sp)


def make_consts():
    c = {}
    c["c_identf"] = np.eye(128, dtype=np.float32)
    s = np.arange(128)[:, None]
    t = np.arange(128)[None, :]
    c["c_U"] = (s <= t).astype(np.float32)
    E = np.zeros((128, 2, 128), np.float32)
    E[15, 0, :] = 1.0
    E[127, 1, :] = 1.0
    c["c_E"] = E
    c["c_maskneg"] = np.where(s <= t, 0.0, -30000.0).astype(np.float32)
    c["c_caus01"] = (s <= t).astype(np.float32)
    sel = np.zeros((128, 8, 128), np.float32)
    for h in range(8):
        sel[h, h, :] = 1.0
    c["c_sel"] = sel
    c["c_onesf"] = np.ones((128, 64), np.float32)
    inv_freq = (10000.0 ** (-np.arange(0, 128, 2, dtype=np.float32) / 128.0)).astype(np.float32)
    pos = np.zeros((128, NT), np.float32)
    for ti in range(NT):
        pos[:, ti] = POS0[ti] + np.arange(128)
    ang = (pos[:, :, None].astype(np.float32) * inv_freq[None, None, :]).astype(np.float32)
    c["c_cos"] = np.cos(ang).astype(np.float32)
    c["c_sin"] = np.sin(ang).astype(np.float32)
    j = np.arange(128, dtype=np.float64)
    lg = np.array([math.log(g) for g in GAMMA])
    c["c_qdec"] = np.exp((j[:, None] + 1.0) * lg[None, :]).astype(np.float32)
    c["c_kdec"] = (np.exp(-(j[:, None] + 1.0) * lg[None, :]) * (128.0 ** -0.5)).astype(np.float32)
    gd = np.zeros((128, 2, 4), np.float32)
    gd[:, 0, :] = np.exp(16.0 * lg)[None, :]
    gd[:, 1, :] = np.exp(128.0 * lg)[None, :]
    c["c_gdecn"] = gd
    return c


CONST_SHAPES = {
    "c_identf": [128, 128], "c_U": [128, 128], "c_E": [128, 2, 128], "c_maskneg": [128, 128],
    "c_caus01": [128, 128], "c_sel": [128, 8, 128], "c_onesf": [128, 64],
    "c_cos": [128, NT, 64], "c_sin": [128, NT, 64], "c_qdec": [128, 4], "c_kdec": [128, 4],
    "c_gdecn": [128, 2, 4],
}


def build(layer_ids, first, last):
    nc = bass.Bass("TRN2", target_bir_lowering=False)
    NL = 2

    def din(name, shape):
        return nc.dram_tensor(name, shape, F32, kind="ExternalInput").ap()

    if first:
        x_d = din("x", [2, SEQ, D])
        meta_d = din("meta", [NMETA, D])
    else:
        hin_d = din("hin", [2, L, D])
    attn_norm_d = din("attn_norm", [NL, D])
    w_in_d = din("w_in", [NL, D, INC])
    b_fgate_d = din("b_fgate", [NL, 8])
    ret_gn_d = din("ret_gn", [NL, 512])
    w_out_d = din("w_out", [NL, D, D])
    ffn_norm_d = din("ffn_norm", [NL, D])
    w_gate_d = din("w_gate", [NL, D, DFF])
    w_up_d = din("w_up", [NL, D, DFF])
    w_down_d = din("w_down", [NL, DFF, D])
    final_norm_d = din("final_norm", [1, D])
    cd = {k: din(k, v) for k, v in CONST_SHAPES.items()}
    if last:
        out_d = nc.dram_tensor("out", [2, SEQ, D], F32, kind="ExternalOutput").ap()
    else:
        out_d = nc.dram_tensor("hout", [2, L, D], F32, kind="ExternalOutput").ap()

    with ExitStack() as st:
        def sb(name, shape, dt):
            return st.enter_context(nc.sbuf_tensor(name, shape, dt))

        def ps(name, shape, dt):
            return st.enter_context(nc.psum_tensor(name, shape, dt))

        S = Sched(nc)

        H = sb("H", [128, NT, D], F32)
        ring = [sb(f"ring{i}", [128, SLOT_EL], BF16) for i in range(NSLOT)]
        identf = sb("identf", [128, 128], F32)
        identb = sb("identb", [128, 128], BF16)
        Umat = sb("Umat", [128, 128], F32)
        Esel = sb("Esel", [128, 2, 128], F32)
        maskneg = sb("maskneg", [128, 128], BF16)
        caus01 = sb("caus01", [128, 128], F32)
        selb = sb("selb", [128, 8, 128], BF16)
        onesf = sb("onesf", [128, 64], F32)
        cosT = sb("cosT", [128, NT, 64], F32)
        sinT = sb("sinT", [128, NT, 64], F32)
        qdec = sb("qdec", [128, 4], F32)
        kdec = sb("kdec", [128, 4], F32)
        gdecn = sb("gdecn", [128, 2, 4], F32)
        epsT = sb("epsT", [128, 1], F32)
        oneT = sb("oneT", [128, 1], F32)
        gB = sb("gB", [128, D], F32)
        gnB = sb("gnB", [128, 512], F32)
        bfB = sb("bfB", [128, 8], F32)
        wlog = sb("wlog", [128, 8, 8], BF16)
        negc = sb("negc", [128, NT, 8], F32)
        crowT = sb("crowT", [128, 528], BF16)
        S32 = sb("S32", [128, 4, 128], F32)
        Sbf = sb("Sbf", [128, 4, 128], BF16)
        ss = [sb(f"ss{i}", [128, 1], F32) for i in range(2)]
        rstd = [sb(f"rstd{i}", [128, 1], F32) for i in range(2)]
        rstd2 = sb("rstd2", [128, NT], F32)
        ssg = sb("ssg", [128, 4], F32)
        rstdg = sb("rstdg", [128, 4], F32)
        tl5 = sb("tl5", [128, 5, 8], F32)
        te5 = sb("te5", [128, 5, 8], F32)
        tl = sb("tl", [128, 8], F32)
        te = sb("te", [128, 8], F32)
        spv = sb("spv", [128, 5, 8], F32)
        st4 = sb("st4", [128, 6, 4], F32)
        tmpA = sb("tmpA", [128, 512], F32)
        tmpB = sb("tmpB", [128, 512], F32)
        tmpC = sb("tmpC", [128, 512], F32)
        ARENA_EL = 38600
        A = sb("arena", [128, ARENA_EL], BF16)
        off = [0]

        def carve(n):
            o = off[0]
            off[0] += n
            assert off[0] <= ARENA_EL, off[0]
            return A[:, o:o + n]

        Kc = carve(4 * L).rearrange("p (c n) -> p c n", c=4)
        vc_off = off[0]
        Vc = carve(NT * 520).rearrange("p (t h e) -> p t h e", t=NT, h=8)
        xnT = carve(8 * 528).rearrange("p (c n) -> p c n", c=8)
        qT = carve(8 * 528).rearrange("p (c n) -> p c n", c=8)
        mixR = carve(4 * 528).rearrange("p (c n) -> p c n", c=4)
        mixF = carve(8 * 528).rearrange("p (c n) -> p c n", c=8)
        PTb = [carve(512) for _ in range(2)]
        xn = [carve(1024) for _ in range(2)]
        qp = carve(512)
        kp = carve(512)
        rvb = carve(512)
        rqkT = carve(8 * 128).rearrange("p (c n) -> p c n", c=8)
        mT = carve(512).rearrange("p (c n) -> p c n", c=4)
        ro = carve(512)
        ROB = [ro, ro]
        ROK = ["ro", "ro"]
        p1_end = off[0]
        off[0] = 0
        hnT = carve(8 * L).rearrange("p (c n) -> p c n", c=8)
        actT = [carve(4 * 512).rearrange("p (c n) -> p c n", c=4) for _ in range(2)]
        xn2 = [carve(1024) for _ in range(2)]
        p2_end = off[0]
        print("arena p1", p1_end, "p2", p2_end, "sbuf remaining", nc.sbuf_bytes_remaining)

        P = [ps(f"P{i}", [128, 512], F32) for i in (0, 1)]
        PTR = ps("PTR", [128, 8, 128], BF16)
        SC = [ps(f"SC{i}", [128, 512], F32) for i in range(2)]
        OTp = ps("OT", [128, 512], F32)
        P6 = ps("P6", [128, 512], F32)
        P7 = ps("P7", [128, 512], F32)
        YB = [OTp, P6, P7]

        cnt = {"u": 0, "sc": 0, "y": 0, "w": 0, "n": 0, "ro": 0}

        def ld(dst, src, key, q="sp"):
            S.dma(q, "c_" + key, dst, src, writes=[key])

        ld(identf[:], cd["c_identf"], "identf")
        ld(Umat[:], cd["c_U"], "Umat")
        ld(Esel[:], cd["c_E"], "Esel")
        ld(caus01[:], cd["c_caus01"], "caus01")
        ld(onesf[:], cd["c_onesf"], "onesf")
        ld(cosT[:], cd["c_cos"], "cosT")
        ld(sinT[:], cd["c_sin"], "sinT")
        ld(qdec[:], cd["c_qdec"], "qdec")
        ld(kdec[:], cd["c_kdec"], "kdec")
        ld(gdecn[:], cd["c_gdecn"], "gdecn")
        ld(identb[:], cd["c_identf"], "identb", q="pool")
        ld(maskneg[:], cd["c_maskneg"], "maskneg", q="pool")
        ld(selb[:], cd["c_sel"], "selb", q="pool")
        S.call("dve", "memset", crowT[:], 0.0, W=["crowT"])
        S.call("dve", "memset", A[:, :], 0.0,
               W=["Kc", "Vc", "Vones", "xnT", "qT", "mixR", "mixF", "PT0", "PT1", "xn0", "xn1", "qp", "kp", "rvb",
                  "rqkT", "mT", "ro", "hnT0", "hnT1", "hnT2", "hnT3", "hnT4", "actT0", "actT1", "xn2_0", "xn2_1"])
        S.call("dve", "memset", epsT[:], EPS,
                   W=["epsT"])
        S.call("dve", "memset", oneT[:], 1.0,
                   W=["oneT"])

        def wload(src_ap, ncol_total, view):
            i = cnt["w"] % NSLOT
            cnt["w"] += 1
            dst = view(ring[i])
            S.dma("pool", f"w{i}", dst, src_ap, writes=[f"ring{i}"])
            return i

        def wview8(i, ncols):
            return ring[i][:, 0:8 * ncols].rearrange("p (c n) -> p c n", c=8)

        def norm_stats(t, p, jbuf, jkey):
            n = NTOK[t]
            Ht = f"H{t}"
            S.call("dve", "memset", ss[p][0:n, :], 0.0,
                   W=[f"ss{p}"])
            S.call("act", "activation", out=jbuf[0:n, :], in_=H[0:n, t, :], func=AF.Square,
                                               accum_out=ss[p][0:n, :],
                   R=[Ht], W=[f"ss{p}", jkey])
            S.call("act", "activation", out=rstd[p][0:n, :], in_=ss[p][0:n, :], func=AF.Sqrt,
                                               scale=1.0 / D, bias=epsT[0:n, 0:1],
                   R=[f"ss{p}", "epsT"], W=[f"rstd{p}"])
            S.call("dve", "reciprocal", out=rstd[p][0:n, :], in_=rstd[p][0:n, :],
                   R=[f"rstd{p}"], W=[f"rstd{p}"])

        def norm_to_T(t, xnbuf, xnkey, dstT, col, dstkey, pre=False):
            n = NTOK[t]
            p = cnt["n"] % 2
            cnt["n"] += 1
            xb = xnbuf[p]
            if pre:
                rs_ap, rs_key = rstd2[0:n, t:t + 1], f"rs2_{t}"
            else:
                norm_stats(t, p, xb, f"{xnkey}{p}")
                rs_ap, rs_key = rstd[p][0:n, 0:1], f"rstd{p}"
            S.call("dve", "scalar_tensor_tensor", out=xb[0:n, :], in0=H[0:n, t, :],
                   scalar=rs_ap, in1=gB[0:n, :], op0=ALU.mult, op1=ALU.mult,
                   R=[f"H{t}", rs_key, "gB"], W=[f"{xnkey}{p}"])
            for c in range(8):
                S.call("pe", "transpose", out=PTR[:, c, 0:n], in_=xb[0:n, c * 128:(c + 1) * 128],
                                                      identity=identb[0:n, 0:n],
                   R=[f"{xnkey}{p}", "identb"], W=["PTR"])
            S.call("act", "copy", out=dstT[:, :, col:col + n], in_=PTR[:, :, 0:n],
                   R=["PTR"], W=[dstkey])

        def mm_tok(t, colT, srcT, srckey, slot, ncols, outp, outkey):
            n = NTOK[t]
            wv = wview8(slot, ncols)
            for k in range(8):
                S.call("pe", "matmul", outp[0:n, 0:ncols], lhsT=srcT[:, k, colT:colT + n],
                                                   rhs=wv[:, k, :], start=(k == 0), stop=(k == 7),
                   R=[srckey, f"ring{slot}"], W=[outkey])

        def phase1(s, li, first_layer_of_prog):
            S.dma("sp", "c_gB", gB[:], attn_norm_d[li:li + 1, :].broadcast_to([128, D]), writes=["gB"])
            S.dma("sp", "c_gnB", gnB[:], ret_gn_d[li:li + 1, :].broadcast_to([128, 512]), writes=["gnB"])
            S.dma("sp", "c_bfB", bfB[:], b_fgate_d[li:li + 1, :].broadcast_to([128, 8]), writes=["bfB"])
            S.dma("pool", "c_wlog", wlog[:],
                  w_in_d[li, :, 1536:1544].rearrange("(c p) n -> p c n", p=128), writes=["wlog"])

            for gi, tiles in enumerate(GROUPS):
                gpos0 = POS0[tiles[0]]
                gcols = {}
                cc = 0
                for t in tiles:
                    gcols[t] = cc
                    cc += NTOK[t]
                gw = cc
                segs = [(0, 16), (16, 528)] if gi == 0 else [(0, 512)]

                if gi == 0:
                    for t in tiles:
                        norm_to_T(t, xn, "xn", xnT, gcols[t], "xnT")
                    S.call("dve", "memset", qT[:, :, :], 0.0, W=["qT"])
                    S.call("dve", "memset", mixF[64:128, :, :], 0.0, W=["mixF"])
                    S.call("dve", "memset", Vc[:, :, :, 64:65], 1.0, W=["Vones"])

                def win_chunk(c0):
                    return wload(w_in_d[li, :, c0:c0 + 512].rearrange("(c p) n -> p c n", p=128), 512,
                                 lambda r: r[:, 0:4096].rearrange("p (c n) -> p c n", c=8))
                for j, t in enumerate(tiles):
                    n = NTOK[t]
                    col = gcols[t]
                    for k in range(8):
                        S.call("pe", "matmul", P6[0:n, 0:8], lhsT=xnT[:, k, col:col + n], rhs=wlog[:, k, :],
                               start=(k == 0), stop=(k == 7), R=["xnT", "wlog"], W=["P6"])
                    S.call("dve", "tensor_tensor", out=tl5[0:n, j, :], in0=P6[0:n, 0:8], in1=bfB[0:n, :], op=ALU.add,
                           R=["P6", "bfB"], W=[f"tl{j}"])
                for j, t in enumerate(tiles):
                    n = NTOK[t]
                    S.call("act", "activation", out=te5[0:n, j, :], in_=tl5[0:n, j, :], func=AF.Exp, scale=-1.0,
                           R=[f"tl{j}"], W=[f"te{j}"])
                for j, t in enumerate(tiles):
                    n = NTOK[t]
                    S.call("act", "activation", out=spv[0:n, j, :], in_=te5[0:n, j, :], func=AF.Ln,
                           bias=oneT[0:n, 0:1], R=[f"te{j}", "oneT"], W=[f"spv{j}"])
                for which, c0 in (("q", 0), ("k", 512)):
                    slot = win_chunk(c0)
                    wv = wview8(slot, 512)
                    for pair in range(4):
                        for (a, b) in segs:
                            u = cnt["u"] % 2
                            cnt["u"] += 1
                            w = b - a
                            for k in range(8):
                                S.call("pe", "matmul",
                                    P[u][:, 0:w], lhsT=wv[:, k, pair * 128:(pair + 1) * 128],
                                    rhs=xnT[:, k, a:b], start=(k == 0), stop=(k == 7),
                   R=["xnT", f"ring{slot}"], W=[f"P{u}"])
                            if which == "q":
                                S.call("dve", "tensor_scalar", out=qT[0:64, 2 * pair, a:b], in0=P[u][0:64, 0:w],
                                       scalar1=0.125, scalar2=None, op0=ALU.mult, R=[f"P{u}"], W=["qT"])
                                S.call("dve", "tensor_scalar", out=qT[64:128, 2 * pair + 1, a:b],
                                       in0=P[u][64:128, 0:w], scalar1=0.125, scalar2=None, op0=ALU.mult,
                                       R=[f"P{u}"], W=["qT"])
                            else:
                                S.call("dve", "tensor_copy", out=Kc[:, pair, gpos0 + a:gpos0 + b], in_=P[u][:, 0:w],
                                       R=[f"P{u}"], W=["Kc"])
                slot = win_chunk(1024)

                def emit_crow(t):
                    n = NTOK[t]
                    col = gcols[t]
                    S.call("pe", "matmul", P6[0:8, 16:16 + n], lhsT=negc[0:n, t, :], rhs=identf[0:n, 0:n],
                           start=True, stop=True, R=["negc", "identf"], W=["P6"])
                    S.call("act", "mul", out=crowT[0:8, col:col + n], in_=P6[0:8, 16:16 + n], mul=-1.0,
                           R=["P6"], W=["crowT"])

                for j, t in enumerate(tiles):
                    n = NTOK[t]
                    u = cnt["u"] % 2
                    cnt["u"] += 1
                    mm_tok(t, gcols[t], xnT, "xnT", slot, 512, P[u], f"P{u}")
                    S.call("dve", "tensor_copy", out=Vc[0:n, t, :, 0:64],
                           in_=P[u][0:n, :].rearrange("p (h e) -> p h e", h=8), R=[f"P{u}"], W=["Vc"])
                    S.call("pe", "matmul", P6[0:n, 8:16], lhsT=Umat[0:n, 0:n], rhs=spv[0:n, j, :],
                           start=True, stop=(t == 0), R=[f"spv{j}", "Umat"], W=["P6"])
                    if t > 0:
                        npv = NTOK[t - 1]
                        esel = 0 if t - 1 == 0 else 1
                        S.call("pe", "matmul", P6[0:n, 8:16], lhsT=Esel[0:npv, esel, 0:n], rhs=negc[0:npv, t - 1, :],
                               start=False, stop=True, R=["negc", "Esel"], W=["P6"])
                    S.call("dve", "tensor_copy", out=negc[0:n, t, :], in_=P6[0:n, 8:16], R=["P6"], W=["negc"])
                    if j > 0:
                        emit_crow(tiles[j - 1])
                emit_crow(tiles[-1])

                srq = win_chunk(1544)
                srk = win_chunk(2056)
                srv = win_chunk(2568)
                srg = win_chunk(3080)

                def ret_steps(t):
                    n = NTOK[t]
                    col = gcols[t]
                    rob = cnt["ro"] % 2
                    cnt["ro"] += 1
                    o4 = SC[1][0:n, :].rearrange("p (h e) -> p h e", h=4)

                    def rot(src, dec, dst, dstkey, srckey):
                        s4 = src[0:n, :].rearrange("p (h two e) -> p h two e", h=4, two=2)
                        x1 = s4[:, :, 0, :]
                        x2 = s4[:, :, 1, :]
                        cb = cosT[0:n, t, :].unsqueeze(1).broadcast_to([n, 4, 64])
                        sbb = sinT[0:n, t, :].unsqueeze(1).broadcast_to([n, 4, 64])
                        A4 = tmpA[0:n, :].rearrange("p (h two e) -> p h two e", h=4, two=2)
                        B4 = tmpB[0:n, :].rearrange("p (h two e) -> p h two e", h=4, two=2)
                        S.call("dve", "tensor_tensor", out=A4[:, :, 0, :], in0=x1, in1=cb, op=ALU.mult,
                               R=[srckey, "cosT"], W=["tmpA"])
                        S.call("dve", "tensor_tensor", out=A4[:, :, 1, :], in0=x1, in1=sbb, op=ALU.mult,
                               R=[srckey, "sinT"], W=["tmpA"])
                        S.call("dve", "tensor_tensor", out=B4[:, :, 0, :], in0=x2, in1=sbb, op=ALU.mult,
                               R=[srckey, "sinT"], W=["tmpB"])
                        S.call("dve", "tensor_tensor", out=B4[:, :, 1, :], in0=x2, in1=cb, op=ALU.mult,
                               R=[srckey, "cosT"], W=["tmpB"])
                        S.call("dve", "tensor_tensor", out=A4[:, :, 0, :], in0=A4[:, :, 0, :], in1=B4[:, :, 0, :],
                               op=ALU.subtract, R=["tmpA", "tmpB"], W=["tmpA"])
                        S.call("dve", "tensor_tensor", out=A4[:, :, 1, :], in0=A4[:, :, 1, :], in1=B4[:, :, 1, :],
                               op=ALU.add, R=["tmpA", "tmpB"], W=["tmpA"])
                        S.call("dve", "tensor_tensor",
                               out=dst[0:n, :].rearrange("p (h e) -> p h e", h=4),
                               in0=tmpA[0:n, :].rearrange("p (h e) -> p h e", h=4),
                               in1=dec[0:n, :].unsqueeze(2).broadcast_to([n, 4, 128]), op=ALU.mult,
                               R=["tmpA", "qdec", "kdec"], W=[dstkey])

                    def F0():
                        mm_tok(t, col, xnT, "xnT", srq, 512, P[0], "P0")
                        mm_tok(t, col, xnT, "xnT", srk, 512, P[1], "P1")
                        mm_tok(t, col, xnT, "xnT", srv, 512, SC[0], "SC0")
                        mm_tok(t, col, xnT, "xnT", srg, 512, OTp, "OT")

                    def F0b():
                        S.call("act", "copy", out=rvb[0:n, :], in_=SC[0][0:n, :], R=["SC0"], W=["rvb"])

                    def F1():
                        rot(P[0], qdec, qp, "qp", "P0")

                    def F3():
                        rot(P[1], kdec, kp, "kp", "P1")

                    def F4():
                        for c in range(4):
                            S.call("pe", "transpose", out=PTR[:, c, 0:n], in_=qp[0:n, c * 128:(c + 1) * 128],
                                   identity=identb[0:n, 0:n], R=["qp", "identb"], W=["PTR"])
                        for c in range(4):
                            S.call("pe", "transpose", out=PTR[:, 4 + c, 0:n], in_=kp[0:n, c * 128:(c + 1) * 128],
                                   identity=identb[0:n, 0:n], R=["kp", "identb"], W=["PTR"])
                        S.call("act", "copy", out=rqkT[:, :, 0:n], in_=PTR[:, :, 0:n], R=["PTR"], W=["rqkT"])

                    def F5():
                        for hh in range(4):
                            S.call("pe", "matmul", SC[0][0:n, hh * 128:hh * 128 + n],
                                   lhsT=rqkT[:, 4 + hh, 0:n], rhs=rqkT[:, hh, 0:n], start=True, stop=True,
                                   R=["rqkT"], W=["SC0"])
                        S.call("dve", "tensor_tensor", out=mT[0:n, :, 0:n],
                               in0=SC[0][0:n, :].rearrange("p (h e) -> p h e", h=4)[:, :, 0:n],
                               in1=caus01[0:n, 0:n].unsqueeze(1).broadcast_to([n, 4, n]), op=ALU.mult,
                               R=["SC0", "caus01"], W=["mT"])

                    def F6():
                        for hh in range(4):
                            S.call("pe", "matmul", SC[1][0:n, hh * 128:(hh + 1) * 128], lhsT=mT[0:n, hh, 0:n],
                                   rhs=rvb[0:n, hh * 128:(hh + 1) * 128], start=True, stop=(t == 0),
                                   R=["mT", "rvb"], W=["SC1"])
                            if t > 0:
                                S.call("pe", "matmul", SC[1][0:n, hh * 128:(hh + 1) * 128],
                                       lhsT=rqkT[:, hh, 0:n], rhs=Sbf[:, hh, :], start=False, stop=True,
                                       R=["rqkT", "Sbf"], W=["SC1"])
                        for hh in range(4):
                            S.call("pe", "matmul", P7[:, hh * 128:(hh + 1) * 128],
                                   lhsT=kp[0:n, hh * 128:(hh + 1) * 128], rhs=rvb[0:n, hh * 128:(hh + 1) * 128],
                                   start=True, stop=True, R=["kp", "rvb"], W=["P7"])

                    def F7():
                        S.call("act", "activation", out=tmpC[0:n, :], in_=OTp[0:n, :], func=AF.Silu,
                               R=["OT"], W=["tmpC"])
                        S.call("dve", "tensor_tensor", out=tmpC[0:n, :], in0=tmpC[0:n, :], in1=gnB[0:n, :],
                               op=ALU.mult, R=["tmpC", "gnB"], W=["tmpC"])

                    def B0():
                        S4v = S32[:].rearrange("p h e -> p (h e)")
                        if t == 0:
                            S.call("dve", "tensor_copy", out=S4v, in_=P7[:, :], R=["P7"], W=["S32"])
                        else:
                            S.call("dve", "tensor_tensor", out=S4v, in0=S4v, in1=P7[:, :], op=ALU.add,
                                   R=["P7", "S32"], W=["S32"])
                        gsel = 0 if t == 0 else 1
                        S.call("dve", "tensor_tensor", out=S32[:], in0=S32[:],
                               in1=gdecn[:, gsel, :].unsqueeze(2).broadcast_to([128, 4, 128]), op=ALU.mult,
                               R=["S32", "gdecn"], W=["S32"])
                        S.call("act", "copy", out=Sbf[:], in_=S32[:], R=["S32"], W=["Sbf"])

                    def B1():
                        S.call("dve", "tensor_reduce", out=st4[0:n, 0, :], in_=o4, axis=AX.X, op=ALU.add,
                               R=["SC1"], W=["st4"])
                        S.call("act", "activation", out=P6[0:n, :], in_=SC[1][0:n, :], func=AF.Square,
                               R=["SC1"], W=["P6"])

                    def B2():
                        S.call("dve", "tensor_reduce", out=st4[0:n, 1, :],
                               in_=P6[0:n, :].rearrange("p (h e) -> p h e", h=4), axis=AX.X, op=ALU.add,
                               R=["P6"], W=["st4"])
                        S.call("dve", "tensor_scalar", out=st4[0:n, 2, :], in0=st4[0:n, 0, :],
                               scalar1=1.0 / 128, scalar2=None, op0=ALU.mult, R=["st4"], W=["st4"])
                        S.call("dve", "tensor_tensor", out=st4[0:n, 3, :], in0=st4[0:n, 2, :], in1=st4[0:n, 2, :],
                               op=ALU.mult, R=["st4"], W=["st4"])
                        S.call("dve", "scalar_tensor_tensor", out=st4[0:n, 4, :], in0=st4[0:n, 1, :],
                               scalar=1.0 / 128, in1=st4[0:n, 3, :], op0=ALU.mult, op1=ALU.subtract,
                               R=["st4"], W=["st4"])
                        S.call("act", "activation", out=st4[0:n, 5, :], in_=st4[0:n, 4, :], func=AF.Sqrt,
                               bias=epsT[0:n, 0:1], scale=1.0, R=["st4", "epsT"], W=["st4"])

                    def B3():
                        S.call("dve", "reciprocal", out=st4[0:n, 5, :], in_=st4[0:n, 5, :], R=["st4"], W=["st4"])
                        S.call("dve", "scalar_tensor_tensor", out=st4[0:n, 3, :], in0=st4[0:n, 2, :], scalar=-1.0,
                               in1=st4[0:n, 5, :], op0=ALU.mult, op1=ALU.mult, R=["st4"], W=["st4"])
                        for hh in range(4):
                            S.call("act", "activation", out=P6[0:n, hh * 128:(hh + 1) * 128],
                                   in_=SC[1][0:n, hh * 128:(hh + 1) * 128], func=AF.Identity,
                                   scale=st4[0:n, 5, hh:hh + 1], bias=st4[0:n, 3, hh:hh + 1],
                                   R=["SC1", "st4"], W=["P6"])

                    def B4():
                        S.call("dve", "tensor_tensor", out=ROB[rob][0:n, :], in0=P6[0:n, :], in1=tmpC[0:n, :],
                               op=ALU.mult, R=["P6", "tmpC"], W=[ROK[rob]])

                    def B5():
                        for c in range(4):
                            S.call("pe", "transpose", out=PTR[:, c, 0:n], in_=ROB[rob][0:n, c * 128:(c + 1) * 128],
                                   identity=identb[0:n, 0:n], R=[ROK[rob], "identb"], W=["PTR"])
                        S.call("act", "copy", out=mixR[:, :, col:col + n], in_=PTR[:, 0:4, 0:n],
                               R=["PTR"], W=["mixR"])

                    return dict(F0=F0, F0b=F0b, F1=F1, F3=F3, F4=F4, F5=F5, F6=F6, F7=F7,
                                B0=B0, B1=B1, B2=B2, B3=B3, B4=B4, B5=B5)

                prev = None
                for t in tiles:
                    cur = ret_steps(t)
                    cur["F0"]()
                    if prev:
                        prev["B0"]()
                        prev["B1"]()
                    cur["F0b"]()
                    cur["F1"]()
                    if prev:
                        prev["B2"]()
                    cur["F3"]()
                    cur["F4"]()
                    if prev:
                        prev["B3"]()
                    cur["F5"]()
                    if prev:
                        prev["B4"]()
                    cur["F6"]()
                    cur["F7"]()
                    if prev:
                        prev["B5"]()
                    prev = cur
                prev["B0"]()
                tail_steps = [prev[k] for k in ("B1", "B2", "B3", "B4", "B5")]
                if gi == 0:
                    while tail_steps:
                        tail_steps.pop(0)()

                SCB = [SC[0], P[0], P[1]]
                SCK = ["SC0", "P0", "P1"]
                OTB = [OTp, P7]
                OTK = ["OT", "P7"]
                LOOK = 2
                n1_sched = {}
                att_ctr = [0]
                if gi + 1 < len(GROUPS):
                    nxt = GROUPS[gi + 1]

                    def n1_stats(jj_unused, t_unused):
                        S.call("dve", "memset", ssg[:, :], 0.0, W=["ssg"])
                        for jj, t in enumerate(nxt):
                            S.call("act", "activation", out=xn[0][:, :], in_=H[:, t, :], func=AF.Square,
                                   accum_out=ssg[:, jj:jj + 1], R=[f"H{t}"], W=["ssg", "xn0"])
                        S.call("act", "activation", out=rstdg[:, :], in_=ssg[:, :], func=AF.Sqrt,
                               scale=1.0 / D, bias=epsT[:, 0:1], R=["ssg", "epsT"], W=["rstdg"])
                        S.call("dve", "reciprocal", out=rstdg[:, :], in_=rstdg[:, :], R=["rstdg"], W=["rstdg"])

                    def n1_A(jj, t):
                        xi = jj % 2
                        S.call("dve", "scalar_tensor_tensor", out=xn[xi][:, :], in0=H[:, t, :],
                               scalar=rstdg[:, jj:jj + 1], in1=gB[:, :], op0=ALU.mult, op1=ALU.mult,
                               R=[f"H{t}", "rstdg", "gB"], W=[f"xn{xi}"])

                    def n1_B(jj, t):
                        xi = jj % 2
                        for c in range(8):
                            S.call("pe", "transpose", out=PTR[:, c, :], in_=xn[xi][:, c * 128:(c + 1) * 128],
                                   identity=identb[:, :], R=[f"xn{xi}", "identb"], W=["PTR"])
                        S.call("dve", "tensor_copy", out=xnT[:, :, 128 * jj:128 * jj + 128], in_=PTR[:, :, :],
                               R=["PTR"], W=["xnT"])

                    plan = [(12, "S", 0), (14, "A", 0), (15, "A", 1), (19, "B", 0), (20, "A", 2), (24, "B", 1),
                            (25, "A", 3), (29, "B", 2), (34, "B", 3)]
                    for (at, kind, jj) in plan:
                        fn = {"A": n1_A, "B": n1_B, "S": n1_stats}[kind]
                        n1_sched.setdefault(at, []).append((fn, jj, nxt[jj]))

                def n1_tick():
                    for (fn, jj, tt) in n1_sched.pop(att_ctr[0], []):
                        fn(jj, tt)
                    att_ctr[0] += 1

                for (a, b) in segs:
                    w = b - a
                    pa0 = gpos0 + a
                    ktiles = [t for t in range(NT) if POS0[t] < pa0 + w]
                    items = [(h, kt) for h in range(8) for kt in ktiles]
                    geo = {}
                    for kt in ktiles:
                        nk = NTOK[kt]
                        kp0 = POS0[kt]
                        if kp0 < pa0:
                            units = [(a, b, False)]
                            lo = a
                        else:
                            kc = a + (kp0 - pa0)
                            units = [(kc, kc + nk, True)]
                            if kc + nk < b:
                                units.append((kc + nk, b, False))
                            lo = kc
                        geo[kt] = (nk, kp0, units, lo)

                    def emit_scores(i):
                        h, kt = items[i]
                        nk, kp0, units, lo = geo[kt]
                        pr0 = 0 if h % 2 == 0 else 64
                        pair = h // 2
                        u = i % 3
                        for (ua, ub, diag) in units:
                            S.call("pe", "matmul",
                                   SCB[u][0:nk, ua - a:ub - a], lhsT=Kc[:, pair, kp0:kp0 + nk],
                                   rhs=qT[:, h, ua:ub], start=True, stop=False,
                                   R=["Kc", "qT"], W=[SCK[u]])
                            S.call("pe", "matmul",
                                   SCB[u][0:nk, ua - a:ub - a], lhsT=selb[:, h, 0:nk],
                                   rhs=crowT[:, ua:ub], start=False, stop=(not diag),
                                   R=["selb", "crowT"], W=[SCK[u]])
                            if diag:
                                S.call("pe", "matmul",
                                       SCB[u][0:nk, ua - a:ub - a], lhsT=identb[0:nk, 0:nk],
                                       rhs=maskneg[0:nk, 0:nk], start=False, stop=True,
                                       R=["identb", "maskneg"], W=[SCK[u]])

                    def emit_rest(i):
                        h, kt = items[i]
                        nk, kp0, units, lo = geo[kt]
                        u = i % 3
                        pu = i % 2
                        ob = h % 2
                        S.call("act", "activation",
                               out=PTb[pu][0:nk, lo - a:b - a], in_=SCB[u][0:nk, lo - a:b - a], func=AF.Exp,
                               bias=negc[0:nk, kt, h:h + 1], scale=1.0,
                               R=[SCK[u], "negc"], W=[f"PT{pu}"])
                        S.call("pe", "matmul",
                               OTB[ob][:, lo - a:b - a],
                               lhsT=A[0:nk, vc_off + (kt * 8 + h) * 65:vc_off + (kt * 8 + h) * 65 + 128],
                               rhs=PTb[pu][0:nk, lo - a:b - a],
                               start=(kt == ktiles[0]), stop=(kt == ktiles[-1]), skip_group_check=True,
                               R=[f"PT{pu}", "Vc", "Vones"], W=[OTK[ob]])

                    def emit_norm(h):
                        ob = h % 2
                        S.call("dve", "reciprocal", out=tmpC[64:65, 0:w], in_=OTB[ob][64:65, 0:w],
                               R=[OTK[ob]], W=["tmpC"])
                        S.call("pe", "matmul", P6[0:64, 0:w], lhsT=onesf[64:65, 0:64], rhs=tmpC[64:65, 0:w],
                               start=True, stop=True, R=["tmpC", "onesf"], W=["P6"])
                        S.call("dve", "tensor_copy", out=tmpA[0:64, 0:w], in_=OTB[ob][0:64, 0:w],
                               R=[OTK[ob]], W=["tmpA"])
                        S.call("dve", "tensor_tensor",
                               out=mixF[0:64, h, a:b], in0=tmpA[0:64, 0:w], in1=P6[0:64, 0:w], op=ALU.mult,
                               R=["tmpA", "P6"], W=["mixF"])

                    pending = []
                    nit = len(items)
                    for i in range(nit + LOOK):
                        if i < nit:
                            emit_scores(i)
                        j = i - LOOK
                        if j >= 0:
                            emit_rest(j)
                            if items[j][1] == ktiles[-1]:
                                pending.append((j + min(3, len(ktiles)), items[j][0]))
                        while pending and pending[0][0] <= j:
                            emit_norm(pending.pop(0)[1])
                        if tail_steps and i % 2 == 1:
                            tail_steps.pop(0)()
                        n1_tick()
                    for _, hh in pending:
                        emit_norm(hh)
                while tail_steps:
                    tail_steps.pop(0)()
                for at in sorted(n1_sched):
                    for (fn, jj, tt) in n1_sched[at]:
                        fn(jj, tt)
                n1_sched.clear()

                sfa = wload(w_out_d[li, 0:256, :].rearrange("(h p) n -> p h n", p=64), 0,
                            lambda r: r[0:64, 0:4096].rearrange("p (c n) -> p c n", c=4))
                sfb = wload(w_out_d[li, 256:512, :].rearrange("(h p) n -> p h n", p=64), 0,
                            lambda r: r[0:64, 0:4096].rearrange("p (c n) -> p c n", c=4))
                srt = wload(w_out_d[li, 512:1024, :].rearrange("(c p) n -> p c n", p=128), 0,
                            lambda r: r[:, 0:4096].rearrange("p (c n) -> p c n", c=4))
                wfa = ring[sfa][:, 0:4096].rearrange("p (c n) -> p c n", c=4)
                wfb = ring[sfb][:, 0:4096].rearrange("p (c n) -> p c n", c=4)
                wrt = ring[srt][:, 0:4096].rearrange("p (c n) -> p c n", c=4)
                for t in tiles:
                    n = NTOK[t]
                    col = gcols[t]
                    for half in range(2):
                        yb = cnt["y"] % 3
                        cnt["y"] += 1
                        ybk = ["OT", "P6", "P7"][yb]
                        hs = slice(half * 512, (half + 1) * 512)
                        for h in range(8):
                            wsrc = wfa if h < 4 else wfb
                            sk = sfa if h < 4 else sfb
                            S.call("pe", "matmul",
                                YB[yb][0:n, :], lhsT=mixF[:, h, col:col + n], rhs=wsrc[:, h % 4, hs],
                                start=(h == 0), stop=False,
                   R=["mixF", f"ring{sk}"], W=[ybk])
                        for c in range(4):
                            S.call("pe", "matmul",
                                YB[yb][0:n, :], lhsT=mixR[:, c, col:col + n], rhs=wrt[:, c, hs],
                                start=False, stop=(c == 3),
                   R=["mixR", f"ring{srt}"], W=[ybk])
                        S.call("dve", "tensor_tensor",
                            out=H[0:n, t, hs], in0=H[0:n, t, hs], in1=YB[yb][0:n, :], op=ALU.add,
                   R=[ybk, f"H{t}"], W=[f"H{t}"])
                    S.call("dve", "memset", rstd2[0:n, t:t + 1], 0.0, W=[f"rs2_{t}"])
                    S.call("act", "activation", out=xn[1][0:n, :], in_=H[0:n, t, :], func=AF.Square,
                           accum_out=rstd2[0:n, t:t + 1], R=[f"H{t}"], W=[f"rs2_{t}", "xn1"])
                    S.call("act", "activation", out=rstd2[0:n, t:t + 1], in_=rstd2[0:n, t:t + 1], func=AF.Sqrt,
                           scale=1.0 / D, bias=epsT[0:n, 0:1], R=[f"rs2_{t}", "epsT"], W=[f"rs2_{t}"])
                    S.call("dve", "reciprocal", out=rstd2[0:n, t:t + 1], in_=rstd2[0:n, t:t + 1],
                           R=[f"rs2_{t}"], W=[f"rs2_{t}"])

        def phase2(s, li):
            S.dma("sp", "c_gB", gB[:], ffn_norm_d[li:li + 1, :].broadcast_to([128, D]), writes=["gB"])
            segs = [(0, 16), (16, 528), (528, 1040), (1040, 1552), (1552, 2064)]
            seg_tiles = [[0], [1, 2, 3, 4], [5, 6, 7, 8], [9, 10, 11, 12], [13, 14, 15, 16]]

            def norm_seg(si):
                for t in seg_tiles[si]:
                    norm_to_T(t, xn2, "xn2_", hnT, POS0[t], f"hnT{si}", pre=True)

            norm_seg(0)
            norm_seg(1)
            first_fg = True
            f0 = 0
            while f0 < DFF:
                fw = min(512, DFF - f0)
                nfc = fw // 128
                sg = wload(w_gate_d[li, :, f0:f0 + fw].rearrange("(c p) n -> p c n", p=128), fw,
                           lambda r, fw=fw: r[:, 0:8 * fw].rearrange("p (c n) -> p c n", c=8))
                su = wload(w_up_d[li, :, f0:f0 + fw].rearrange("(c p) n -> p c n", p=128), fw,
                           lambda r, fw=fw: r[:, 0:8 * fw].rearrange("p (c n) -> p c n", c=8))
                sd = wload(w_down_d[li, f0:f0 + fw, :].rearrange("(c p) n -> p c n", p=128), fw,
                           lambda r, nfc=nfc: r[:, 0:nfc * 1024].rearrange("p (c n) -> p c n", c=nfc))
                wg = wview8(sg, fw)
                wu = wview8(su, fw)
                wd = ring[sd][:, 0:nfc * 1024].rearrange("p (c n) -> p c n", c=nfc)
                down_pending = None
                for si, (a, b) in enumerate(segs):
                    w = b - a
                    ab = cnt["u"] % 2
                    cnt["u"] += 1
                    if first_fg and si + 2 < len(segs):
                        norm_seg(si + 2)
                    for fc in range(nfc):
                        for k in range(8):
                            S.call("pe", "matmul",
                                P[0][:, 0:w], lhsT=wg[:, k, fc * 128:(fc + 1) * 128], rhs=hnT[:, k, a:b],
                                start=(k == 0), stop=(k == 7),
                   R=[f"hnT{si}", f"ring{sg}"], W=["P0"])
                        for k in range(8):
                            S.call("pe", "matmul",
                                P[1][:, 0:w], lhsT=wu[:, k, fc * 128:(fc + 1) * 128], rhs=hnT[:, k, a:b],
                                start=(k == 0), stop=(k == 7),
                   R=[f"hnT{si}", f"ring{su}"], W=["P1"])
                        S.call("act", "activation", out=tmpA[:, 0:w], in_=P[0][:, 0:w], func=AF.Silu,
                   R=["P0"], W=["tmpA"])
                        S.call("dve", "tensor_tensor",
                            out=actT[ab][:, fc, 0:w], in0=tmpA[:, 0:w], in1=P[1][:, 0:w], op=ALU.mult,
                   R=["tmpA", "P1"], W=[f"actT{ab}"])
                    def emit_down(si=si, a=a, ab=ab):
                        for t in seg_tiles[si]:
                            n = NTOK[t]
                            tc0 = POS0[t] - a
                            for half in range(2):
                                yb = cnt["y"] % 3
                                cnt["y"] += 1
                                ybk = ["OT", "P6", "P7"][yb]
                                hs = slice(half * 512, (half + 1) * 512)
                                for fc in range(nfc):
                                    S.call("pe", "matmul",
                                           YB[yb][0:n, :], lhsT=actT[ab][:, fc, tc0:tc0 + n], rhs=wd[:, fc, hs],
                                           start=(fc == 0), stop=(fc == nfc - 1),
                                           R=[f"actT{ab}", f"ring{sd}"], W=[ybk])
                                S.call("dve", "tensor_tensor",
                                       out=H[0:n, t, hs], in0=H[0:n, t, hs], in1=YB[yb][0:n, :], op=ALU.add,
                                       R=[ybk, f"H{t}"], W=[f"H{t}"])

                    if down_pending is not None:
                        down_pending()
                    down_pending = emit_down
                if down_pending is not None:
                    down_pending()
                    down_pending = None
                f0 += fw
                first_fg = False

        for s in range(2):
            if first:
                S.dma("sp", "ldx", H[:, 1:NT, :], x_d[s].rearrange("(t p) d -> p t d", p=128),
                      writes=[f"H{t}" for t in range(1, NT)])
                S.dma("sp", "ldx", H[0:16, 0, :], meta_d, writes=["H0"])
            else:
                S.dma("sp", "ldx", H[:, 1:NT, :], hin_d[s, 16:L, :].rearrange("(t p) d -> p t d", p=128),
                      writes=[f"H{t}" for t in range(1, NT)])
                S.dma("sp", "ldx", H[0:16, 0, :], hin_d[s, 0:16, :], writes=["H0"])
            for li in layer_ids:
                phase1(s, li, False)
                S.barrier()
                phase2(s, li)
                S.barrier()
            if last:
                S.dma("sp", "c_gB", gB[:], final_norm_d[0:1, :].broadcast_to([128, D]), writes=["gB"])
                for t in range(1, NT):
                    p = cnt["n"] % 2
                    cnt["n"] += 1
                    norm_stats(t, p, xn[p], f"xn{p}")
                    ob = [tmpA, tmpB][p]
                    obk = ["tmpA", "tmpB"][p]
                    for half in range(2):
                        hs = slice(half * 512, (half + 1) * 512)
                        S.call("dve", "scalar_tensor_tensor",
                            out=ob[:, :], in0=H[:, t, hs], scalar=rstd[p][:, 0:1], in1=gB[:, hs],
                            op0=ALU.mult, op1=ALU.mult,
                   R=[f"H{t}", f"rstd{p}", "gB"], W=[obk])
                        S.dma("sp", "st", out_d[s, (t - 1) * 128:t * 128, hs], ob[:, :], reads=[obk])
            else:
                S.dma("sp", "st", out_d[s, 16:L, :].rearrange("(t p) d -> p t d", p=128), H[:, 1:NT, :],
                      reads=[f"H{t}" for t in range(1, NT)])
                S.dma("sp", "st", out_d[s, 0:16, :], H[0:16, 0, :], reads=["H0"])
            S.barrier()
        S.final_wait("sp", ["st"])
        n_ops = {e: len(S.ops[e]) for e in S.ENG}
        print("ops per engine", n_ops)
        S.emit(st)
    return nc


_CACHE = {}


def _get_prog(key):
    if key not in _CACHE:
        _CACHE[key] = build(*key)
    return _CACHE[key]


def kernel(x, meta_tokens, attn_norm, w_in, b_fgate, ret_gn, w_out, ffn_norm, w_gate, w_up, w_down, final_norm):
    f = lambda a: np.ascontiguousarray(np.asarray(a, dtype=np.float32))
    consts = make_consts()
    shared = {
        "attn_norm": f(attn_norm), "w_in": f(w_in), "b_fgate": f(b_fgate), "ret_gn": f(ret_gn),
        "w_out": f(w_out), "ffn_norm": f(ffn_norm), "w_gate": f(w_gate), "w_up": f(w_up),
        "w_down": f(w_down), "final_norm": f(final_norm).reshape(1, D),
    }
    shared.update(consts)
    x = f(x)
    meta = f(meta_tokens)
    cores = list(range(NCORES))
    if MODE == "fused":
        nc = _get_prog(((0, 1), True, True))
        in_maps = [dict(shared, x=x[2 * c:2 * c + 2], meta=meta) for c in cores]
        res = run_bass_kernel_spmd(nc, in_maps, core_ids=cores)
        return np.concatenate([r["out"] for r in res.results], axis=0)
    else:
        nc0 = _get_prog(((0,), True, False))
        in_maps = [dict(shared, x=x[2 * c:2 * c + 2], meta=meta) for c in cores]
        res0 = run_bass_kernel_spmd(nc0, in_maps, core_ids=cores)
        nc1 = _get_prog(((1,), False, True))
        in_maps = [dict(shared, hin=res0.results[c]["hout"]) for c in cores]
        res1 = run_bass_kernel_spmd(nc1, in_maps, core_ids=cores)
        return np.concatenate([r["out"] for r in res1.results], axis=0)
```

```python
import math
from contextlib import ExitStack

import numpy as np
import concourse.bass as bass
import concourse.mybir as mybir
from concourse.bass_utils import run_bass_kernel_spmd

F32 = mybir.dt.float32
BF16 = mybir.dt.bfloat16
AF = mybir.ActivationFunctionType
ALU = mybir.AluOpType
AX = mybir.AxisListType

NCORES = 8
D = 1024
SEQ = 2048
NMETA = 16
L = SEQ + NMETA
NT = 17
DFF = 2816
INC = 3592
EPS = 1e-6
NTOK = [16] + [128] * 16
POS0 = [0] + [16 + 128 * i for i in range(16)]
GROUPS = [[0, 1, 2, 3, 4], [5, 6, 7, 8], [9, 10, 11, 12], [13, 14, 15, 16]]
GAMMA = [1.0 - 2.0 ** (-5 - h) for h in range(4)]
NSLOT = 4
SLOT_EL = 4096

MODE = "fused"


class Sched:
    ENG = ("pe", "act", "dve", "pool", "sp")

    def __init__(self, nc):
        self.nc = nc
        self.ops = {e: [] for e in self.ENG}
        self.state = {}
        self.waited = {e: {} for e in self.ENG}
        self.dma_cnt = {}

    def _filter(self, eng, deps):
        out = []
        wd = self.waited[eng]
        for t in deps:
            if t[0] == "eng":
                _, e, idx = t
                if e == eng and e == "pe":
                    continue
                if wd.get(e, -1) >= idx:
                    continue
                wd[e] = idx
                self.ops[e][idx]["signal"] = True
                out.append(t)
            else:
                _, name, val = t
                if wd.get(name, -1) >= val:
                    continue
                wd[name] = val
                out.append(t)
        return out

    def _deps(self, eng, reads, writes):
        deps = []
        for k in reads:
            st = self.state.get(k)
            if st and st["w"] is not None:
                deps.append(st["w"])
        for k in writes:
            st = self.state.get(k)
            if st:
                if st["w"] is not None:
                    deps.append(st["w"])
                deps.extend(st["r"])
        return self._filter(eng, deps)

    def _commit(self, tok, reads, writes):
        for k in reads:
            st = self.state.setdefault(k, {"w": None, "r": []})
            if tok[0] == "eng":
                st["r"] = [t for t in st["r"] if not (t[0] == "eng" and t[1] == tok[1])]
            st["r"].append(tok)
        for k in writes:
            self.state[k] = {"w": tok, "r": []}

    def op(self, eng, fn, reads=(), writes=()):
        waits = self._deps(eng, reads, writes)
        idx = len(self.ops[eng])
        self.ops[eng].append({"fn": fn, "waits": waits, "signal": False, "dma": None})
        self._commit(("eng", eng, idx), reads, writes)

    def call(self, eng, name, *args, R=(), W=(), **kwargs):
        self.op(eng, (lambda e: getattr(e, name)(*args, **kwargs)), reads=R, writes=W)

    def dma(self, q, sem, out_ap, in_ap, reads=(), writes=()):
        waits = self._deps(q, reads, writes)
        self.dma_cnt[sem] = self.dma_cnt.get(sem, 0) + 16
        val = self.dma_cnt[sem]
        self.ops[q].append({"fn": (lambda e: e.dma_start(out=out_ap, in_=in_ap)),
                            "waits": waits, "signal": False, "dma": sem})
        self._commit(("sem", sem, val), reads, writes)

    def barrier(self):
        toks = []
        for e in self.ENG:
            for idx in range(len(self.ops[e]) - 1, -1, -1):
                o = self.ops[e][idx]
                if o["dma"] is None and o["fn"] is not None:
                    toks.append(("eng", e, idx))
                    break
        for s, v in self.dma_cnt.items():
            toks.append(("sem", s, v))
        for e in self.ENG:
            waits = self._filter(e, [t for t in toks if not (t[0] == "eng" and t[1] == e)])
            self.ops[e].append({"fn": None, "waits": waits, "signal": False, "dma": None})

    def final_wait(self, eng, sems):
        waits = [("sem", s, self.dma_cnt[s]) for s in sems if s in self.dma_cnt]
        self.ops[eng].append({"fn": None, "waits": waits, "signal": False, "dma": None})

    def emit(self, stack):
        nc = self.nc
        esem = {e: stack.enter_context(nc.semaphore("s_" + e)) for e in self.ENG}
        dsem = {s: stack.enter_context(nc.semaphore("d_" + s)) for s in self.dma_cnt}
        block = stack.enter_context(nc.Block())
        sigval = {}
        for e in self.ENG:
            c = 0
            for i, o in enumerate(self.ops[e]):
                if o["signal"]:
                    c += 1
                    sigval[(e, i)] = c

        def run(e, engobj):
            for i, o in enumerate(self.ops[e]):
                for t in o["waits"]:
                    if t[0] == "eng":
                        engobj.wait_ge(esem[t[1]], sigval[(t[1], t[2])])
                    else:
                        engobj.wait_ge(dsem[t[1]], t[2])
                if o["fn"] is None:
                    continue
                ins = o["fn"](engobj)
                if o["dma"] is not None:
                    ins.then_inc(dsem[o["dma"]], 16)
                elif o["signal"]:
                    ins.then_inc(esem[e], 1)

        @block.tensor
        def _(pe):
            run("pe", pe)

        @block.scalar
        def _(act):
            run("act", act)

        @block.vector
        def _(dve):
            run("dve", dve)

        @block.gpsimd
        def _(pool):
            run("pool", pool)

        @block.sync
        def _(sp):
            run("sp", sp)


def make_consts():
    c = {}
    c["c_identf"] = np.eye(128, dtype=np.float32)
    s = np.arange(128)[:, None]
    t = np.arange(128)[None, :]
    c["c_U"] = (s <= t).astype(np.float32)
    E = np.zeros((128, 2, 128), np.float32)
    E[15, 0, :] = 1.0
    E[127, 1, :] = 1.0
    c["c_E"] = E
    c["c_maskneg"] = np.where(s <= t, 0.0, -30000.0).astype(np.float32)
    c["c_caus01"] = (s <= t).astype(np.float32)
    sel = np.zeros((128, 8, 128), np.float32)
    for h in range(8):
        sel[h, h, :] = 1.0
    c["c_sel"] = sel
    c["c_onesf"] = np.ones((128, 64), np.float32)
    inv_freq = (10000.0 ** (-np.arange(0, 128, 2, dtype=np.float32) / 128.0)).astype(np.float32)
    pos = np.zeros((128, NT), np.float32)
    for ti in range(NT):
        pos[:, ti] = POS0[ti] + np.arange(128)
    ang = (pos[:, :, None].astype(np.float32) * inv_freq[None, None, :]).astype(np.float32)
    c["c_cos"] = np.cos(ang).astype(np.float32)
    c["c_sin"] = np.sin(ang).astype(np.float32)
    j = np.arange(128, dtype=np.float64)
    lg = np.array([math.log(g) for g in GAMMA])
    c["c_qdec"] = np.exp((j[:, None] + 1.0) * lg[None, :]).astype(np.float32)
    c["c_kdec"] = (np.exp(-(j[:, None] + 1.0) * lg[None, :]) * (128.0 ** -0.5)).astype(np.float32)
    gd = np.zeros((128, 2, 4), np.float32)
    gd[:, 0, :] = np.exp(16.0 * lg)[None, :]
    gd[:, 1, :] = np.exp(128.0 * lg)[None, :]
    c["c_gdecn"] = gd
    return c


CONST_SHAPES = {
    "c_identf": [128, 128], "c_U": [128, 128], "c_E": [128, 2, 128], "c_maskneg": [128, 128],
    "c_caus01": [128, 128], "c_sel": [128, 8, 128], "c_onesf": [128, 64],
    "c_cos": [128, NT, 64], "c_sin": [128, NT, 64], "c_qdec": [128, 4], "c_kdec": [128, 4],
    "c_gdecn": [128, 2, 4],
}


def build(layer_ids, first, last):
    nc = bass.Bass("TRN2", target_bir_lowering=False)
    NL = 2

    def din(name, shape):
        return nc.dram_tensor(name, shape, F32, kind="ExternalInput").ap()

    if first:
        x_d = din("x", [2, SEQ, D])
        meta_d = din("meta", [NMETA, D])
    else:
        hin_d = din("hin", [2, L, D])
    attn_norm_d = din("attn_norm", [NL, D])
    w_in_d = din("w_in", [NL, D, INC])
    b_fgate_d = din("b_fgate", [NL, 8])
    ret_gn_d = din("ret_gn", [NL, 512])
    w_out_d = din("w_out", [NL, D, D])
    ffn_norm_d = din("ffn_norm", [NL, D])
    w_gate_d = din("w_gate", [NL, D, DFF])
    w_up_d = din("w_up", [NL, D, DFF])
    w_down_d = din("w_down", [NL, DFF, D])
    final_norm_d = din("final_norm", [1, D])
    cd = {k: din(k, v) for k, v in CONST_SHAPES.items()}
    if last:
        out_d = nc.dram_tensor("out", [2, SEQ, D], F32, kind="ExternalOutput").ap()
    else:
        out_d = nc.dram_tensor("hout", [2, L, D], F32, kind="ExternalOutput").ap()

    with ExitStack() as st:
        def sb(name, shape, dt):
            return st.enter_context(nc.sbuf_tensor(name, shape, dt))

        def ps(name, shape, dt):
            return st.enter_context(nc.psum_tensor(name, shape, dt))

        S = Sched(nc)

        H = sb("H", [128, NT, D], F32)
        ring = [sb(f"ring{i}", [128, SLOT_EL], BF16) for i in range(NSLOT)]
        identf = sb("identf", [128, 128], F32)
        identb = sb("identb", [128, 128], BF16)
        Umat = sb("Umat", [128, 128], F32)
        Esel = sb("Esel", [128, 2, 128], F32)
        maskneg = sb("maskneg", [128, 128], BF16)
        caus01 = sb("caus01", [128, 128], F32)
        selb = sb("selb", [128, 8, 128], BF16)
        onesf = sb("onesf", [128, 64], F32)
        cosT = sb("cosT", [128, NT, 64], F32)
        sinT = sb("sinT", [128, NT, 64], F32)
        qdec = sb("qdec", [128, 4], F32)
        kdec = sb("kdec", [128, 4], F32)
        gdecn = sb("gdecn", [128, 2, 4], F32)
        epsT = sb("epsT", [128, 1], F32)
        oneT = sb("oneT", [128, 1], F32)
        gB = sb("gB", [128, D], F32)
        gnB = sb("gnB", [128, 512], F32)
        bfB = sb("bfB", [128, 8], F32)
        wlog = sb("wlog", [128, 8, 8], BF16)
        negc = sb("negc", [128, NT, 8], F32)
        crowT = sb("crowT", [128, 528], BF16)
        S32 = sb("S32", [128, 4, 128], F32)
        Sbf = sb("Sbf", [128, 4, 128], BF16)
        ss = [sb(f"ss{i}", [128, 1], F32) for i in range(2)]
        rstd = [sb(f"rstd{i}", [128, 1], F32) for i in range(2)]
        rstd2 = sb("rstd2", [128, NT], F32)
        ssg = sb("ssg", [128, 4], F32)
        rstdg = sb("rstdg", [128, 4], F32)
        tl5 = sb("tl5", [128, 5, 8], F32)
        te5 = sb("te5", [128, 5, 8], F32)
        tl = sb("tl", [128, 8], F32)
        te = sb("te", [128, 8], F32)
        spv = sb("spv", [128, 5, 8], F32)
        st4 = sb("st4", [128, 6, 4], F32)
        tmpA = sb("tmpA", [128, 512], F32)
        tmpB = sb("tmpB", [128, 512], F32)
        tmpC = sb("tmpC", [128, 512], F32)
        ARENA_EL = 38600
        A = sb("arena", [128, ARENA_EL], BF16)
        off = [0]

        def carve(n):
            o = off[0]
            off[0] += n
            assert off[0] <= ARENA_EL, off[0]
            return A[:, o:o + n]

        Kc = carve(4 * L).rearrange("p (c n) -> p c n", c=4)
        vc_off = off[0]
        Vc = carve(NT * 520).rearrange("p (t h e) -> p t h e", t=NT, h=8)
        xnT = carve(8 * 528).rearrange("p (c n) -> p c n", c=8)
        qT = carve(8 * 528).rearrange("p (c n) -> p c n", c=8)
        mixR = carve(4 * 528).rearrange("p (c n) -> p c n", c=4)
        mixF = carve(8 * 528).rearrange("p (c n) -> p c n", c=8)
        PTb = [carve(512) for _ in range(2)]
        xn = [carve(1024) for _ in range(2)]
        qp = carve(512)
        kp = carve(512)
        rvb = carve(512)
        rqkT = carve(8 * 128).rearrange("p (c n) -> p c n", c=8)
        mT = carve(512).rearrange("p (c n) -> p c n", c=4)
        ro = carve(512)
        ROB = [ro, ro]
        ROK = ["ro", "ro"]
        p1_end = off[0]
        off[0] = 0
        hnT = carve(8 * L).rearrange("p (c n) -> p c n", c=8)
        actT = [carve(4 * 512).rearrange("p (c n) -> p c n", c=4) for _ in range(2)]
        xn2 = [carve(1024) for _ in range(2)]
        p2_end = off[0]
        print("arena p1", p1_end, "p2", p2_end, "sbuf remaining", nc.sbuf_bytes_remaining)

        P = [ps(f"P{i}", [128, 512], F32) for i in (0, 1)]
        PTR = ps("PTR", [128, 8, 128], BF16)
        SC = [ps(f"SC{i}", [128, 512], F32) for i in range(2)]
        OTp = ps("OT", [128, 512], F32)
        P6 = ps("P6", [128, 512], F32)
        P7 = ps("P7", [128, 512], F32)
        YB = [OTp, P6, P7]

        cnt = {"u": 0, "sc": 0, "y": 0, "w": 0, "n": 0, "ro": 0}

        def ld(dst, src, key, q="sp"):
            S.dma(q, "c_" + key, dst, src, writes=[key])

        ld(identf[:], cd["c_identf"], "identf")
        ld(Umat[:], cd["c_U"], "Umat")
        ld(Esel[:], cd["c_E"], "Esel")
        ld(caus01[:], cd["c_caus01"], "caus01")
        ld(onesf[:], cd["c_onesf"], "onesf")
        ld(cosT[:], cd["c_cos"], "cosT")
        ld(sinT[:], cd["c_sin"], "sinT")
        ld(qdec[:], cd["c_qdec"], "qdec")
        ld(kdec[:], cd["c_kdec"], "kdec")
        ld(gdecn[:], cd["c_gdecn"], "gdecn")
        ld(identb[:], cd["c_identf"], "identb", q="pool")
        ld(maskneg[:], cd["c_maskneg"], "maskneg", q="pool")
        ld(selb[:], cd["c_sel"], "selb", q="pool")
        S.call("dve", "memset", crowT[:], 0.0, W=["crowT"])
        S.call("dve", "memset", A[:, :], 0.0,
               W=["Kc", "Vc", "Vones", "xnT", "qT", "mixR", "mixF", "PT0", "PT1", "xn0", "xn1", "qp", "kp", "rvb",
                  "rqkT", "mT", "ro", "hnT0", "hnT1", "hnT2", "hnT3", "hnT4", "actT0", "actT1", "xn2_0", "xn2_1"])
        S.call("dve", "memset", epsT[:], EPS,
                   W=["epsT"])
        S.call("dve", "memset", oneT[:], 1.0,
                   W=["oneT"])

        def wload(src_ap, ncol_total, view):
            i = cnt["w"] % NSLOT
            cnt["w"] += 1
            dst = view(ring[i])
            S.dma("pool", f"w{i}", dst, src_ap, writes=[f"ring{i}"])
            return i

        pref = {}

        def wview8(i, ncols):
            return ring[i][:, 0:8 * ncols].rearrange("p (c n) -> p c n", c=8)

        def norm_stats(t, p, jbuf, jkey):
            n = NTOK[t]
            Ht = f"H{t}"
            S.call("dve", "memset", ss[p][0:n, :], 0.0,
                   W=[f"ss{p}"])
            S.call("act", "activation", out=jbuf[0:n, :], in_=H[0:n, t, :], func=AF.Square,
                                               accum_out=ss[p][0:n, :],
                   R=[Ht], W=[f"ss{p}", jkey])
            S.call("act", "activation", out=rstd[p][0:n, :], in_=ss[p][0:n, :], func=AF.Sqrt,
                                               scale=1.0 / D, bias=epsT[0:n, 0:1],
                   R=[f"ss{p}", "epsT"], W=[f"rstd{p}"])
            S.call("dve", "reciprocal", out=rstd[p][0:n, :], in_=rstd[p][0:n, :],
                   R=[f"rstd{p}"], W=[f"rstd{p}"])

        def norm_to_T(t, xnbuf, xnkey, dstT, col, dstkey, pre=False):
            n = NTOK[t]
            p = cnt["n"] % 2
            cnt["n"] += 1
            xb = xnbuf[p]
            if pre:
                rs_ap, rs_key = rstd2[0:n, t:t + 1], f"rs2_{t}"
            else:
                norm_stats(t, p, xb, f"{xnkey}{p}")
                rs_ap, rs_key = rstd[p][0:n, 0:1], f"rstd{p}"
            S.call("dve", "scalar_tensor_tensor", out=xb[0:n, :], in0=H[0:n, t, :],
                   scalar=rs_ap, in1=gB[0:n, :], op0=ALU.mult, op1=ALU.mult,
                   R=[f"H{t}", rs_key, "gB"], W=[f"{xnkey}{p}"])
            for c in range(8):
                S.call("pe", "transpose", out=PTR[:, c, 0:n], in_=xb[0:n, c * 128:(c + 1) * 128],
                                                      identity=identb[0:n, 0:n],
                   R=[f"{xnkey}{p}", "identb"], W=["PTR"])
            S.call("act", "copy", out=dstT[:, :, col:col + n], in_=PTR[:, :, 0:n],
                   R=["PTR"], W=[dstkey])

        def mm_tok(t, colT, srcT, srckey, slot, ncols, outp, outkey):
            n = NTOK[t]
            wv = wview8(slot, ncols)
            for k in range(8):
                S.call("pe", "matmul", outp[0:n, 0:ncols], lhsT=srcT[:, k, colT:colT + n],
                                                   rhs=wv[:, k, :], start=(k == 0), stop=(k == 7),
                   R=[srckey, f"ring{slot}"], W=[outkey])

        def phase1(s, li, first_layer_of_prog):
            S.dma("sp", "c_gB", gB[:], attn_norm_d[li:li + 1, :].broadcast_to([128, D]), writes=["gB"])
            S.dma("sp", "c_gnB", gnB[:], ret_gn_d[li:li + 1, :].broadcast_to([128, 512]), writes=["gnB"])
            S.dma("sp", "c_bfB", bfB[:], b_fgate_d[li:li + 1, :].broadcast_to([128, 8]), writes=["bfB"])
            S.dma("pool", "c_wlog", wlog[:],
                  w_in_d[li, :, 1536:1544].rearrange("(c p) n -> p c n", p=128), writes=["wlog"])
            S.call("dve", "memset", qT[:, :, :], 0.0, W=["qT"])
            S.call("dve", "memset", mixF[64:128, :, :], 0.0, W=["mixF"])
            S.call("dve", "memset", Vc[:, :, :, 64:65], 1.0,
                   W=["Vones"])

            for gi, tiles in enumerate(GROUPS):
                gpos0 = POS0[tiles[0]]
                gcols = {}
                cc = 0
                for t in tiles:
                    gcols[t] = cc
                    cc += NTOK[t]
                gw = cc
                segs = [(0, 16), (16, 528)] if gi == 0 else [(0, 512)]

                if gi == 0:
                    for t in tiles:
                        norm_to_T(t, xn, "xn", xnT, gcols[t], "xnT")

                def win_chunk(c0):
                    return wload(w_in_d[li, :, c0:c0 + 512].rearrange("(c p) n -> p c n", p=128), 512,
                                 lambda r: r[:, 0:4096].rearrange("p (c n) -> p c n", c=8))
                for j, t in enumerate(tiles):
                    n = NTOK[t]
                    col = gcols[t]
                    for k in range(8):
                        S.call("pe", "matmul", P6[0:n, 0:8], lhsT=xnT[:, k, col:col + n], rhs=wlog[:, k, :],
                               start=(k == 0), stop=(k == 7), R=["xnT", "wlog"], W=["P6"])
                    S.call("dve", "tensor_tensor", out=tl5[0:n, j, :], in0=P6[0:n, 0:8], in1=bfB[0:n, :], op=ALU.add,
                           R=["P6", "bfB"], W=[f"tl{j}"])
                for j, t in enumerate(tiles):
                    n = NTOK[t]
                    S.call("act", "activation", out=te5[0:n, j, :], in_=tl5[0:n, j, :], func=AF.Exp, scale=-1.0,
                           R=[f"tl{j}"], W=[f"te{j}"])
                for j, t in enumerate(tiles):
                    n = NTOK[t]
                    S.call("act", "activation", out=spv[0:n, j, :], in_=te5[0:n, j, :], func=AF.Ln,
                           bias=oneT[0:n, 0:1], R=[f"te{j}", "oneT"], W=[f"spv{j}"])
                for which, c0 in (("q", 0), ("k", 512)):
                    if which == "q" and gi == 0 and ("q", li) in pref:
                        slot = pref.pop(("q", li))
                    else:
                        slot = win_chunk(c0)
                    wv = wview8(slot, 512)
                    for pair in range(4):
                        for (a, b) in segs:
                            u = cnt["u"] % 2
                            cnt["u"] += 1
                            w = b - a
                            for k in range(8):
                                S.call("pe", "matmul",
                                    P[u][:, 0:w], lhsT=wv[:, k, pair * 128:(pair + 1) * 128],
                                    rhs=xnT[:, k, a:b], start=(k == 0), stop=(k == 7),
                   R=["xnT", f"ring{slot}"], W=[f"P{u}"])
                            if which == "q":
                                S.call("dve", "tensor_scalar", out=qT[0:64, 2 * pair, a:b], in0=P[u][0:64, 0:w],
                                       scalar1=0.125, scalar2=None, op0=ALU.mult, R=[f"P{u}"], W=["qT"])
                                S.call("dve", "tensor_scalar", out=qT[64:128, 2 * pair + 1, a:b],
                                       in0=P[u][64:128, 0:w], scalar1=0.125, scalar2=None, op0=ALU.mult,
                                       R=[f"P{u}"], W=["qT"])
                            else:
                                S.call("dve", "tensor_copy", out=Kc[:, pair, gpos0 + a:gpos0 + b], in_=P[u][:, 0:w],
                                       R=[f"P{u}"], W=["Kc"])
                slot = win_chunk(1024)

                def emit_crow(t):
                    n = NTOK[t]
                    col = gcols[t]
                    S.call("pe", "matmul", P6[0:8, 16:16 + n], lhsT=negc[0:n, t, :], rhs=identf[0:n, 0:n],
                           start=True, stop=True, R=["negc", "identf"], W=["P6"])
                    S.call("act", "mul", out=crowT[0:8, col:col + n], in_=P6[0:8, 16:16 + n], mul=-1.0,
                           R=["P6"], W=["crowT"])

                for j, t in enumerate(tiles):
                    n = NTOK[t]
                    u = cnt["u"] % 2
                    cnt["u"] += 1
                    mm_tok(t, gcols[t], xnT, "xnT", slot, 512, P[u], f"P{u}")
                    S.call("dve", "tensor_copy", out=Vc[0:n, t, :, 0:64],
                           in_=P[u][0:n, :].rearrange("p (h e) -> p h e", h=8), R=[f"P{u}"], W=["Vc"])
                    S.call("pe", "matmul", P6[0:n, 8:16], lhsT=Umat[0:n, 0:n], rhs=spv[0:n, j, :],
                           start=True, stop=(t == 0), R=[f"spv{j}", "Umat"], W=["P6"])
                    if t > 0:
                        npv = NTOK[t - 1]
                        esel = 0 if t - 1 == 0 else 1
                        S.call("pe", "matmul", P6[0:n, 8:16], lhsT=Esel[0:npv, esel, 0:n], rhs=negc[0:npv, t - 1, :],
                               start=False, stop=True, R=["negc", "Esel"], W=["P6"])
                    S.call("dve", "tensor_copy", out=negc[0:n, t, :], in_=P6[0:n, 8:16], R=["P6"], W=["negc"])
                    if j > 0:
                        emit_crow(tiles[j - 1])
                emit_crow(tiles[-1])

                srq = win_chunk(1544)
                srk = win_chunk(2056)
                srv = win_chunk(2568)
                srg = win_chunk(3080)

                def ret_steps(t):
                    n = NTOK[t]
                    col = gcols[t]
                    rob = cnt["ro"] % 2
                    cnt["ro"] += 1
                    o4 = SC[1][0:n, :].rearrange("p (h e) -> p h e", h=4)

                    def rot(src, dec, dst, dstkey, srckey):
                        s4 = src[0:n, :].rearrange("p (h two e) -> p h two e", h=4, two=2)
                        x1 = s4[:, :, 0, :]
                        x2 = s4[:, :, 1, :]
                        cb = cosT[0:n, t, :].unsqueeze(1).broadcast_to([n, 4, 64])
                        sbb = sinT[0:n, t, :].unsqueeze(1).broadcast_to([n, 4, 64])
                        A4 = tmpA[0:n, :].rearrange("p (h two e) -> p h two e", h=4, two=2)
                        B4 = tmpB[0:n, :].rearrange("p (h two e) -> p h two e", h=4, two=2)
                        S.call("dve", "tensor_tensor", out=A4[:, :, 0, :], in0=x1, in1=cb, op=ALU.mult,
                               R=[srckey, "cosT"], W=["tmpA"])
                        S.call("dve", "tensor_tensor", out=A4[:, :, 1, :], in0=x1, in1=sbb, op=ALU.mult,
                               R=[srckey, "sinT"], W=["tmpA"])
                        S.call("dve", "tensor_tensor", out=B4[:, :, 0, :], in0=x2, in1=sbb, op=ALU.mult,
                               R=[srckey, "sinT"], W=["tmpB"])
                        S.call("dve", "tensor_tensor", out=B4[:, :, 1, :], in0=x2, in1=cb, op=ALU.mult,
                               R=[srckey, "cosT"], W=["tmpB"])
                        S.call("dve", "tensor_tensor", out=A4[:, :, 0, :], in0=A4[:, :, 0, :], in1=B4[:, :, 0, :],
                               op=ALU.subtract, R=["tmpA", "tmpB"], W=["tmpA"])
                        S.call("dve", "tensor_tensor", out=A4[:, :, 1, :], in0=A4[:, :, 1, :], in1=B4[:, :, 1, :],
                               op=ALU.add, R=["tmpA", "tmpB"], W=["tmpA"])
                        S.call("dve", "tensor_tensor",
                               out=dst[0:n, :].rearrange("p (h e) -> p h e", h=4),
                               in0=tmpA[0:n, :].rearrange("p (h e) -> p h e", h=4),
                               in1=dec[0:n, :].unsqueeze(2).broadcast_to([n, 4, 128]), op=ALU.mult,
                               R=["tmpA", "qdec", "kdec"], W=[dstkey])

                    def F0():
                        mm_tok(t, col, xnT, "xnT", srq, 512, P[0], "P0")
                        mm_tok(t, col, xnT, "xnT", srk, 512, P[1], "P1")
                        mm_tok(t, col, xnT, "xnT", srv, 512, SC[0], "SC0")
                        mm_tok(t, col, xnT, "xnT", srg, 512, OTp, "OT")

                    def F0b():
                        S.call("act", "copy", out=rvb[0:n, :], in_=SC[0][0:n, :], R=["SC0"], W=["rvb"])

                    def F1():
                        rot(P[0], qdec, qp, "qp", "P0")

                    def F3():
                        rot(P[1], kdec, kp, "kp", "P1")

                    def F4():
                        for c in range(4):
                            S.call("pe", "transpose", out=PTR[:, c, 0:n], in_=qp[0:n, c * 128:(c + 1) * 128],
                                   identity=identb[0:n, 0:n], R=["qp", "identb"], W=["PTR"])
                        for c in range(4):
                            S.call("pe", "transpose", out=PTR[:, 4 + c, 0:n], in_=kp[0:n, c * 128:(c + 1) * 128],
                                   identity=identb[0:n, 0:n], R=["kp", "identb"], W=["PTR"])
                        S.call("act", "copy", out=rqkT[:, :, 0:n], in_=PTR[:, :, 0:n], R=["PTR"], W=["rqkT"])

                    def F5():
                        for hh in range(4):
                            S.call("pe", "matmul", SC[0][0:n, hh * 128:hh * 128 + n],
                                   lhsT=rqkT[:, 4 + hh, 0:n], rhs=rqkT[:, hh, 0:n], start=True, stop=True,
                                   R=["rqkT"], W=["SC0"])
                        S.call("dve", "tensor_tensor", out=mT[0:n, :, 0:n],
                               in0=SC[0][0:n, :].rearrange("p (h e) -> p h e", h=4)[:, :, 0:n],
                               in1=caus01[0:n, 0:n].unsqueeze(1).broadcast_to([n, 4, n]), op=ALU.mult,
                               R=["SC0", "caus01"], W=["mT"])

                    def F6():
                        for hh in range(4):
                            S.call("pe", "matmul", SC[1][0:n, hh * 128:(hh + 1) * 128], lhsT=mT[0:n, hh, 0:n],
                                   rhs=rvb[0:n, hh * 128:(hh + 1) * 128], start=True, stop=(t == 0),
                                   R=["mT", "rvb"], W=["SC1"])
                            if t > 0:
                                S.call("pe", "matmul", SC[1][0:n, hh * 128:(hh + 1) * 128],
                                       lhsT=rqkT[:, hh, 0:n], rhs=Sbf[:, hh, :], start=False, stop=True,
                                       R=["rqkT", "Sbf"], W=["SC1"])
                        for hh in range(4):
                            S.call("pe", "matmul", P7[:, hh * 128:(hh + 1) * 128],
                                   lhsT=kp[0:n, hh * 128:(hh + 1) * 128], rhs=rvb[0:n, hh * 128:(hh + 1) * 128],
                                   start=True, stop=True, R=["kp", "rvb"], W=["P7"])

                    def F7():
                        S.call("act", "activation", out=tmpC[0:n, :], in_=OTp[0:n, :], func=AF.Silu,
                               R=["OT"], W=["tmpC"])
                        S.call("dve", "tensor_tensor", out=tmpC[0:n, :], in0=tmpC[0:n, :], in1=gnB[0:n, :],
                               op=ALU.mult, R=["tmpC", "gnB"], W=["tmpC"])

                    def B0():
                        S4v = S32[:].rearrange("p h e -> p (h e)")
                        if t == 0:
                            S.call("dve", "tensor_copy", out=S4v, in_=P7[:, :], R=["P7"], W=["S32"])
                        else:
                            S.call("dve", "tensor_tensor", out=S4v, in0=S4v, in1=P7[:, :], op=ALU.add,
                                   R=["P7", "S32"], W=["S32"])
                        gsel = 0 if t == 0 else 1
                        S.call("dve", "tensor_tensor", out=S32[:], in0=S32[:],
                               in1=gdecn[:, gsel, :].unsqueeze(2).broadcast_to([128, 4, 128]), op=ALU.mult,
                               R=["S32", "gdecn"], W=["S32"])
                        S.call("act", "copy", out=Sbf[:], in_=S32[:], R=["S32"], W=["Sbf"])

                    def B1():
                        S.call("dve", "tensor_reduce", out=st4[0:n, 0, :], in_=o4, axis=AX.X, op=ALU.add,
                               R=["SC1"], W=["st4"])
                        S.call("act", "activation", out=P6[0:n, :], in_=SC[1][0:n, :], func=AF.Square,
                               R=["SC1"], W=["P6"])

                    def B2():
                        S.call("dve", "tensor_reduce", out=st4[0:n, 1, :],
                               in_=P6[0:n, :].rearrange("p (h e) -> p h e", h=4), axis=AX.X, op=ALU.add,
                               R=["P6"], W=["st4"])
                        S.call("dve", "tensor_scalar", out=st4[0:n, 2, :], in0=st4[0:n, 0, :],
                               scalar1=1.0 / 128, scalar2=None, op0=ALU.mult, R=["st4"], W=["st4"])
                        S.call("dve", "tensor_tensor", out=st4[0:n, 3, :], in0=st4[0:n, 2, :], in1=st4[0:n, 2, :],
                               op=ALU.mult, R=["st4"], W=["st4"])
                        S.call("dve", "scalar_tensor_tensor", out=st4[0:n, 4, :], in0=st4[0:n, 1, :],
                               scalar=1.0 / 128, in1=st4[0:n, 3, :], op0=ALU.mult, op1=ALU.subtract,
                               R=["st4"], W=["st4"])
                        S.call("act", "activation", out=st4[0:n, 5, :], in_=st4[0:n, 4, :], func=AF.Sqrt,
                               bias=epsT[0:n, 0:1], scale=1.0, R=["st4", "epsT"], W=["st4"])

                    def B3():
                        S.call("dve", "reciprocal", out=st4[0:n, 5, :], in_=st4[0:n, 5, :], R=["st4"], W=["st4"])
                        S.call("dve", "scalar_tensor_tensor", out=st4[0:n, 3, :], in0=st4[0:n, 2, :], scalar=-1.0,
                               in1=st4[0:n, 5, :], op0=ALU.mult, op1=ALU.mult, R=["st4"], W=["st4"])
                        for hh in range(4):
                            S.call("act", "activation", out=P6[0:n, hh * 128:(hh + 1) * 128],
                                   in_=SC[1][0:n, hh * 128:(hh + 1) * 128], func=AF.Identity,
                                   scale=st4[0:n, 5, hh:hh + 1], bias=st4[0:n, 3, hh:hh + 1],
                                   R=["SC1", "st4"], W=["P6"])

                    def B4():
                        S.call("dve", "tensor_tensor", out=ROB[rob][0:n, :], in0=P6[0:n, :], in1=tmpC[0:n, :],
                               op=ALU.mult, R=["P6", "tmpC"], W=[ROK[rob]])

                    def B5():
                        for c in range(4):
                            S.call("pe", "transpose", out=PTR[:, c, 0:n], in_=ROB[rob][0:n, c * 128:(c + 1) * 128],
                                   identity=identb[0:n, 0:n], R=[ROK[rob], "identb"], W=["PTR"])
                        S.call("act", "copy", out=mixR[:, :, col:col + n], in_=PTR[:, 0:4, 0:n],
                               R=["PTR"], W=["mixR"])

                    return dict(F0=F0, F0b=F0b, F1=F1, F3=F3, F4=F4, F5=F5, F6=F6, F7=F7,
                                B0=B0, B1=B1, B2=B2, B3=B3, B4=B4, B5=B5)

                prev = None
                for t in tiles:
                    cur = ret_steps(t)
                    cur["F0"]()
                    if prev:
                        prev["B0"]()
                        prev["B1"]()
                    cur["F0b"]()
                    cur["F1"]()
                    if prev:
                        prev["B2"]()
                    cur["F3"]()
                    cur["F4"]()
                    if prev:
                        prev["B3"]()
                    cur["F5"]()
                    if prev:
                        prev["B4"]()
                    cur["F6"]()
                    cur["F7"]()
                    if prev:
                        prev["B5"]()
                    prev = cur
                prev["B0"]()
                tail_steps = [prev[k] for k in ("B1", "B2", "B3", "B4", "B5")]
                if gi == 0:
                    while tail_steps:
                        tail_steps.pop(0)()

                SCB = [SC[0], P[0], P[1]]
                SCK = ["SC0", "P0", "P1"]
                OTB = [OTp, P7]
                OTK = ["OT", "P7"]
                LOOK = 2
                n1_sched = {}
                att_ctr = [0]
                if gi + 1 < len(GROUPS):
                    nxt = GROUPS[gi + 1]
                    S.call("dve", "memset", ssg[:, :], 0.0, W=["ssg"])
                    for jj, t in enumerate(nxt):
                        S.call("act", "activation", out=xn[0][:, :], in_=H[:, t, :], func=AF.Square,
                               accum_out=ssg[:, jj:jj + 1], R=[f"H{t}"], W=["ssg", "xn0"])
                    S.call("act", "activation", out=rstdg[:, :], in_=ssg[:, :], func=AF.Sqrt,
                           scale=1.0 / D, bias=epsT[:, 0:1], R=["ssg", "epsT"], W=["rstdg"])
                    S.call("dve", "reciprocal", out=rstdg[:, :], in_=rstdg[:, :], R=["rstdg"], W=["rstdg"])

                    def n1_A(jj, t):
                        xi = jj % 2
                        S.call("dve", "scalar_tensor_tensor", out=xn[xi][:, :], in0=H[:, t, :],
                               scalar=rstdg[:, jj:jj + 1], in1=gB[:, :], op0=ALU.mult, op1=ALU.mult,
                               R=[f"H{t}", "rstdg", "gB"], W=[f"xn{xi}"])

                    def n1_B(jj, t):
                        xi = jj % 2
                        for c in range(8):
                            S.call("pe", "transpose", out=PTR[:, c, :], in_=xn[xi][:, c * 128:(c + 1) * 128],
                                   identity=identb[:, :], R=[f"xn{xi}", "identb"], W=["PTR"])
                        S.call("dve", "tensor_copy", out=xnT[:, :, 128 * jj:128 * jj + 128], in_=PTR[:, :, :],
                               R=["PTR"], W=["xnT"])

                    plan = [(0, "A", 0), (1, "A", 1), (5, "B", 0), (6, "A", 2), (10, "B", 1), (11, "A", 3),
                            (15, "B", 2), (20, "B", 3)]
                    for (at, kind, jj) in plan:
                        fn = n1_A if kind == "A" else n1_B
                        n1_sched.setdefault(at, []).append((fn, jj, nxt[jj]))

                def n1_tick():
                    for (fn, jj, tt) in n1_sched.pop(att_ctr[0], []):
                        fn(jj, tt)
                    att_ctr[0] += 1

                for (a, b) in segs:
                    w = b - a
                    pa0 = gpos0 + a
                    ktiles = [t for t in range(NT) if POS0[t] < pa0 + w]
                    items = [(h, kt) for h in range(8) for kt in ktiles]
                    geo = {}
                    for kt in ktiles:
                        nk = NTOK[kt]
                        kp0 = POS0[kt]
                        if kp0 < pa0:
                            units = [(a, b, False)]
                            lo = a
                        else:
                            kc = a + (kp0 - pa0)
                            units = [(kc, kc + nk, True)]
                            if kc + nk < b:
                                units.append((kc + nk, b, False))
                            lo = kc
                        geo[kt] = (nk, kp0, units, lo)

                    def emit_scores(i):
                        h, kt = items[i]
                        nk, kp0, units, lo = geo[kt]
                        pr0 = 0 if h % 2 == 0 else 64
                        pair = h // 2
                        u = i % 3
                        for (ua, ub, diag) in units:
                            S.call("pe", "matmul",
                                   SCB[u][0:nk, ua - a:ub - a], lhsT=Kc[:, pair, kp0:kp0 + nk],
                                   rhs=qT[:, h, ua:ub], start=True, stop=False,
                                   R=["Kc", "qT"], W=[SCK[u]])
                            S.call("pe", "matmul",
                                   SCB[u][0:nk, ua - a:ub - a], lhsT=selb[:, h, 0:nk],
                                   rhs=crowT[:, ua:ub], start=False, stop=(not diag),
                                   R=["selb", "crowT"], W=[SCK[u]])
                            if diag:
                                S.call("pe", "matmul",
                                       SCB[u][0:nk, ua - a:ub - a], lhsT=identb[0:nk, 0:nk],
                                       rhs=maskneg[0:nk, 0:nk], start=False, stop=True,
                                       R=["identb", "maskneg"], W=[SCK[u]])

                    def emit_rest(i):
                        h, kt = items[i]
                        nk, kp0, units, lo = geo[kt]
                        u = i % 3
                        pu = i % 2
                        ob = h % 2
                        S.call("act", "activation",
                               out=PTb[pu][0:nk, lo - a:b - a], in_=SCB[u][0:nk, lo - a:b - a], func=AF.Exp,
                               bias=negc[0:nk, kt, h:h + 1], scale=1.0,
                               R=[SCK[u], "negc"], W=[f"PT{pu}"])
                        S.call("pe", "matmul",
                               OTB[ob][:, lo - a:b - a],
                               lhsT=A[0:nk, vc_off + (kt * 8 + h) * 65:vc_off + (kt * 8 + h) * 65 + 128],
                               rhs=PTb[pu][0:nk, lo - a:b - a],
                               start=(kt == ktiles[0]), stop=(kt == ktiles[-1]), skip_group_check=True,
                               R=[f"PT{pu}", "Vc", "Vones"], W=[OTK[ob]])

                    def emit_norm(h):
                        ob = h % 2
                        S.call("dve", "reciprocal", out=tmpC[64:65, 0:w], in_=OTB[ob][64:65, 0:w],
                               R=[OTK[ob]], W=["tmpC"])
                        S.call("pe", "matmul", P6[0:64, 0:w], lhsT=onesf[64:65, 0:64], rhs=tmpC[64:65, 0:w],
                               start=True, stop=True, R=["tmpC", "onesf"], W=["P6"])
                        S.call("dve", "tensor_copy", out=tmpA[0:64, 0:w], in_=OTB[ob][0:64, 0:w],
                               R=[OTK[ob]], W=["tmpA"])
                        S.call("dve", "tensor_tensor",
                               out=mixF[0:64, h, a:b], in0=tmpA[0:64, 0:w], in1=P6[0:64, 0:w], op=ALU.mult,
                               R=["tmpA", "P6"], W=["mixF"])

                    pending = []
                    nit = len(items)
                    for i in range(nit + LOOK):
                        if i < nit:
                            emit_scores(i)
                        j = i - LOOK
                        if j >= 0:
                            emit_rest(j)
                            if items[j][1] == ktiles[-1]:
                                pending.append((j + min(3, len(ktiles)), items[j][0]))
                        while pending and pending[0][0] <= j:
                            emit_norm(pending.pop(0)[1])
                        if tail_steps and i % 2 == 1:
                            tail_steps.pop(0)()
                        n1_tick()
                    for _, hh in pending:
                        emit_norm(hh)
                while tail_steps:
                    tail_steps.pop(0)()
                for at in sorted(n1_sched):
                    for (fn, jj, tt) in n1_sched[at]:
                        fn(jj, tt)
                n1_sched.clear()

                sfa = wload(w_out_d[li, 0:256, :].rearrange("(h p) n -> p h n", p=64), 0,
                            lambda r: r[0:64, 0:4096].rearrange("p (c n) -> p c n", c=4))
                sfb = wload(w_out_d[li, 256:512, :].rearrange("(h p) n -> p h n", p=64), 0,
                            lambda r: r[0:64, 0:4096].rearrange("p (c n) -> p c n", c=4))
                srt = wload(w_out_d[li, 512:1024, :].rearrange("(c p) n -> p c n", p=128), 0,
                            lambda r: r[:, 0:4096].rearrange("p (c n) -> p c n", c=4))
                wfa = ring[sfa][:, 0:4096].rearrange("p (c n) -> p c n", c=4)
                wfb = ring[sfb][:, 0:4096].rearrange("p (c n) -> p c n", c=4)
                wrt = ring[srt][:, 0:4096].rearrange("p (c n) -> p c n", c=4)
                for t in tiles:
                    n = NTOK[t]
                    col = gcols[t]
                    for half in range(2):
                        yb = cnt["y"] % 3
                        cnt["y"] += 1
                        ybk = ["OT", "P6", "P7"][yb]
                        hs = slice(half * 512, (half + 1) * 512)
                        for h in range(8):
                            wsrc = wfa if h < 4 else wfb
                            sk = sfa if h < 4 else sfb
                            S.call("pe", "matmul",
                                YB[yb][0:n, :], lhsT=mixF[:, h, col:col + n], rhs=wsrc[:, h % 4, hs],
                                start=(h == 0), stop=False,
                   R=["mixF", f"ring{sk}"], W=[ybk])
                        for c in range(4):
                            S.call("pe", "matmul",
                                YB[yb][0:n, :], lhsT=mixR[:, c, col:col + n], rhs=wrt[:, c, hs],
                                start=False, stop=(c == 3),
                   R=["mixR", f"ring{srt}"], W=[ybk])
                        S.call("dve", "tensor_tensor",
                            out=H[0:n, t, hs], in0=H[0:n, t, hs], in1=YB[yb][0:n, :], op=ALU.add,
                   R=[ybk, f"H{t}"], W=[f"H{t}"])
                    S.call("dve", "memset", rstd2[0:n, t:t + 1], 0.0, W=[f"rs2_{t}"])
                    S.call("act", "activation", out=xn[1][0:n, :], in_=H[0:n, t, :], func=AF.Square,
                           accum_out=rstd2[0:n, t:t + 1], R=[f"H{t}"], W=[f"rs2_{t}", "xn1"])
                    S.call("act", "activation", out=rstd2[0:n, t:t + 1], in_=rstd2[0:n, t:t + 1], func=AF.Sqrt,
                           scale=1.0 / D, bias=epsT[0:n, 0:1], R=[f"rs2_{t}", "epsT"], W=[f"rs2_{t}"])
                    S.call("dve", "reciprocal", out=rstd2[0:n, t:t + 1], in_=rstd2[0:n, t:t + 1],
                           R=[f"rs2_{t}"], W=[f"rs2_{t}"])

        def phase2(s, li):
            S.dma("sp", "c_gB", gB[:], ffn_norm_d[li:li + 1, :].broadcast_to([128, D]), writes=["gB"])
            segs = [(0, 16), (16, 528), (528, 1040), (1040, 1552), (1552, 2064)]
            seg_tiles = [[0], [1, 2, 3, 4], [5, 6, 7, 8], [9, 10, 11, 12], [13, 14, 15, 16]]

            def norm_seg(si):
                for t in seg_tiles[si]:
                    norm_to_T(t, xn2, "xn2_", hnT, POS0[t], f"hnT{si}", pre=True)

            norm_seg(0)
            norm_seg(1)
            first_fg = True
            f0 = 0
            while f0 < DFF:
                fw = min(512, DFF - f0)
                nfc = fw // 128
                if f0 == 0 and ("wg", li) in pref:
                    sg = pref.pop(("wg", li))
                else:
                    sg = wload(w_gate_d[li, :, f0:f0 + fw].rearrange("(c p) n -> p c n", p=128), fw,
                               lambda r, fw=fw: r[:, 0:8 * fw].rearrange("p (c n) -> p c n", c=8))
                su = wload(w_up_d[li, :, f0:f0 + fw].rearrange("(c p) n -> p c n", p=128), fw,
                           lambda r, fw=fw: r[:, 0:8 * fw].rearrange("p (c n) -> p c n", c=8))
                sd = wload(w_down_d[li, f0:f0 + fw, :].rearrange("(c p) n -> p c n", p=128), fw,
                           lambda r, nfc=nfc: r[:, 0:nfc * 1024].rearrange("p (c n) -> p c n", c=nfc))
                wg = wview8(sg, fw)
                wu = wview8(su, fw)
                wd = ring[sd][:, 0:nfc * 1024].rearrange("p (c n) -> p c n", c=nfc)
                down_pending = None
                for si, (a, b) in enumerate(segs):
                    w = b - a
                    ab = cnt["u"] % 2
                    cnt["u"] += 1
                    if first_fg and si + 2 < len(segs):
                        norm_seg(si + 2)
                    for fc in range(nfc):
                        for k in range(8):
                            S.call("pe", "matmul",
                                P[0][:, 0:w], lhsT=wg[:, k, fc * 128:(fc + 1) * 128], rhs=hnT[:, k, a:b],
                                start=(k == 0), stop=(k == 7),
                   R=[f"hnT{si}", f"ring{sg}"], W=["P0"])
                        for k in range(8):
                            S.call("pe", "matmul",
                                P[1][:, 0:w], lhsT=wu[:, k, fc * 128:(fc + 1) * 128], rhs=hnT[:, k, a:b],
                                start=(k == 0), stop=(k == 7),
                   R=[f"hnT{si}", f"ring{su}"], W=["P1"])
                        S.call("act", "activation", out=tmpA[:, 0:w], in_=P[0][:, 0:w], func=AF.Silu,
                   R=["P0"], W=["tmpA"])
                        S.call("dve", "tensor_tensor",
                            out=actT[ab][:, fc, 0:w], in0=tmpA[:, 0:w], in1=P[1][:, 0:w], op=ALU.mult,
                   R=["tmpA", "P1"], W=[f"actT{ab}"])
                    def emit_down(si=si, a=a, ab=ab):
                        for t in seg_tiles[si]:
                            n = NTOK[t]
                            tc0 = POS0[t] - a
                            for half in range(2):
                                yb = cnt["y"] % 3
                                cnt["y"] += 1
                                ybk = ["OT", "P6", "P7"][yb]
                                hs = slice(half * 512, (half + 1) * 512)
                                for fc in range(nfc):
                                    S.call("pe", "matmul",
                                           YB[yb][0:n, :], lhsT=actT[ab][:, fc, tc0:tc0 + n], rhs=wd[:, fc, hs],
                                           start=(fc == 0), stop=(fc == nfc - 1),
                                           R=[f"actT{ab}", f"ring{sd}"], W=[ybk])
                                S.call("dve", "tensor_tensor",
                                       out=H[0:n, t, hs], in0=H[0:n, t, hs], in1=YB[yb][0:n, :], op=ALU.add,
                                       R=[ybk, f"H{t}"], W=[f"H{t}"])

                    if down_pending is not None:
                        down_pending()
                    down_pending = emit_down
                if down_pending is not None:
                    down_pending()
                    down_pending = None
                f0 += fw
                first_fg = False

        for s in range(2):
            if first:
                S.dma("sp", "ldx", H[:, 1:NT, :], x_d[s].rearrange("(t p) d -> p t d", p=128),
                      writes=[f"H{t}" for t in range(1, NT)])
                S.dma("sp", "ldx", H[0:16, 0, :], meta_d, writes=["H0"])
            else:
                S.dma("sp", "ldx", H[:, 1:NT, :], hin_d[s, 16:L, :].rearrange("(t p) d -> p t d", p=128),
                      writes=[f"H{t}" for t in range(1, NT)])
                S.dma("sp", "ldx", H[0:16, 0, :], hin_d[s, 0:16, :], writes=["H0"])
            for li in layer_ids:
                phase1(s, li, False)
                pref[("wg", li)] = wload(w_gate_d[li, :, 0:512].rearrange("(c p) n -> p c n", p=128), 512,
                                         lambda r: r[:, 0:4096].rearrange("p (c n) -> p c n", c=8))
                S.barrier()
                phase2(s, li)
                idx = layer_ids.index(li)
                nli = layer_ids[idx + 1] if idx + 1 < len(layer_ids) else (layer_ids[0] if s == 0 else None)
                if nli is not None:
                    pref[("q", nli)] = wload(w_in_d[nli, :, 0:512].rearrange("(c p) n -> p c n", p=128), 512,
                                             lambda r: r[:, 0:4096].rearrange("p (c n) -> p c n", c=8))
                S.barrier()
            if last:
                S.dma("sp", "c_gB", gB[:], final_norm_d[0:1, :].broadcast_to([128, D]), writes=["gB"])
                for t in range(1, NT):
                    p = cnt["n"] % 2
                    cnt["n"] += 1
                    norm_stats(t, p, xn[p], f"xn{p}")
                    ob = [tmpA, tmpB][p]
                    obk = ["tmpA", "tmpB"][p]
                    for half in range(2):
                        hs = slice(half * 512, (half + 1) * 512)
                        S.call("dve", "scalar_tensor_tensor",
                            out=ob[:, :], in0=H[:, t, hs], scalar=rstd[p][:, 0:1], in1=gB[:, hs],
                            op0=ALU.mult, op1=ALU.mult,
                   R=[f"H{t}", f"rstd{p}", "gB"], W=[obk])
                        S.dma("sp", "st", out_d[s, (t - 1) * 128:t * 128, hs], ob[:, :], reads=[obk])
            else:
                S.dma("sp", "st", out_d[s, 16:L, :].rearrange("(t p) d -> p t d", p=128), H[:, 1:NT, :],
                      reads=[f"H{t}" for t in range(1, NT)])
                S.dma("sp", "st", out_d[s, 0:16, :], H[0:16, 0, :], reads=["H0"])
            S.barrier()
        S.final_wait("sp", ["st"])
        n_ops = {e: len(S.ops[e]) for e in S.ENG}
        print("ops per engine", n_ops)
        S.emit(st)
    return nc


_CACHE = {}


def _get_prog(key):
    if key not in _CACHE:
        _CACHE[key] = build(*key)
    return _CACHE[key]


def kernel(x, meta_tokens, attn_norm, w_in, b_fgate, ret_gn, w_out, ffn_norm, w_gate, w_up, w_down, final_norm):
    f = lambda a: np.ascontiguousarray(np.asarray(a, dtype=np.float32))
    consts = make_consts()
    shared = {
        "attn_norm": f(attn_norm), "w_in": f(w_in), "b_fgate": f(b_fgate), "ret_gn": f(ret_gn),
        "w_out": f(w_out), "ffn_norm": f(ffn_norm), "w_gate": f(w_gate), "w_up": f(w_up),
        "w_down": f(w_down), "final_norm": f(final_norm).reshape(1, D),
    }
    shared.update(consts)
    x = f(x)
    meta = f(meta_tokens)
    cores = list(range(NCORES))
    if MODE == "fused":
        nc = _get_prog(((0, 1), True, True))
        in_maps = [dict(shared, x=x[2 * c:2 * c + 2], meta=meta) for c in cores]
        res = run_bass_kernel_spmd(nc, in_maps, core_ids=cores)
        return np.concatenate([r["out"] for r in res.results], axis=0)
    else:
        nc0 = _get_prog(((0,), True, False))
        in_maps = [dict(shared, x=x[2 * c:2 * c + 2], meta=meta) for c in cores]
        res0 = run_bass_kernel_spmd(nc0, in_maps, core_ids=cores)
        nc1 = _get_prog(((1,), False, True))
        in_maps = [dict(shared, hin=res0.results[c]["hout"]) for c in cores]
        res1 = run_bass_kernel_spmd(nc1, in_maps, core_ids=cores)
        return np.concatenate([r["out"] for r in res1.results], axis=0)
```

```python
import math
from contextlib import ExitStack

import numpy as np
import concourse.bass as bass
import concourse.mybir as mybir
from concourse.bass_utils import run_bass_kernel_spmd

F32 = mybir.dt.float32
BF16 = mybir.dt.bfloat16
AF = mybir.ActivationFunctionType
ALU = mybir.AluOpType
AX = mybir.AxisListType

NCORES = 8
D = 1024
SEQ = 2048
NMETA = 16
L = SEQ + NMETA
NT = 17
DFF = 2816
INC = 3592
EPS = 1e-6
NTOK = [16] + [128] * 16
POS0 = [0] + [16 + 128 * i for i in range(16)]
GROUPS = [[0, 1, 2, 3, 4], [5, 6, 7, 8], [9, 10, 11, 12], [13, 14, 15, 16]]
GAMMA = [1.0 - 2.0 ** (-5 - h) for h in range(4)]
NSLOT = 4
SLOT_EL = 4096

MODE = "fused"


class Sched:
    ENG = ("pe", "act", "dve", "pool", "sp")

    def __init__(self, nc):
        self.nc = nc
        self.ops = {e: [] for e in self.ENG}
        self.state = {}
        self.waited = {e: {} for e in self.ENG}
        self.dma_cnt = {}

    def _filter(self, eng, deps):
        out = []
        wd = self.waited[eng]
        for t in deps:
            if t[0] == "eng":
                _, e, idx = t
                if e == eng and e == "pe":
                    continue
                if wd.get(e, -1) >= idx:
                    continue
                wd[e] = idx
                self.ops[e][idx]["signal"] = True
                out.append(t)
            else:
                _, name, val = t
                if wd.get(name, -1) >= val:
                    continue
                wd[name] = val
                out.append(t)
        return out

    def _deps(self, eng, reads, writes):
        deps = []
        for k in reads:
            st = self.state.get(k)
            if st and st["w"] is not None:
                deps.append(st["w"])
        for k in writes:
            st = self.state.get(k)
            if st:
                if st["w"] is not None:
                    deps.append(st["w"])
                deps.extend(st["r"])
        return self._filter(eng, deps)

    def _commit(self, tok, reads, writes):
        for k in reads:
            st = self.state.setdefault(k, {"w": None, "r": []})
            if tok[0] == "eng":
                st["r"] = [t for t in st["r"] if not (t[0] == "eng" and t[1] == tok[1])]
            st["r"].append(tok)
        for k in writes:
            self.state[k] = {"w": tok, "r": []}

    def op(self, eng, fn, reads=(), writes=()):
        waits = self._deps(eng, reads, writes)
        idx = len(self.ops[eng])
        self.ops[eng].append({"fn": fn, "waits": waits, "signal": False, "dma": None})
        self._commit(("eng", eng, idx), reads, writes)

    def call(self, eng, name, *args, R=(), W=(), **kwargs):
        self.op(eng, (lambda e: getattr(e, name)(*args, **kwargs)), reads=R, writes=W)

    def dma(self, q, sem, out_ap, in_ap, reads=(), writes=()):
        waits = self._deps(q, reads, writes)
        self.dma_cnt[sem] = self.dma_cnt.get(sem, 0) + 16
        val = self.dma_cnt[sem]
        self.ops[q].append({"fn": (lambda e: e.dma_start(out=out_ap, in_=in_ap)),
                            "waits": waits, "signal": False, "dma": sem})
        self._commit(("sem", sem, val), reads, writes)

    def barrier(self):
        toks = []
        for e in self.ENG:
            for idx in range(len(self.ops[e]) - 1, -1, -1):
                o = self.ops[e][idx]
                if o["dma"] is None and o["fn"] is not None:
                    toks.append(("eng", e, idx))
                    break
        for s, v in self.dma_cnt.items():
            toks.append(("sem", s, v))
        for e in self.ENG:
            waits = self._filter(e, [t for t in toks if not (t[0] == "eng" and t[1] == e)])
            self.ops[e].append({"fn": None, "waits": waits, "signal": False, "dma": None})

    def final_wait(self, eng, sems):
        waits = [("sem", s, self.dma_cnt[s]) for s in sems if s in self.dma_cnt]
        self.ops[eng].append({"fn": None, "waits": waits, "signal": False, "dma": None})

    def emit(self, stack):
        nc = self.nc
        esem = {e: stack.enter_context(nc.semaphore("s_" + e)) for e in self.ENG}
        dsem = {s: stack.enter_context(nc.semaphore("d_" + s)) for s in self.dma_cnt}
        block = stack.enter_context(nc.Block())
        sigval = {}
        for e in self.ENG:
            c = 0
            for i, o in enumerate(self.ops[e]):
                if o["signal"]:
                    c += 1
                    sigval[(e, i)] = c

        def run(e, engobj):
            for i, o in enumerate(self.ops[e]):
                for t in o["waits"]:
                    if t[0] == "eng":
                        engobj.wait_ge(esem[t[1]], sigval[(t[1], t[2])])
                    else:
                        engobj.wait_ge(dsem[t[1]], t[2])
                if o["fn"] is None:
                    continue
                ins = o["fn"](engobj)
                if o["dma"] is not None:
                    ins.then_inc(dsem[o["dma"]], 16)
                elif o["signal"]:
                    ins.then_inc(esem[e], 1)

        @block.tensor
        def _(pe):
            run("pe", pe)

        @block.scalar
        def _(act):
            run("act", act)

        @block.vector
        def _(dve):
            run("dve", dve)

        @block.gpsimd
        def _(pool):
            run("pool", pool)

        @block.sync
        def _(sp):
            run("sp", sp)


def make_consts():
    c = {}
    c["c_identf"] = np.eye(128, dtype=np.float32)
    s = np.arange(128)[:, None]
    t = np.arange(128)[None, :]
    c["c_U"] = (s <= t).astype(np.float32)
    E = np.zeros((128, 2, 128), np.float32)
    E[15, 0, :] = 1.0
    E[127, 1, :] = 1.0
    c["c_E"] = E
    c["c_maskneg"] = np.where(s <= t, 0.0, -30000.0).astype(np.float32)
    c["c_caus01"] = (s <= t).astype(np.float32)
    sel = np.zeros((128, 8, 128), np.float32)
    for h in range(8):
        sel[h, h, :] = 1.0
    c["c_sel"] = sel
    c["c_onesf"] = np.ones((128, 64), np.float32)
    inv_freq = (10000.0 ** (-np.arange(0, 128, 2, dtype=np.float32) / 128.0)).astype(np.float32)
    pos = np.zeros((128, NT), np.float32)
    for ti in range(NT):
        pos[:, ti] = POS0[ti] + np.arange(128)
    ang = (pos[:, :, None].astype(np.float32) * inv_freq[None, None, :]).astype(np.float32)
    c["c_cos"] = np.cos(ang).astype(np.float32)
    c["c_sin"] = np.sin(ang).astype(np.float32)
    j = np.arange(128, dtype=np.float64)
    lg = np.array([math.log(g) for g in GAMMA])
    c["c_qdec"] = np.exp((j[:, None] + 1.0) * lg[None, :]).astype(np.float32)
    c["c_kdec"] = (np.exp(-(j[:, None] + 1.0) * lg[None, :]) * (128.0 ** -0.5)).astype(np.float32)
    gd = np.zeros((128, 2, 4), np.float32)
    gd[:, 0, :] = np.exp(16.0 * lg)[None, :]
    gd[:, 1, :] = np.exp(128.0 * lg)[None, :]
    c["c_gdecn"] = gd
    return c


CONST_SHAPES = {
    "c_identf": [128, 128], "c_U": [128, 128], "c_E": [128, 2, 128], "c_maskneg": [128, 128],
    "c_caus01": [128, 128], "c_sel": [128, 8, 128], "c_onesf": [128, 64],
    "c_cos": [128, NT, 64], "c_sin": [128, NT, 64], "c_qdec": [128, 4], "c_kdec": [128, 4],
    "c_gdecn": [128, 2, 4],
}


def build(layer_ids, first, last):
    nc = bass.Bass("TRN2", target_bir_lowering=False)
    NL = 2

    def din(name, shape):
        return nc.dram_tensor(name, shape, F32, kind="ExternalInput").ap()

    if first:
        x_d = din("x", [2, SEQ, D])
        meta_d = din("meta", [NMETA, D])
    else:
        hin_d = din("hin", [2, L, D])
    attn_norm_d = din("attn_norm", [NL, D])
    w_in_d = din("w_in", [NL, D, INC])
    b_fgate_d = din("b_fgate", [NL, 8])
    ret_gn_d = din("ret_gn", [NL, 512])
    w_out_d = din("w_out", [NL, D, D])
    ffn_norm_d = din("ffn_norm", [NL, D])
    w_gate_d = din("w_gate", [NL, D, DFF])
    w_up_d = din("w_up", [NL, D, DFF])
    w_down_d = din("w_down", [NL, DFF, D])
    final_norm_d = din("final_norm", [1, D])
    cd = {k: din(k, v) for k, v in CONST_SHAPES.items()}
    if last:
        out_d = nc.dram_tensor("out", [2, SEQ, D], F32, kind="ExternalOutput").ap()
    else:
        out_d = nc.dram_tensor("hout", [2, L, D], F32, kind="ExternalOutput").ap()

    with ExitStack() as st:
        def sb(name, shape, dt):
            return st.enter_context(nc.sbuf_tensor(name, shape, dt))

        def ps(name, shape, dt):
            return st.enter_context(nc.psum_tensor(name, shape, dt))

        S = Sched(nc)

        H = sb("H", [128, NT, D], F32)
        ring = [sb(f"ring{i}", [128, SLOT_EL], BF16) for i in range(NSLOT)]
        identf = sb("identf", [128, 128], F32)
        identb = sb("identb", [128, 128], BF16)
        Umat = sb("Umat", [128, 128], F32)
        Esel = sb("Esel", [128, 2, 128], F32)
        maskneg = sb("maskneg", [128, 128], BF16)
        caus01 = sb("caus01", [128, 128], F32)
        selb = sb("selb", [128, 8, 128], BF16)
        onesf = sb("onesf", [128, 64], F32)
        cosT = sb("cosT", [128, NT, 64], F32)
        sinT = sb("sinT", [128, NT, 64], F32)
        qdec = sb("qdec", [128, 4], F32)
        kdec = sb("kdec", [128, 4], F32)
        gdecn = sb("gdecn", [128, 2, 4], F32)
        epsT = sb("epsT", [128, 1], F32)
        oneT = sb("oneT", [128, 1], F32)
        gB = sb("gB", [128, D], F32)
        gnB = sb("gnB", [128, 512], F32)
        bfB = sb("bfB", [128, 8], F32)
        wlog = sb("wlog", [128, 8, 8], BF16)
        negc = sb("negc", [128, NT, 8], F32)
        crowT = sb("crowT", [128, 528], BF16)
        S32 = sb("S32", [128, 4, 128], F32)
        Sbf = sb("Sbf", [128, 4, 128], BF16)
        ss = [sb(f"ss{i}", [128, 1], F32) for i in range(2)]
        rstd = [sb(f"rstd{i}", [128, 1], F32) for i in range(2)]
        rstd2 = sb("rstd2", [128, NT], F32)
        ssg = sb("ssg", [128, 4], F32)
        rstdg = sb("rstdg", [128, 4], F32)
        tl5 = sb("tl5", [128, 5, 8], F32)
        te5 = sb("te5", [128, 5, 8], F32)
        tl = sb("tl", [128, 8], F32)
        te = sb("te", [128, 8], F32)
        spv = sb("spv", [128, 5, 8], F32)
        st4 = sb("st4", [128, 6, 4], F32)
        tmpA = sb("tmpA", [128, 512], F32)
        tmpB = sb("tmpB", [128, 512], F32)
        tmpC = sb("tmpC", [128, 512], F32)
        ARENA_EL = 38600
        A = sb("arena", [128, ARENA_EL], BF16)
        off = [0]

        def carve(n):
            o = off[0]
            off[0] += n
            assert off[0] <= ARENA_EL, off[0]
            return A[:, o:o + n]

        Kc = carve(4 * L).rearrange("p (c n) -> p c n", c=4)
        vc_off = off[0]
        Vc = carve(NT * 520).rearrange("p (t h e) -> p t h e", t=NT, h=8)
        xnT = carve(8 * 528).rearrange("p (c n) -> p c n", c=8)
        qT = carve(8 * 528).rearrange("p (c n) -> p c n", c=8)
        mixR = carve(4 * 528).rearrange("p (c n) -> p c n", c=4)
        mixF = carve(8 * 528).rearrange("p (c n) -> p c n", c=8)
        PTb = [carve(512) for _ in range(2)]
        xn = [carve(1024) for _ in range(2)]
        qp = carve(512)
        kp = carve(512)
        rvb = carve(512)
        rqkT = carve(8 * 128).rearrange("p (c n) -> p c n", c=8)
        mT = carve(512).rearrange("p (c n) -> p c n", c=4)
        ro = carve(512)
        ROB = [ro, ro]
        ROK = ["ro", "ro"]
        p1_end = off[0]
        off[0] = 0
        hnT = carve(8 * L).rearrange("p (c n) -> p c n", c=8)
        actT = [carve(4 * 512).rearrange("p (c n) -> p c n", c=4) for _ in range(2)]
        xn2 = [carve(1024) for _ in range(2)]
        p2_end = off[0]
        print("arena p1", p1_end, "p2", p2_end, "sbuf remaining", nc.sbuf_bytes_remaining)

        P = [ps(f"P{i}", [128, 512], F32) for i in (0, 1)]
        PTR = ps("PTR", [128, 8, 128], BF16)
        SC = [ps(f"SC{i}", [128, 512], F32) for i in range(2)]
        OTp = ps("OT", [128, 512], F32)
        P6 = ps("P6", [128, 512], F32)
        P7 = ps("P7", [128, 512], F32)
        YB = [OTp, P6, P7]

        cnt = {"u": 0, "sc": 0, "y": 0, "w": 0, "n": 0, "ro": 0}

        def ld(dst, src, key, q="sp"):
            S.dma(q, "c_" + key, dst, src, writes=[key])

        ld(identf[:], cd["c_identf"], "identf")
        ld(Umat[:], cd["c_U"], "Umat")
        ld(Esel[:], cd["c_E"], "Esel")
        ld(caus01[:], cd["c_caus01"], "caus01")
        ld(onesf[:], cd["c_onesf"], "onesf")
        ld(cosT[:], cd["c_cos"], "cosT")
        ld(sinT[:], cd["c_sin"], "sinT")
        ld(qdec[:], cd["c_qdec"], "qdec")
        ld(kdec[:], cd["c_kdec"], "kdec")
        ld(gdecn[:], cd["c_gdecn"], "gdecn")
        ld(identb[:], cd["c_identf"], "identb", q="pool")
        ld(maskneg[:], cd["c_maskneg"], "maskneg", q="pool")
        ld(selb[:], cd["c_sel"], "selb", q="pool")
        S.call("dve", "memset", crowT[:], 0.0, W=["crowT"])
        S.call("dve", "memset", A[:, :], 0.0,
               W=["Kc", "Vc", "Vones", "xnT", "qT", "mixR", "mixF", "PT0", "PT1", "xn0", "xn1", "qp", "kp", "rvb",
                  "rqkT", "mT", "ro", "hnT0", "hnT1", "hnT2", "hnT3", "hnT4", "actT0", "actT1", "xn2_0", "xn2_1"])
        S.call("dve", "memset", epsT[:], EPS,
                   W=["epsT"])
        S.call("dve", "memset", oneT[:], 1.0,
                   W=["oneT"])

        def wload(src_ap, ncol_total, view):
            i = cnt["w"] % NSLOT
            cnt["w"] += 1
            dst = view(ring[i])
            S.dma("pool", f"w{i}", dst, src_ap, writes=[f"ring{i}"])
            return i

        def wview8(i, ncols):
            return ring[i][:, 0:8 * ncols].rearrange("p (c n) -> p c n", c=8)

        def norm_stats(t, p, jbuf, jkey):
            n = NTOK[t]
            Ht = f"H{t}"
            S.call("dve", "memset", ss[p][0:n, :], 0.0,
                   W=[f"ss{p}"])
            S.call("act", "activation", out=jbuf[0:n, :], in_=H[0:n, t, :], func=AF.Square,
                                               accum_out=ss[p][0:n, :],
                   R=[Ht], W=[f"ss{p}", jkey])
            S.call("act", "activation", out=rstd[p][0:n, :], in_=ss[p][0:n, :], func=AF.Sqrt,
                                               scale=1.0 / D, bias=epsT[0:n, 0:1],
                   R=[f"ss{p}", "epsT"], W=[f"rstd{p}"])
            S.call("dve", "reciprocal", out=rstd[p][0:n, :], in_=rstd[p][0:n, :],
                   R=[f"rstd{p}"], W=[f"rstd{p}"])

        def norm_to_T(t, xnbuf, xnkey, dstT, col, dstkey, pre=False):
            n = NTOK[t]
            p = cnt["n"] % 2
            cnt["n"] += 1
            xb = xnbuf[p]
            if pre:
                rs_ap, rs_key = rstd2[0:n, t:t + 1], f"rs2_{t}"
            else:
                norm_stats(t, p, xb, f"{xnkey}{p}")
                rs_ap, rs_key = rstd[p][0:n, 0:1], f"rstd{p}"
            S.call("dve", "scalar_tensor_tensor", out=xb[0:n, :], in0=H[0:n, t, :],
                   scalar=rs_ap, in1=gB[0:n, :], op0=ALU.mult, op1=ALU.mult,
                   R=[f"H{t}", rs_key, "gB"], W=[f"{xnkey}{p}"])
            for c in range(8):
                S.call("pe", "transpose", out=PTR[:, c, 0:n], in_=xb[0:n, c * 128:(c + 1) * 128],
                                                      identity=identb[0:n, 0:n],
                   R=[f"{xnkey}{p}", "identb"], W=["PTR"])
            S.call("act", "copy", out=dstT[:, :, col:col + n], in_=PTR[:, :, 0:n],
                   R=["PTR"], W=[dstkey])

        def mm_tok(t, colT, srcT, srckey, slot, ncols, outp, outkey):
            n = NTOK[t]
            wv = wview8(slot, ncols)
            for k in range(8):
                S.call("pe", "matmul", outp[0:n, 0:ncols], lhsT=srcT[:, k, colT:colT + n],
                                                   rhs=wv[:, k, :], start=(k == 0), stop=(k == 7),
                   R=[srckey, f"ring{slot}"], W=[outkey])

        def phase1(s, li, first_layer_of_prog):
            S.dma("sp", "c_gB", gB[:], attn_norm_d[li:li + 1, :].broadcast_to([128, D]), writes=["gB"])
            S.dma("sp", "c_gnB", gnB[:], ret_gn_d[li:li + 1, :].broadcast_to([128, 512]), writes=["gnB"])
            S.dma("sp", "c_bfB", bfB[:], b_fgate_d[li:li + 1, :].broadcast_to([128, 8]), writes=["bfB"])
            S.dma("pool", "c_wlog", wlog[:],
                  w_in_d[li, :, 1536:1544].rearrange("(c p) n -> p c n", p=128), writes=["wlog"])
            S.call("dve", "memset", qT[:, :, :], 0.0, W=["qT"])
            S.call("dve", "memset", mixF[64:128, :, :], 0.0, W=["mixF"])
            S.call("dve", "memset", Vc[:, :, :, 64:65], 1.0,
                   W=["Vones"])

            for gi, tiles in enumerate(GROUPS):
                gpos0 = POS0[tiles[0]]
                gcols = {}
                cc = 0
                for t in tiles:
                    gcols[t] = cc
                    cc += NTOK[t]
                gw = cc
                segs = [(0, 16), (16, 528)] if gi == 0 else [(0, 512)]

                if gi == 0:
                    for t in tiles:
                        norm_to_T(t, xn, "xn", xnT, gcols[t], "xnT")

                def win_chunk(c0):
                    return wload(w_in_d[li, :, c0:c0 + 512].rearrange("(c p) n -> p c n", p=128), 512,
                                 lambda r: r[:, 0:4096].rearrange("p (c n) -> p c n", c=8))
                for j, t in enumerate(tiles):
                    n = NTOK[t]
                    col = gcols[t]
                    for k in range(8):
                        S.call("pe", "matmul", P6[0:n, 0:8], lhsT=xnT[:, k, col:col + n], rhs=wlog[:, k, :],
                               start=(k == 0), stop=(k == 7), R=["xnT", "wlog"], W=["P6"])
                    S.call("dve", "tensor_tensor", out=tl5[0:n, j, :], in0=P6[0:n, 0:8], in1=bfB[0:n, :], op=ALU.add,
                           R=["P6", "bfB"], W=[f"tl{j}"])
                for j, t in enumerate(tiles):
                    n = NTOK[t]
                    S.call("act", "activation", out=te5[0:n, j, :], in_=tl5[0:n, j, :], func=AF.Exp, scale=-1.0,
                           R=[f"tl{j}"], W=[f"te{j}"])
                for j, t in enumerate(tiles):
                    n = NTOK[t]
                    S.call("act", "activation", out=spv[0:n, j, :], in_=te5[0:n, j, :], func=AF.Ln,
                           bias=oneT[0:n, 0:1], R=[f"te{j}", "oneT"], W=[f"spv{j}"])
                for which, c0 in (("q", 0), ("k", 512)):
                    slot = win_chunk(c0)
                    wv = wview8(slot, 512)
                    for pair in range(4):
                        for (a, b) in segs:
                            u = cnt["u"] % 2
                            cnt["u"] += 1
                            w = b - a
                            for k in range(8):
                                S.call("pe", "matmul",
                                    P[u][:, 0:w], lhsT=wv[:, k, pair * 128:(pair + 1) * 128],
                                    rhs=xnT[:, k, a:b], start=(k == 0), stop=(k == 7),
                   R=["xnT", f"ring{slot}"], W=[f"P{u}"])
                            if which == "q":
                                S.call("dve", "tensor_scalar", out=qT[0:64, 2 * pair, a:b], in0=P[u][0:64, 0:w],
                                       scalar1=0.125, scalar2=None, op0=ALU.mult, R=[f"P{u}"], W=["qT"])
                                S.call("dve", "tensor_scalar", out=qT[64:128, 2 * pair + 1, a:b],
                                       in0=P[u][64:128, 0:w], scalar1=0.125, scalar2=None, op0=ALU.mult,
                                       R=[f"P{u}"], W=["qT"])
                            else:
                                S.call("dve", "tensor_copy", out=Kc[:, pair, gpos0 + a:gpos0 + b], in_=P[u][:, 0:w],
                                       R=[f"P{u}"], W=["Kc"])
                slot = win_chunk(1024)

                def emit_crow(t):
                    n = NTOK[t]
                    col = gcols[t]
                    S.call("pe", "matmul", P6[0:8, 16:16 + n], lhsT=negc[0:n, t, :], rhs=identf[0:n, 0:n],
                           start=True, stop=True, R=["negc", "identf"], W=["P6"])
                    S.call("act", "mul", out=crowT[0:8, col:col + n], in_=P6[0:8, 16:16 + n], mul=-1.0,
                           R=["P6"], W=["crowT"])

                for j, t in enumerate(tiles):
                    n = NTOK[t]
                    u = cnt["u"] % 2
                    cnt["u"] += 1
                    mm_tok(t, gcols[t], xnT, "xnT", slot, 512, P[u], f"P{u}")
                    S.call("dve", "tensor_copy", out=Vc[0:n, t, :, 0:64],
                           in_=P[u][0:n, :].rearrange("p (h e) -> p h e", h=8), R=[f"P{u}"], W=["Vc"])
                    S.call("pe", "matmul", P6[0:n, 8:16], lhsT=Umat[0:n, 0:n], rhs=spv[0:n, j, :],
                           start=True, stop=(t == 0), R=[f"spv{j}", "Umat"], W=["P6"])
                    if t > 0:
                        npv = NTOK[t - 1]
                        esel = 0 if t - 1 == 0 else 1
                        S.call("pe", "matmul", P6[0:n, 8:16], lhsT=Esel[0:npv, esel, 0:n], rhs=negc[0:npv, t - 1, :],
                               start=False, stop=True, R=["negc", "Esel"], W=["P6"])
                    S.call("dve", "tensor_copy", out=negc[0:n, t, :], in_=P6[0:n, 8:16], R=["P6"], W=["negc"])
                    if j > 0:
                        emit_crow(tiles[j - 1])
                emit_crow(tiles[-1])

                srq = win_chunk(1544)
                srk = win_chunk(2056)
                srv = win_chunk(2568)
                srg = win_chunk(3080)

                def ret_steps(t):
                    n = NTOK[t]
                    col = gcols[t]
                    rob = cnt["ro"] % 2
                    cnt["ro"] += 1
                    o4 = SC[1][0:n, :].rearrange("p (h e) -> p h e", h=4)

                    def rot(src, dec, dst, dstkey, srckey):
                        s4 = src[0:n, :].rearrange("p (h two e) -> p h two e", h=4, two=2)
                        x1 = s4[:, :, 0, :]
                        x2 = s4[:, :, 1, :]
                        cb = cosT[0:n, t, :].unsqueeze(1).broadcast_to([n, 4, 64])
                        sbb = sinT[0:n, t, :].unsqueeze(1).broadcast_to([n, 4, 64])
                        A4 = tmpA[0:n, :].rearrange("p (h two e) -> p h two e", h=4, two=2)
                        B4 = tmpB[0:n, :].rearrange("p (h two e) -> p h two e", h=4, two=2)
                        S.call("dve", "tensor_tensor", out=A4[:, :, 0, :], in0=x1, in1=cb, op=ALU.mult,
                               R=[srckey, "cosT"], W=["tmpA"])
                        S.call("dve", "tensor_tensor", out=A4[:, :, 1, :], in0=x1, in1=sbb, op=ALU.mult,
                               R=[srckey, "sinT"], W=["tmpA"])
                        S.call("dve", "tensor_tensor", out=B4[:, :, 0, :], in0=x2, in1=sbb, op=ALU.mult,
                               R=[srckey, "sinT"], W=["tmpB"])
                        S.call("dve", "tensor_tensor", out=B4[:, :, 1, :], in0=x2, in1=cb, op=ALU.mult,
                               R=[srckey, "cosT"], W=["tmpB"])
                        S.call("dve", "tensor_tensor", out=A4[:, :, 0, :], in0=A4[:, :, 0, :], in1=B4[:, :, 0, :],
                               op=ALU.subtract, R=["tmpA", "tmpB"], W=["tmpA"])
                        S.call("dve", "tensor_tensor", out=A4[:, :, 1, :], in0=A4[:, :, 1, :], in1=B4[:, :, 1, :],
                               op=ALU.add, R=["tmpA", "tmpB"], W=["tmpA"])
                        S.call("dve", "tensor_tensor",
                               out=dst[0:n, :].rearrange("p (h e) -> p h e", h=4),
                               in0=tmpA[0:n, :].rearrange("p (h e) -> p h e", h=4),
                               in1=dec[0:n, :].unsqueeze(2).broadcast_to([n, 4, 128]), op=ALU.mult,
                               R=["tmpA", "qdec", "kdec"], W=[dstkey])

                    def F0():
                        mm_tok(t, col, xnT, "xnT", srq, 512, P[0], "P0")
                        mm_tok(t, col, xnT, "xnT", srk, 512, P[1], "P1")
                        mm_tok(t, col, xnT, "xnT", srv, 512, SC[0], "SC0")
                        mm_tok(t, col, xnT, "xnT", srg, 512, OTp, "OT")

                    def F0b():
                        S.call("act", "copy", out=rvb[0:n, :], in_=SC[0][0:n, :], R=["SC0"], W=["rvb"])

                    def F1():
                        rot(P[0], qdec, qp, "qp", "P0")

                    def F3():
                        rot(P[1], kdec, kp, "kp", "P1")

                    def F4a():
                        for c in range(4):
                            S.call("pe", "transpose", out=PTR[:, c, 0:n], in_=qp[0:n, c * 128:(c + 1) * 128],
                                   identity=identb[0:n, 0:n], R=["qp", "identb"], W=["PTR"])
                        S.call("act", "copy", out=rqkT[:, 0:4, 0:n], in_=PTR[:, 0:4, 0:n], R=["PTR"], W=["rqkT"])

                    def F4():
                        for c in range(4):
                            S.call("pe", "transpose", out=PTR[:, 4 + c, 0:n], in_=kp[0:n, c * 128:(c + 1) * 128],
                                   identity=identb[0:n, 0:n], R=["kp", "identb"], W=["PTR"])
                        S.call("act", "copy", out=rqkT[:, 4:8, 0:n], in_=PTR[:, 4:8, 0:n], R=["PTR"], W=["rqkT"])

                    def F5():
                        for hh in range(4):
                            S.call("pe", "matmul", SC[0][0:n, hh * 128:hh * 128 + n],
                                   lhsT=rqkT[:, 4 + hh, 0:n], rhs=rqkT[:, hh, 0:n], start=True, stop=True,
                                   R=["rqkT"], W=["SC0"])
                        S.call("dve", "tensor_tensor", out=mT[0:n, :, 0:n],
                               in0=SC[0][0:n, :].rearrange("p (h e) -> p h e", h=4)[:, :, 0:n],
                               in1=caus01[0:n, 0:n].unsqueeze(1).broadcast_to([n, 4, n]), op=ALU.mult,
                               R=["SC0", "caus01"], W=["mT"])

                    def F6():
                        for hh in range(4):
                            S.call("pe", "matmul", SC[1][0:n, hh * 128:(hh + 1) * 128], lhsT=mT[0:n, hh, 0:n],
                                   rhs=rvb[0:n, hh * 128:(hh + 1) * 128], start=True, stop=(t == 0),
                                   R=["mT", "rvb"], W=["SC1"])
                            if t > 0:
                                S.call("pe", "matmul", SC[1][0:n, hh * 128:(hh + 1) * 128],
                                       lhsT=rqkT[:, hh, 0:n], rhs=Sbf[:, hh, :], start=False, stop=True,
                                       R=["rqkT", "Sbf"], W=["SC1"])
                        for hh in range(4):
                            S.call("pe", "matmul", P7[:, hh * 128:(hh + 1) * 128],
                                   lhsT=kp[0:n, hh * 128:(hh + 1) * 128], rhs=rvb[0:n, hh * 128:(hh + 1) * 128],
                                   start=True, stop=True, R=["kp", "rvb"], W=["P7"])

                    def F7():
                        S.call("act", "activation", out=tmpC[0:n, :], in_=OTp[0:n, :], func=AF.Silu,
                               R=["OT"], W=["tmpC"])
                        S.call("dve", "tensor_tensor", out=tmpC[0:n, :], in0=tmpC[0:n, :], in1=gnB[0:n, :],
                               op=ALU.mult, R=["tmpC", "gnB"], W=["tmpC"])

                    def B0():
                        S4v = S32[:].rearrange("p h e -> p (h e)")
                        if t == 0:
                            S.call("dve", "tensor_copy", out=S4v, in_=P7[:, :], R=["P7"], W=["S32"])
                        else:
                            S.call("dve", "tensor_tensor", out=S4v, in0=S4v, in1=P7[:, :], op=ALU.add,
                                   R=["P7", "S32"], W=["S32"])
                        gsel = 0 if t == 0 else 1
                        S.call("dve", "tensor_tensor", out=S32[:], in0=S32[:],
                               in1=gdecn[:, gsel, :].unsqueeze(2).broadcast_to([128, 4, 128]), op=ALU.mult,
                               R=["S32", "gdecn"], W=["S32"])
                        S.call("act", "copy", out=Sbf[:], in_=S32[:], R=["S32"], W=["Sbf"])

                    def B1():
                        S.call("dve", "tensor_reduce", out=st4[0:n, 0, :], in_=o4, axis=AX.X, op=ALU.add,
                               R=["SC1"], W=["st4"])
                        S.call("act", "activation", out=P6[0:n, :], in_=SC[1][0:n, :], func=AF.Square,
                               R=["SC1"], W=["P6"])

                    def B2():
                        S.call("dve", "tensor_reduce", out=st4[0:n, 1, :],
                               in_=P6[0:n, :].rearrange("p (h e) -> p h e", h=4), axis=AX.X, op=ALU.add,
                               R=["P6"], W=["st4"])
                        S.call("dve", "tensor_scalar", out=st4[0:n, 2, :], in0=st4[0:n, 0, :],
                               scalar1=1.0 / 128, scalar2=None, op0=ALU.mult, R=["st4"], W=["st4"])
                        S.call("dve", "tensor_tensor", out=st4[0:n, 3, :], in0=st4[0:n, 2, :], in1=st4[0:n, 2, :],
                               op=ALU.mult, R=["st4"], W=["st4"])
                        S.call("dve", "scalar_tensor_tensor", out=st4[0:n, 4, :], in0=st4[0:n, 1, :],
                               scalar=1.0 / 128, in1=st4[0:n, 3, :], op0=ALU.mult, op1=ALU.subtract,
                               R=["st4"], W=["st4"])
                        S.call("act", "activation", out=st4[0:n, 5, :], in_=st4[0:n, 4, :], func=AF.Sqrt,
                               bias=epsT[0:n, 0:1], scale=1.0, R=["st4", "epsT"], W=["st4"])

                    def B3():
                        S.call("dve", "reciprocal", out=st4[0:n, 5, :], in_=st4[0:n, 5, :], R=["st4"], W=["st4"])
                        S.call("dve", "scalar_tensor_tensor", out=st4[0:n, 3, :], in0=st4[0:n, 2, :], scalar=-1.0,
                               in1=st4[0:n, 5, :], op0=ALU.mult, op1=ALU.mult, R=["st4"], W=["st4"])
                        for hh in range(4):
                            S.call("act", "activation", out=P6[0:n, hh * 128:(hh + 1) * 128],
                                   in_=SC[1][0:n, hh * 128:(hh + 1) * 128], func=AF.Identity,
                                   scale=st4[0:n, 5, hh:hh + 1], bias=st4[0:n, 3, hh:hh + 1],
                                   R=["SC1", "st4"], W=["P6"])

                    def B4():
                        S.call("dve", "tensor_tensor", out=ROB[rob][0:n, :], in0=P6[0:n, :], in1=tmpC[0:n, :],
                               op=ALU.mult, R=["P6", "tmpC"], W=[ROK[rob]])

                    def B5():
                        for c in range(4):
                            S.call("pe", "transpose", out=PTR[:, c, 0:n], in_=ROB[rob][0:n, c * 128:(c + 1) * 128],
                                   identity=identb[0:n, 0:n], R=[ROK[rob], "identb"], W=["PTR"])
                        S.call("act", "copy", out=mixR[:, :, col:col + n], in_=PTR[:, 0:4, 0:n],
                               R=["PTR"], W=["mixR"])

                    return dict(F0=F0, F0b=F0b, F1=F1, F4a=F4a, F3=F3, F4=F4, F5=F5, F6=F6, F7=F7,
                                B0=B0, B1=B1, B2=B2, B3=B3, B4=B4, B5=B5)

                prev = None
                for t in tiles:
                    cur = ret_steps(t)
                    cur["F0"]()
                    if prev:
                        prev["B0"]()
                        prev["B1"]()
                    cur["F0b"]()
                    cur["F1"]()
                    cur["F4a"]()
                    if prev:
                        prev["B2"]()
                    cur["F3"]()
                    cur["F4"]()
                    if prev:
                        prev["B3"]()
                    cur["F5"]()
                    if prev:
                        prev["B4"]()
                    cur["F6"]()
                    cur["F7"]()
                    if prev:
                        prev["B5"]()
                    prev = cur
                prev["B0"]()
                tail_steps = [prev[k] for k in ("B1", "B2", "B3", "B4", "B5")]

                SCB = [SC[0], P[0], P[1]]
                SCK = ["SC0", "P0", "P1"]
                OTB = [OTp, P7]
                OTK = ["OT", "P7"]
                LOOK = 2
                n1_sched = {}
                att_ctr = [0]
                if gi + 1 < len(GROUPS):
                    nxt = GROUPS[gi + 1]
                    S.call("dve", "memset", ssg[:, :], 0.0, W=["ssg"])
                    for jj, t in enumerate(nxt):
                        S.call("act", "activation", out=xn[0][:, :], in_=H[:, t, :], func=AF.Square,
                               accum_out=ssg[:, jj:jj + 1], R=[f"H{t}"], W=["ssg", "xn0"])
                    S.call("act", "activation", out=rstdg[:, :], in_=ssg[:, :], func=AF.Sqrt,
                           scale=1.0 / D, bias=epsT[:, 0:1], R=["ssg", "epsT"], W=["rstdg"])
                    S.call("dve", "reciprocal", out=rstdg[:, :], in_=rstdg[:, :], R=["rstdg"], W=["rstdg"])

                    def n1_A(jj, t):
                        xi = jj % 2
                        S.call("dve", "scalar_tensor_tensor", out=xn[xi][:, :], in0=H[:, t, :],
                               scalar=rstdg[:, jj:jj + 1], in1=gB[:, :], op0=ALU.mult, op1=ALU.mult,
                               R=[f"H{t}", "rstdg", "gB"], W=[f"xn{xi}"])

                    def n1_B(jj, t):
                        xi = jj % 2
                        for c in range(8):
                            S.call("pe", "transpose", out=PTR[:, c, :], in_=xn[xi][:, c * 128:(c + 1) * 128],
                                   identity=identb[:, :], R=[f"xn{xi}", "identb"], W=["PTR"])
                        S.call("dve", "tensor_copy", out=xnT[:, :, 128 * jj:128 * jj + 128], in_=PTR[:, :, :],
                               R=["PTR"], W=["xnT"])

                    plan = [(0, "A", 0), (1, "A", 1), (5, "B", 0), (6, "A", 2), (10, "B", 1), (11, "A", 3),
                            (15, "B", 2), (20, "B", 3)]
                    for (at, kind, jj) in plan:
                        fn = n1_A if kind == "A" else n1_B
                        n1_sched.setdefault(at, []).append((fn, jj, nxt[jj]))

                def n1_tick():
                    for (fn, jj, tt) in n1_sched.pop(att_ctr[0], []):
                        fn(jj, tt)
                    att_ctr[0] += 1

                for (a, b) in (list(reversed(segs)) if gi == 0 else segs):
                    w = b - a
                    pa0 = gpos0 + a
                    ktiles = [t for t in range(NT) if POS0[t] < pa0 + w]
                    items = [(h, kt) for h in range(8) for kt in ktiles]
                    geo = {}
                    for kt in ktiles:
                        nk = NTOK[kt]
                        kp0 = POS0[kt]
                        if kp0 < pa0:
                            units = [(a, b, False)]
                            lo = a
                        else:
                            kc = a + (kp0 - pa0)
                            units = [(kc, kc + nk, True)]
                            if kc + nk < b:
                                units.append((kc + nk, b, False))
                            lo = kc
                        geo[kt] = (nk, kp0, units, lo)

                    def emit_scores(i):
                        h, kt = items[i]
                        nk, kp0, units, lo = geo[kt]
                        pr0 = 0 if h % 2 == 0 else 64
                        pair = h // 2
                        u = i % 3
                        for (ua, ub, diag) in units:
                            S.call("pe", "matmul",
                                   SCB[u][0:nk, ua - a:ub - a], lhsT=Kc[:, pair, kp0:kp0 + nk],
                                   rhs=qT[:, h, ua:ub], start=True, stop=False,
                                   R=["Kc", "qT"], W=[SCK[u]])
                            S.call("pe", "matmul",
                                   SCB[u][0:nk, ua - a:ub - a], lhsT=selb[:, h, 0:nk],
                                   rhs=crowT[:, ua:ub], start=False, stop=(not diag),
                                   R=["selb", "crowT"], W=[SCK[u]])
                            if diag:
                                S.call("pe", "matmul",
                                       SCB[u][0:nk, ua - a:ub - a], lhsT=identb[0:nk, 0:nk],
                                       rhs=maskneg[0:nk, 0:nk], start=False, stop=True,
                                       R=["identb", "maskneg"], W=[SCK[u]])

                    def emit_rest(i):
                        h, kt = items[i]
                        nk, kp0, units, lo = geo[kt]
                        u = i % 3
                        pu = i % 2
                        ob = h % 2
                        S.call("act", "activation",
                               out=PTb[pu][0:nk, lo - a:b - a], in_=SCB[u][0:nk, lo - a:b - a], func=AF.Exp,
                               bias=negc[0:nk, kt, h:h + 1], scale=1.0,
                               R=[SCK[u], "negc"], W=[f"PT{pu}"])
                        S.call("pe", "matmul",
                               OTB[ob][:, lo - a:b - a],
                               lhsT=A[0:nk, vc_off + (kt * 8 + h) * 65:vc_off + (kt * 8 + h) * 65 + 128],
                               rhs=PTb[pu][0:nk, lo - a:b - a],
                               start=(kt == ktiles[0]), stop=(kt == ktiles[-1]), skip_group_check=True,
                               R=[f"PT{pu}", "Vc", "Vones"], W=[OTK[ob]])

                    def emit_norm(h):
                        ob = h % 2
                        S.call("dve", "reciprocal", out=tmpC[64:65, 0:w], in_=OTB[ob][64:65, 0:w],
                               R=[OTK[ob]], W=["tmpC"])
                        S.call("pe", "matmul", P6[0:64, 0:w], lhsT=onesf[64:65, 0:64], rhs=tmpC[64:65, 0:w],
                               start=True, stop=True, R=["tmpC", "onesf"], W=["P6"])
                        S.call("dve", "tensor_copy", out=tmpA[0:64, 0:w], in_=OTB[ob][0:64, 0:w],
                               R=[OTK[ob]], W=["tmpA"])
                        S.call("dve", "tensor_tensor",
                               out=mixF[0:64, h, a:b], in0=tmpA[0:64, 0:w], in1=P6[0:64, 0:w], op=ALU.mult,
                               R=["tmpA", "P6"], W=["mixF"])

                    pending = []
                    nit = len(items)
                    for i in range(nit + LOOK):
                        if i < nit:
                            emit_scores(i)
                        j = i - LOOK
                        if j >= 0:
                            emit_rest(j)
                            if items[j][1] == ktiles[-1]:
                                pending.append((j + min(3, len(ktiles)), items[j][0]))
                        while pending and pending[0][0] <= j:
                            emit_norm(pending.pop(0)[1])
                        if tail_steps and i % 2 == 1:
                            tail_steps.pop(0)()
                        n1_tick()
                    for _, hh in pending:
                        emit_norm(hh)
                while tail_steps:
                    tail_steps.pop(0)()
                for at in sorted(n1_sched):
                    for (fn, jj, tt) in n1_sched[at]:
                        fn(jj, tt)
                n1_sched.clear()

                sfa = wload(w_out_d[li, 0:256, :].rearrange("(h p) n -> p h n", p=64), 0,
                            lambda r: r[0:64, 0:4096].rearrange("p (c n) -> p c n", c=4))
                sfb = wload(w_out_d[li, 256:512, :].rearrange("(h p) n -> p h n", p=64), 0,
                            lambda r: r[0:64, 0:4096].rearrange("p (c n) -> p c n", c=4))
                srt = wload(w_out_d[li, 512:1024, :].rearrange("(c p) n -> p c n", p=128), 0,
                            lambda r: r[:, 0:4096].rearrange("p (c n) -> p c n", c=4))
                wfa = ring[sfa][:, 0:4096].rearrange("p (c n) -> p c n", c=4)
                wfb = ring[sfb][:, 0:4096].rearrange("p (c n) -> p c n", c=4)
                wrt = ring[srt][:, 0:4096].rearrange("p (c n) -> p c n", c=4)
                for t in tiles:
                    n = NTOK[t]
                    col = gcols[t]
                    for half in range(2):
                        yb = cnt["y"] % 3
                        cnt["y"] += 1
                        ybk = ["OT", "P6", "P7"][yb]
                        hs = slice(half * 512, (half + 1) * 512)
                        for h in range(8):
                            wsrc = wfa if h < 4 else wfb
                            sk = sfa if h < 4 else sfb
                            S.call("pe", "matmul",
                                YB[yb][0:n, :], lhsT=mixF[:, h, col:col + n], rhs=wsrc[:, h % 4, hs],
                                start=(h == 0), stop=False,
                   R=["mixF", f"ring{sk}"], W=[ybk])
                        for c in range(4):
                            S.call("pe", "matmul",
                                YB[yb][0:n, :], lhsT=mixR[:, c, col:col + n], rhs=wrt[:, c, hs],
                                start=False, stop=(c == 3),
                   R=["mixR", f"ring{srt}"], W=[ybk])
                        S.call("dve", "tensor_tensor",
                            out=H[0:n, t, hs], in0=H[0:n, t, hs], in1=YB[yb][0:n, :], op=ALU.add,
                   R=[ybk, f"H{t}"], W=[f"H{t}"])
                    S.call("dve", "memset", rstd2[0:n, t:t + 1], 0.0, W=[f"rs2_{t}"])
                    S.call("act", "activation", out=xn[1][0:n, :], in_=H[0:n, t, :], func=AF.Square,
                           accum_out=rstd2[0:n, t:t + 1], R=[f"H{t}"], W=[f"rs2_{t}", "xn1"])
                    S.call("act", "activation", out=rstd2[0:n, t:t + 1], in_=rstd2[0:n, t:t + 1], func=AF.Sqrt,
                           scale=1.0 / D, bias=epsT[0:n, 0:1], R=[f"rs2_{t}", "epsT"], W=[f"rs2_{t}"])
                    S.call("dve", "reciprocal", out=rstd2[0:n, t:t + 1], in_=rstd2[0:n, t:t + 1],
                           R=[f"rs2_{t}"], W=[f"rs2_{t}"])

        def phase2(s, li):
            S.dma("sp", "c_gB", gB[:], ffn_norm_d[li:li + 1, :].broadcast_to([128, D]), writes=["gB"])
            segs = [(0, 16), (16, 528), (528, 1040), (1040, 1552), (1552, 2064)]
            seg_tiles = [[0], [1, 2, 3, 4], [5, 6, 7, 8], [9, 10, 11, 12], [13, 14, 15, 16]]

            def norm_seg(si):
                for t in seg_tiles[si]:
                    norm_to_T(t, xn2, "xn2_", hnT, POS0[t], f"hnT{si}", pre=True)

            norm_seg(0)
            norm_seg(1)
            first_fg = True
            f0 = 0
            while f0 < DFF:
                fw = min(512, DFF - f0)
                nfc = fw // 128
                sg = wload(w_gate_d[li, :, f0:f0 + fw].rearrange("(c p) n -> p c n", p=128), fw,
                           lambda r, fw=fw: r[:, 0:8 * fw].rearrange("p (c n) -> p c n", c=8))
                su = wload(w_up_d[li, :, f0:f0 + fw].rearrange("(c p) n -> p c n", p=128), fw,
                           lambda r, fw=fw: r[:, 0:8 * fw].rearrange("p (c n) -> p c n", c=8))
                sd = wload(w_down_d[li, f0:f0 + fw, :].rearrange("(c p) n -> p c n", p=128), fw,
                           lambda r, nfc=nfc: r[:, 0:nfc * 1024].rearrange("p (c n) -> p c n", c=nfc))
                wg = wview8(sg, fw)
                wu = wview8(su, fw)
                wd = ring[sd][:, 0:nfc * 1024].rearrange("p (c n) -> p c n", c=nfc)
                down_pending = None
                for si, (a, b) in enumerate(segs):
                    w = b - a
                    ab = cnt["u"] % 2
                    cnt["u"] += 1
                    if first_fg and si + 2 < len(segs):
                        norm_seg(si + 2)
                    for fc in range(nfc):
                        for k in range(8):
                            S.call("pe", "matmul",
                                P[0][:, 0:w], lhsT=wg[:, k, fc * 128:(fc + 1) * 128], rhs=hnT[:, k, a:b],
                                start=(k == 0), stop=(k == 7),
                   R=[f"hnT{si}", f"ring{sg}"], W=["P0"])
                        for k in range(8):
                            S.call("pe", "matmul",
                                P[1][:, 0:w], lhsT=wu[:, k, fc * 128:(fc + 1) * 128], rhs=hnT[:, k, a:b],
                                start=(k == 0), stop=(k == 7),
                   R=[f"hnT{si}", f"ring{su}"], W=["P1"])
                        S.call("act", "activation", out=tmpA[:, 0:w], in_=P[0][:, 0:w], func=AF.Silu,
                   R=["P0"], W=["tmpA"])
                        S.call("dve", "tensor_tensor",
                            out=actT[ab][:, fc, 0:w], in0=tmpA[:, 0:w], in1=P[1][:, 0:w], op=ALU.mult,
                   R=["tmpA", "P1"], W=[f"actT{ab}"])
                    def emit_down(si=si, a=a, ab=ab):
                        for t in seg_tiles[si]:
                            n = NTOK[t]
                            tc0 = POS0[t] - a
                            for half in range(2):
                                yb = cnt["y"] % 3
                                cnt["y"] += 1
                                ybk = ["OT", "P6", "P7"][yb]
                                hs = slice(half * 512, (half + 1) * 512)
                                for fc in range(nfc):
                                    S.call("pe", "matmul",
                                           YB[yb][0:n, :], lhsT=actT[ab][:, fc, tc0:tc0 + n], rhs=wd[:, fc, hs],
                                           start=(fc == 0), stop=(fc == nfc - 1),
                                           R=[f"actT{ab}", f"ring{sd}"], W=[ybk])
                                S.call("dve", "tensor_tensor",
                                       out=H[0:n, t, hs], in0=H[0:n, t, hs], in1=YB[yb][0:n, :], op=ALU.add,
                                       R=[ybk, f"H{t}"], W=[f"H{t}"])

                    if down_pending is not None:
                        down_pending()
                    down_pending = emit_down
                if down_pending is not None:
                    down_pending()
                    down_pending = None
                f0 += fw
                first_fg = False

        for s in range(2):
            if first:
                S.dma("sp", "ldx", H[:, 1:NT, :], x_d[s].rearrange("(t p) d -> p t d", p=128),
                      writes=[f"H{t}" for t in range(1, NT)])
                S.dma("sp", "ldx", H[0:16, 0, :], meta_d, writes=["H0"])
            else:
                S.dma("sp", "ldx", H[:, 1:NT, :], hin_d[s, 16:L, :].rearrange("(t p) d -> p t d", p=128),
                      writes=[f"H{t}" for t in range(1, NT)])
                S.dma("sp", "ldx", H[0:16, 0, :], hin_d[s, 0:16, :], writes=["H0"])
            for li in layer_ids:
                phase1(s, li, False)
                S.barrier()
                phase2(s, li)
                S.barrier()
            if last:
                S.dma("sp", "c_gB", gB[:], final_norm_d[0:1, :].broadcast_to([128, D]), writes=["gB"])
                for t in range(1, NT):
                    p = cnt["n"] % 2
                    cnt["n"] += 1
                    norm_stats(t, p, xn[p], f"xn{p}")
                    ob = [tmpA, tmpB][p]
                    obk = ["tmpA", "tmpB"][p]
                    for half in range(2):
                        hs = slice(half * 512, (half + 1) * 512)
                        S.call("dve", "scalar_tensor_tensor",
                            out=ob[:, :], in0=H[:, t, hs], scalar=rstd[p][:, 0:1], in1=gB[:, hs],
                            op0=ALU.mult, op1=ALU.mult,
                   R=[f"H{t}", f"rstd{p}", "gB"], W=[obk])
                        S.dma("sp", "st", out_d[s, (t - 1) * 128:t * 128, hs], ob[:, :], reads=[obk])
            else:
                S.dma("sp", "st", out_d[s, 16:L, :].rearrange("(t p) d -> p t d", p=128), H[:, 1:NT, :],
                      reads=[f"H{t}" for t in range(1, NT)])
                S.dma("sp", "st", out_d[s, 0:16, :], H[0:16, 0, :], reads=["H0"])
            S.barrier()
        S.final_wait("sp", ["st"])
        n_ops = {e: len(S.ops[e]) for e in S.ENG}
        print("ops per engine", n_ops)
        S.emit(st)
    return nc


_CACHE = {}


def _get_prog(key):
    if key not in _CACHE:
        _CACHE[key] = build(*key)
    return _CACHE[key]


def kernel(x, meta_tokens, attn_norm, w_in, b_fgate, ret_gn, w_out, ffn_norm, w_gate, w_up, w_down, final_norm):
    f = lambda a: np.ascontiguousarray(np.asarray(a, dtype=np.float32))
    consts = make_consts()
    shared = {
        "attn_norm": f(attn_norm), "w_in": f(w_in), "b_fgate": f(b_fgate), "ret_gn": f(ret_gn),
        "w_out": f(w_out), "ffn_norm": f(ffn_norm), "w_gate": f(w_gate), "w_up": f(w_up),
        "w_down": f(w_down), "final_norm": f(final_norm).reshape(1, D),
    }
    shared.update(consts)
    x = f(x)
    meta = f(meta_tokens)
    cores = list(range(NCORES))
    if MODE == "fused":
        nc = _get_prog(((0, 1), True, True))
        in_maps = [dict(shared, x=x[2 * c:2 * c + 2], meta=meta) for c in cores]
        res = run_bass_kernel_spmd(nc, in_maps, core_ids=cores)
        return np.concatenate([r["out"] for r in res.results], axis=0)
    else:
        nc0 = _get_prog(((0,), True, False))
        in_maps = [dict(shared, x=x[2 * c:2 * c + 2], meta=meta) for c in cores]
        res0 = run_bass_kernel_spmd(nc0, in_maps, core_ids=cores)
        nc1 = _get_prog(((1,), False, True))
        in_maps = [dict(shared, hin=res0.results[c]["hout"]) for c in cores]
        res1 = run_bass_kernel_spmd(nc1, in_maps, core_ids=cores)
        return np.concatenate([r["out"] for r in res1.results], axis=0)
```

```python
import math
from contextlib import ExitStack

import numpy as np
import concourse.bass as bass
import concourse.mybir as mybir
from concourse.bass_utils import run_bass_kernel_spmd

F32 = mybir.dt.float32
BF16 = mybir.dt.bfloat16
AF = mybir.ActivationFunctionType
ALU = mybir.AluOpType
AX = mybir.AxisListType

NCORES = 8
D = 1024
SEQ = 2048
NMETA = 16
L = SEQ + NMETA
NT = 17
DFF = 2816
INC = 3592
EPS = 1e-6
NTOK = [16] + [128] * 16
POS0 = [0] + [16 + 128 * i for i in range(16)]
GROUPS = [[0, 1, 2, 3, 4], [5, 6, 7, 8], [9, 10, 11, 12], [13, 14, 15, 16]]
GAMMA = [1.0 - 2.0 ** (-5 - h) for h in range(4)]
NSLOT = 4
SLOT_EL = 4096

MODE = "fused"


class Sched:
    ENG = ("pe", "act", "dve", "pool", "sp")

    def __init__(self, nc):
        self.nc = nc
        self.ops = {e: [] for e in self.ENG}
        self.state = {}
        self.waited = {e: {} for e in self.ENG}
        self.dma_cnt = {}

    def _filter(self, eng, deps):
        out = []
        wd = self.waited[eng]
        for t in deps:
            if t[0] == "eng":
                _, e, idx = t
                if e == eng and e == "pe":
                    continue
                if wd.get(e, -1) >= idx:
                    continue
                wd[e] = idx
                self.ops[e][idx]["signal"] = True
                out.append(t)
            else:
                _, name, val = t
                if wd.get(name, -1) >= val:
                    continue
                wd[name] = val
                out.append(t)
        return out

    def _deps(self, eng, reads, writes):
        deps = []
        for k in reads:
            st = self.state.get(k)
            if st and st["w"] is not None:
                deps.append(st["w"])
        for k in writes:
            st = self.state.get(k)
            if st:
                if st["w"] is not None:
                    deps.append(st["w"])
                deps.extend(st["r"])
        return self._filter(eng, deps)

    def _commit(self, tok, reads, writes):
        for k in reads:
            st = self.state.setdefault(k, {"w": None, "r": []})
            if tok[0] == "eng":
                st["r"] = [t for t in st["r"] if not (t[0] == "eng" and t[1] == tok[1])]
            st["r"].append(tok)
        for k in writes:
            self.state[k] = {"w": tok, "r": []}

    def op(self, eng, fn, reads=(), writes=()):
        waits = self._deps(eng, reads, writes)
        idx = len(self.ops[eng])
        self.ops[eng].append({"fn": fn, "waits": waits, "signal": False, "dma": None})
        self._commit(("eng", eng, idx), reads, writes)

    def call(self, eng, name, *args, R=(), W=(), **kwargs):
        self.op(eng, (lambda e: getattr(e, name)(*args, **kwargs)), reads=R, writes=W)

    def dma(self, q, sem, out_ap, in_ap, reads=(), writes=()):
        waits = self._deps(q, reads, writes)
        self.dma_cnt[sem] = self.dma_cnt.get(sem, 0) + 16
        val = self.dma_cnt[sem]
        self.ops[q].append({"fn": (lambda e: e.dma_start(out=out_ap, in_=in_ap)),
                            "waits": waits, "signal": False, "dma": sem})
        self._commit(("sem", sem, val), reads, writes)

    def barrier(self):
        toks = []
        for e in self.ENG:
            for idx in range(len(self.ops[e]) - 1, -1, -1):
                o = self.ops[e][idx]
                if o["dma"] is None and o["fn"] is not None:
                    toks.append(("eng", e, idx))
                    break
        for s, v in self.dma_cnt.items():
            toks.append(("sem", s, v))
        for e in self.ENG:
            waits = self._filter(e, [t for t in toks if not (t[0] == "eng" and t[1] == e)])
            self.ops[e].append({"fn": None, "waits": waits, "signal": False, "dma": None})

    def final_wait(self, eng, sems):
        waits = [("sem", s, self.dma_cnt[s]) for s in sems if s in self.dma_cnt]
        self.ops[eng].append({"fn": None, "waits": waits, "signal": False, "dma": None})

    def emit(self, stack):
        nc = self.nc
        esem = {e: stack.enter_context(nc.semaphore("s_" + e)) for e in self.ENG}
        dsem = {s: stack.enter_context(nc.semaphore("d_" + s)) for s in self.dma_cnt}
        block = stack.enter_context(nc.Block())
        sigval = {}
        for e in self.ENG:
            c = 0
            for i, o in enumerate(self.ops[e]):
                if o["signal"]:
                    c += 1
                    sigval[(e, i)] = c

        def run(e, engobj):
            for i, o in enumerate(self.ops[e]):
                for t in o["waits"]:
                    if t[0] == "eng":
                        engobj.wait_ge(esem[t[1]], sigval[(t[1], t[2])])
                    else:
                        engobj.wait_ge(dsem[t[1]], t[2])
                if o["fn"] is None:
                    continue
                ins = o["fn"](engobj)
                if o["dma"] is not None:
                    ins.then_inc(dsem[o["dma"]], 16)
                elif o["signal"]:
                    ins.then_inc(esem[e], 1)

        @block.tensor
        def _(pe):
            run("pe", pe)

        @block.scalar
        def _(act):
            run("act", act)

        @block.vector
        def _(dve):
            run("dve", dve)

        @block.gpsimd
        def _(pool):
            run("pool", pool)

        @block.sync
        def _(sp):
            run("sp", sp)


def make_consts():
    c = {}
    c["c_identf"] = np.eye(128, dtype=np.float32)
    s = np.arange(128)[:, None]
    t = np.arange(128)[None, :]
    c["c_U"] = (s <= t).astype(np.float32)
    E = np.zeros((128, 2, 128), np.float32)
    E[15, 0, :] = 1.0
    E[127, 1, :] = 1.0
    c["c_E"] = E
    c["c_maskneg"] = np.where(s <= t, 0.0, -30000.0).astype(np.float32)
    c["c_caus01"] = (s <= t).astype(np.float32)
    sel = np.zeros((128, 8, 128), np.float32)
    for h in range(8):
        sel[h, h, :] = 1.0
    c["c_sel"] = sel
    c["c_onesf"] = np.ones((128, 64), np.float32)
    inv_freq = (10000.0 ** (-np.arange(0, 128, 2, dtype=np.float32) / 128.0)).astype(np.float32)
    pos = np.zeros((128, NT), np.float32)
    for ti in range(NT):
        pos[:, ti] = POS0[ti] + np.arange(128)
    ang = (pos[:, :, None].astype(np.float32) * inv_freq[None, None, :]).astype(np.float32)
    c["c_cos"] = np.cos(ang).astype(np.float32)
    c["c_sin"] = np.sin(ang).astype(np.float32)
    j = np.arange(128, dtype=np.float64)
    lg = np.array([math.log(g) for g in GAMMA])
    c["c_qdec"] = np.exp((j[:, None] + 1.0) * lg[None, :]).astype(np.float32)
    c["c_kdec"] = (np.exp(-(j[:, None] + 1.0) * lg[None, :]) * (128.0 ** -0.5)).astype(np.float32)
    gd = np.zeros((128, 2, 4), np.float32)
    gd[:, 0, :] = np.exp(16.0 * lg)[None, :]
    gd[:, 1, :] = np.exp(128.0 * lg)[None, :]
    c["c_gdecn"] = gd
    return c


CONST_SHAPES = {
    "c_identf": [128, 128], "c_U": [128, 128], "c_E": [128, 2, 128], "c_maskneg": [128, 128],
    "c_caus01": [128, 128], "c_sel": [128, 8, 128], "c_onesf": [128, 64],
    "c_cos": [128, NT, 64], "c_sin": [128, NT, 64], "c_qdec": [128, 4], "c_kdec": [128, 4],
    "c_gdecn": [128, 2, 4],
}


def build(layer_ids, first, last):
    nc = bass.Bass("TRN2", target_bir_lowering=False)
    NL = 2

    def din(name, shape):
        return nc.dram_tensor(name, shape, F32, kind="ExternalInput").ap()

    if first:
        x_d = din("x", [2, SEQ, D])
        meta_d = din("meta", [NMETA, D])
    else:
        hin_d = din("hin", [2, L, D])
    attn_norm_d = din("attn_norm", [NL, D])
    w_in_d = din("w_in", [NL, D, INC])
    b_fgate_d = din("b_fgate", [NL, 8])
    ret_gn_d = din("ret_gn", [NL, 512])
    w_out_d = din("w_out", [NL, D, D])
    ffn_norm_d = din("ffn_norm", [NL, D])
    w_gate_d = din("w_gate", [NL, D, DFF])
    w_up_d = din("w_up", [NL, D, DFF])
    w_down_d = din("w_down", [NL, DFF, D])
    final_norm_d = din("final_norm", [1, D])
    cd = {k: din(k, v) for k, v in CONST_SHAPES.items()}
    if last:
        out_d = nc.dram_tensor("out", [2, SEQ, D], F32, kind="ExternalOutput").ap()
    else:
        out_d = nc.dram_tensor("hout", [2, L, D], F32, kind="ExternalOutput").ap()

    with ExitStack() as st:
        def sb(name, shape, dt):
            return st.enter_context(nc.sbuf_tensor(name, shape, dt))

        def ps(name, shape, dt):
            return st.enter_context(nc.psum_tensor(name, shape, dt))

        S = Sched(nc)

        H = sb("H", [128, NT, D], F32)
        ring = [sb(f"ring{i}", [128, SLOT_EL], BF16) for i in range(NSLOT)]
        identf = sb("identf", [128, 128], F32)
        identb = sb("identb", [128, 128], BF16)
        Umat = sb("Umat", [128, 128], F32)
        Esel = sb("Esel", [128, 2, 128], F32)
        maskneg = sb("maskneg", [128, 128], BF16)
        caus01 = sb("caus01", [128, 128], F32)
        selb = sb("selb", [128, 8, 128], BF16)
        onesf = sb("onesf", [128, 64], F32)
        cosT = sb("cosT", [128, NT, 64], F32)
        sinT = sb("sinT", [128, NT, 64], F32)
        qdec = sb("qdec", [128, 4], F32)
        kdec = sb("kdec", [128, 4], F32)
        gdecn = sb("gdecn", [128, 2, 4], F32)
        epsT = sb("epsT", [128, 1], F32)
        oneT = sb("oneT", [128, 1], F32)
        gB = sb("gB", [128, D], F32)
        gnB = sb("gnB", [128, 512], F32)
        bfB = sb("bfB", [128, 8], F32)
        wlog = sb("wlog", [128, 8, 8], BF16)
        negc = sb("negc", [128, NT, 8], F32)
        crowT = sb("crowT", [128, 528], BF16)
        S32 = sb("S32", [128, 4, 128], F32)
        Sbf = sb("Sbf", [128, 4, 128], BF16)
        ss = [sb(f"ss{i}", [128, 1], F32) for i in range(2)]
        rstd = [sb(f"rstd{i}", [128, 1], F32) for i in range(2)]
        rstd2 = sb("rstd2", [128, NT], F32)
        ssg = sb("ssg", [128, 4], F32)
        rstdg = sb("rstdg", [128, 4], F32)
        tl5 = sb("tl5", [128, 5, 8], F32)
        te5 = sb("te5", [128, 5, 8], F32)
        tl = sb("tl", [128, 8], F32)
        te = sb("te", [128, 8], F32)
        spv = sb("spv", [128, 5, 8], F32)
        st4 = sb("st4", [128, 6, 4], F32)
        tmpA = sb("tmpA", [128, 512], F32)
        tmpB = sb("tmpB", [128, 512], F32)
        tmpC = sb("tmpC", [128, 512], F32)
        ARENA_EL = 38600
        A = sb("arena", [128, ARENA_EL], BF16)
        off = [0]

        def carve(n):
            o = off[0]
            off[0] += n
            assert off[0] <= ARENA_EL, off[0]
            return A[:, o:o + n]

        Kc = carve(4 * L).rearrange("p (c n) -> p c n", c=4)
        vc_off = off[0]
        Vc = carve(NT * 520).rearrange("p (t h e) -> p t h e", t=NT, h=8)
        xnT = carve(8 * 528).rearrange("p (c n) -> p c n", c=8)
        qT = carve(8 * 528).rearrange("p (c n) -> p c n", c=8)
        mixR = carve(4 * 528).rearrange("p (c n) -> p c n", c=4)
        mixF = carve(8 * 528).rearrange("p (c n) -> p c n", c=8)
        PTb = [carve(512) for _ in range(2)]
        xn = [carve(1024) for _ in range(2)]
        qp = carve(512)
        kp = carve(512)
        rvb = carve(512)
        rqkT = carve(8 * 128).rearrange("p (c n) -> p c n", c=8)
        mT = carve(512).rearrange("p (c n) -> p c n", c=4)
        ro = carve(512)
        ROB = [ro, ro]
        ROK = ["ro", "ro"]
        p1_end = off[0]
        off[0] = 0
        hnT = carve(8 * L).rearrange("p (c n) -> p c n", c=8)
        actT = [carve(4 * 512).rearrange("p (c n) -> p c n", c=4) for _ in range(2)]
        xn2 = [carve(1024) for _ in range(2)]
        p2_end = off[0]
        print("arena p1", p1_end, "p2", p2_end, "sbuf remaining", nc.sbuf_bytes_remaining)

        P = [ps(f"P{i}", [128, 512], F32) for i in (0, 1)]
        PTR = ps("PTR", [128, 8, 128], BF16)
        SC = [ps(f"SC{i}", [128, 512], F32) for i in range(2)]
        OTp = ps("OT", [128, 512], F32)
        P6 = ps("P6", [128, 512], F32)
        P7 = ps("P7", [128, 512], F32)
        YB = [OTp, P6, P7]

        cnt = {"u": 0, "sc": 0, "y": 0, "w": 0, "n": 0, "ro": 0}

        def ld(dst, src, key, q="sp"):
            S.dma(q, "c_" + key, dst, src, writes=[key])

        ld(identf[:], cd["c_identf"], "identf")
        ld(Umat[:], cd["c_U"], "Umat")
        ld(Esel[:], cd["c_E"], "Esel")
        ld(caus01[:], cd["c_caus01"], "caus01")
        ld(onesf[:], cd["c_onesf"], "onesf")
        ld(cosT[:], cd["c_cos"], "cosT")
        ld(sinT[:], cd["c_sin"], "sinT")
        ld(qdec[:], cd["c_qdec"], "qdec")
        ld(kdec[:], cd["c_kdec"], "kdec")
        ld(gdecn[:], cd["c_gdecn"], "gdecn")
        ld(identb[:], cd["c_identf"], "identb", q="pool")
        ld(maskneg[:], cd["c_maskneg"], "maskneg", q="pool")
        ld(selb[:], cd["c_sel"], "selb", q="pool")
        S.call("dve", "memset", crowT[:], 0.0, W=["crowT"])
        S.call("dve", "memset", A[:, :], 0.0,
               W=["Kc", "Vc", "Vones", "xnT", "qT", "mixR", "mixF", "PT0", "PT1", "xn0", "xn1", "qp", "kp", "rvb",
                  "rqkT", "mT", "ro", "hnT0", "hnT1", "hnT2", "hnT3", "hnT4", "actT0", "actT1", "xn2_0", "xn2_1"])
        S.call("dve", "memset", epsT[:], EPS,
                   W=["epsT"])
        S.call("dve", "memset", oneT[:], 1.0,
                   W=["oneT"])

        def wload(src_ap, ncol_total, view):
            i = cnt["w"] % NSLOT
            cnt["w"] += 1
            dst = view(ring[i])
            S.dma("pool", f"w{i}", dst, src_ap, writes=[f"ring{i}"])
            return i

        def wview8(i, ncols):
            return ring[i][:, 0:8 * ncols].rearrange("p (c n) -> p c n", c=8)

        def norm_stats(t, p, jbuf, jkey):
            n = NTOK[t]
            Ht = f"H{t}"
            S.call("dve", "memset", ss[p][0:n, :], 0.0,
                   W=[f"ss{p}"])
            S.call("act", "activation", out=jbuf[0:n, :], in_=H[0:n, t, :], func=AF.Square,
                                               accum_out=ss[p][0:n, :],
                   R=[Ht], W=[f"ss{p}", jkey])
            S.call("act", "activation", out=rstd[p][0:n, :], in_=ss[p][0:n, :], func=AF.Sqrt,
                                               scale=1.0 / D, bias=epsT[0:n, 0:1],
                   R=[f"ss{p}", "epsT"], W=[f"rstd{p}"])
            S.call("dve", "reciprocal", out=rstd[p][0:n, :], in_=rstd[p][0:n, :],
                   R=[f"rstd{p}"], W=[f"rstd{p}"])

        def norm_to_T(t, xnbuf, xnkey, dstT, col, dstkey, pre=False):
            n = NTOK[t]
            p = cnt["n"] % 2
            cnt["n"] += 1
            xb = xnbuf[p]
            if pre:
                rs_ap, rs_key = rstd2[0:n, t:t + 1], f"rs2_{t}"
            else:
                norm_stats(t, p, xb, f"{xnkey}{p}")
                rs_ap, rs_key = rstd[p][0:n, 0:1], f"rstd{p}"
            S.call("dve", "scalar_tensor_tensor", out=xb[0:n, :], in0=H[0:n, t, :],
                   scalar=rs_ap, in1=gB[0:n, :], op0=ALU.mult, op1=ALU.mult,
                   R=[f"H{t}", rs_key, "gB"], W=[f"{xnkey}{p}"])
            for c in range(8):
                S.call("pe", "transpose", out=PTR[:, c, 0:n], in_=xb[0:n, c * 128:(c + 1) * 128],
                                                      identity=identb[0:n, 0:n],
                   R=[f"{xnkey}{p}", "identb"], W=["PTR"])
            S.call("act", "copy", out=dstT[:, :, col:col + n], in_=PTR[:, :, 0:n],
                   R=["PTR"], W=[dstkey])

        def mm_tok(t, colT, srcT, srckey, slot, ncols, outp, outkey):
            n = NTOK[t]
            wv = wview8(slot, ncols)
            for k in range(8):
                S.call("pe", "matmul", outp[0:n, 0:ncols], lhsT=srcT[:, k, colT:colT + n],
                                                   rhs=wv[:, k, :], start=(k == 0), stop=(k == 7),
                   R=[srckey, f"ring{slot}"], W=[outkey])

        def phase1(s, li, first_layer_of_prog):
            S.dma("sp", "c_gB", gB[:], attn_norm_d[li:li + 1, :].broadcast_to([128, D]), writes=["gB"])
            S.dma("sp", "c_gnB", gnB[:], ret_gn_d[li:li + 1, :].broadcast_to([128, 512]), writes=["gnB"])
            S.dma("sp", "c_bfB", bfB[:], b_fgate_d[li:li + 1, :].broadcast_to([128, 8]), writes=["bfB"])
            S.dma("pool", "c_wlog", wlog[:],
                  w_in_d[li, :, 1536:1544].rearrange("(c p) n -> p c n", p=128), writes=["wlog"])
            S.call("dve", "memset", qT[:, :, :], 0.0, W=["qT"])
            S.call("dve", "memset", mixF[64:128, :, :], 0.0, W=["mixF"])
            S.call("dve", "memset", Vc[:, :, :, 64:65], 1.0,
                   W=["Vones"])

            for gi, tiles in enumerate(GROUPS):
                gpos0 = POS0[tiles[0]]
                gcols = {}
                cc = 0
                for t in tiles:
                    gcols[t] = cc
                    cc += NTOK[t]
                gw = cc
                segs = [(0, 16), (16, 528)] if gi == 0 else [(0, 512)]

                if gi == 0:
                    for t in tiles:
                        norm_to_T(t, xn, "xn", xnT, gcols[t], "xnT")

                def win_chunk(c0):
                    return wload(w_in_d[li, :, c0:c0 + 512].rearrange("(c p) n -> p c n", p=128), 512,
                                 lambda r: r[:, 0:4096].rearrange("p (c n) -> p c n", c=8))
                for j, t in enumerate(tiles):
                    n = NTOK[t]
                    col = gcols[t]
                    for k in range(8):
                        S.call("pe", "matmul", P6[0:n, 0:8], lhsT=xnT[:, k, col:col + n], rhs=wlog[:, k, :],
                               start=(k == 0), stop=(k == 7), R=["xnT", "wlog"], W=["P6"])
                    S.call("dve", "tensor_tensor", out=tl5[0:n, j, :], in0=P6[0:n, 0:8], in1=bfB[0:n, :], op=ALU.add,
                           R=["P6", "bfB"], W=[f"tl{j}"])
                for j, t in enumerate(tiles):
                    n = NTOK[t]
                    S.call("act", "activation", out=te5[0:n, j, :], in_=tl5[0:n, j, :], func=AF.Exp, scale=-1.0,
                           R=[f"tl{j}"], W=[f"te{j}"])
                for j, t in enumerate(tiles):
                    n = NTOK[t]
                    S.call("act", "activation", out=spv[0:n, j, :], in_=te5[0:n, j, :], func=AF.Ln,
                           bias=oneT[0:n, 0:1], R=[f"te{j}", "oneT"], W=[f"spv{j}"])
                for which, c0 in (("q", 0), ("k", 512)):
                    slot = win_chunk(c0)
                    wv = wview8(slot, 512)
                    for pair in range(4):
                        for (a, b) in segs:
                            u = cnt["u"] % 2
                            cnt["u"] += 1
                            w = b - a
                            for k in range(8):
                                S.call("pe", "matmul",
                                    P[u][:, 0:w], lhsT=wv[:, k, pair * 128:(pair + 1) * 128],
                                    rhs=xnT[:, k, a:b], start=(k == 0), stop=(k == 7),
                   R=["xnT", f"ring{slot}"], W=[f"P{u}"])
                            if which == "q":
                                S.call("dve", "tensor_scalar", out=qT[0:64, 2 * pair, a:b], in0=P[u][0:64, 0:w],
                                       scalar1=0.125, scalar2=None, op0=ALU.mult, R=[f"P{u}"], W=["qT"])
                                S.call("dve", "tensor_scalar", out=qT[64:128, 2 * pair + 1, a:b],
                                       in0=P[u][64:128, 0:w], scalar1=0.125, scalar2=None, op0=ALU.mult,
                                       R=[f"P{u}"], W=["qT"])
                            else:
                                S.call("dve", "tensor_copy", out=Kc[:, pair, gpos0 + a:gpos0 + b], in_=P[u][:, 0:w],
                                       R=[f"P{u}"], W=["Kc"])
                slot = win_chunk(1024)

                def emit_crow(t):
                    n = NTOK[t]
                    col = gcols[t]
                    S.call("pe", "matmul", P6[0:8, 16:16 + n], lhsT=negc[0:n, t, :], rhs=identf[0:n, 0:n],
                           start=True, stop=True, R=["negc", "identf"], W=["P6"])
                    S.call("act", "mul", out=crowT[0:8, col:col + n], in_=P6[0:8, 16:16 + n], mul=-1.0,
                           R=["P6"], W=["crowT"])

                for j, t in enumerate(tiles):
                    n = NTOK[t]
                    u = cnt["u"] % 2
                    cnt["u"] += 1
                    mm_tok(t, gcols[t], xnT, "xnT", slot, 512, P[u], f"P{u}")
                    S.call("dve", "tensor_copy", out=Vc[0:n, t, :, 0:64],
                           in_=P[u][0:n, :].rearrange("p (h e) -> p h e", h=8), R=[f"P{u}"], W=["Vc"])
                    S.call("pe", "matmul", P6[0:n, 8:16], lhsT=Umat[0:n, 0:n], rhs=spv[0:n, j, :],
                           start=True, stop=(t == 0), R=[f"spv{j}", "Umat"], W=["P6"])
                    if t > 0:
                        npv = NTOK[t - 1]
                        esel = 0 if t - 1 == 0 else 1
                        S.call("pe", "matmul", P6[0:n, 8:16], lhsT=Esel[0:npv, esel, 0:n], rhs=negc[0:npv, t - 1, :],
                               start=False, stop=True, R=["negc", "Esel"], W=["P6"])
                    S.call("dve", "tensor_copy", out=negc[0:n, t, :], in_=P6[0:n, 8:16], R=["P6"], W=["negc"])
                    if j > 0:
                        emit_crow(tiles[j - 1])
                emit_crow(tiles[-1])

                srq = win_chunk(1544)
                srk = win_chunk(2056)
                srv = win_chunk(2568)
                srg = win_chunk(3080)

                def ret_steps(t):
                    n = NTOK[t]
                    col = gcols[t]
                    rob = cnt["ro"] % 2
                    cnt["ro"] += 1
                    o4 = SC[1][0:n, :].rearrange("p (h e) -> p h e", h=4)

                    def rot(src, dec, dst, dstkey, srckey):
                        s4 = src[0:n, :].rearrange("p (h two e) -> p h two e", h=4, two=2)
                        x1 = s4[:, :, 0, :]
                        x2 = s4[:, :, 1, :]
                        cb = cosT[0:n, t, :].unsqueeze(1).broadcast_to([n, 4, 64])
                        sbb = sinT[0:n, t, :].unsqueeze(1).broadcast_to([n, 4, 64])
                        A4 = tmpA[0:n, :].rearrange("p (h two e) -> p h two e", h=4, two=2)
                        B4 = tmpB[0:n, :].rearrange("p (h two e) -> p h two e", h=4, two=2)
                        S.call("dve", "tensor_tensor", out=A4[:, :, 0, :], in0=x1, in1=cb, op=ALU.mult,
                               R=[srckey, "cosT"], W=["tmpA"])
                        S.call("dve", "tensor_tensor", out=A4[:, :, 1, :], in0=x1, in1=sbb, op=ALU.mult,
                               R=[srckey, "sinT"], W=["tmpA"])
                        S.call("dve", "tensor_tensor", out=B4[:, :, 0, :], in0=x2, in1=sbb, op=ALU.mult,
                               R=[srckey, "sinT"], W=["tmpB"])
                        S.call("dve", "tensor_tensor", out=B4[:, :, 1, :], in0=x2, in1=cb, op=ALU.mult,
                               R=[srckey, "cosT"], W=["tmpB"])
                        S.call("dve", "tensor_tensor", out=A4[:, :, 0, :], in0=A4[:, :, 0, :], in1=B4[:, :, 0, :],
                               op=ALU.subtract, R=["tmpA", "tmpB"], W=["tmpA"])
                        S.call("dve", "tensor_tensor", out=A4[:, :, 1, :], in0=A4[:, :, 1, :], in1=B4[:, :, 1, :],
                               op=ALU.add, R=["tmpA", "tmpB"], W=["tmpA"])
                        S.call("dve", "tensor_tensor",
                               out=dst[0:n, :].rearrange("p (h e) -> p h e", h=4),
                               in0=tmpA[0:n, :].rearrange("p (h e) -> p h e", h=4),
                               in1=dec[0:n, :].unsqueeze(2).broadcast_to([n, 4, 128]), op=ALU.mult,
                               R=["tmpA", "qdec", "kdec"], W=[dstkey])

                    def F0():
                        mm_tok(t, col, xnT, "xnT", srq, 512, P[0], "P0")
                        mm_tok(t, col, xnT, "xnT", srk, 512, P[1], "P1")
                        mm_tok(t, col, xnT, "xnT", srv, 512, SC[0], "SC0")
                        mm_tok(t, col, xnT, "xnT", srg, 512, OTp, "OT")

                    def F0b():
                        S.call("act", "copy", out=rvb[0:n, :], in_=SC[0][0:n, :], R=["SC0"], W=["rvb"])

                    def F1():
                        rot(P[0], qdec, qp, "qp", "P0")

                    def F3():
                        rot(P[1], kdec, kp, "kp", "P1")

                    def F4():
                        for c in range(4):
                            S.call("pe", "transpose", out=PTR[:, c, 0:n], in_=qp[0:n, c * 128:(c + 1) * 128],
                                   identity=identb[0:n, 0:n], R=["qp", "identb"], W=["PTR"])
                        for c in range(4):
                            S.call("pe", "transpose", out=PTR[:, 4 + c, 0:n], in_=kp[0:n, c * 128:(c + 1) * 128],
                                   identity=identb[0:n, 0:n], R=["kp", "identb"], W=["PTR"])
                        S.call("act", "copy", out=rqkT[:, :, 0:n], in_=PTR[:, :, 0:n], R=["PTR"], W=["rqkT"])

                    def F5():
                        for hh in range(4):
                            S.call("pe", "matmul", SC[0][0:n, hh * 128:hh * 128 + n],
                                   lhsT=rqkT[:, 4 + hh, 0:n], rhs=rqkT[:, hh, 0:n], start=True, stop=True,
                                   R=["rqkT"], W=["SC0"])
                        S.call("dve", "tensor_tensor", out=mT[0:n, :, 0:n],
                               in0=SC[0][0:n, :].rearrange("p (h e) -> p h e", h=4)[:, :, 0:n],
                               in1=caus01[0:n, 0:n].unsqueeze(1).broadcast_to([n, 4, n]), op=ALU.mult,
                               R=["SC0", "caus01"], W=["mT"])

                    def F6():
                        for hh in range(4):
                            S.call("pe", "matmul", SC[1][0:n, hh * 128:(hh + 1) * 128], lhsT=mT[0:n, hh, 0:n],
                                   rhs=rvb[0:n, hh * 128:(hh + 1) * 128], start=True, stop=(t == 0),
                                   R=["mT", "rvb"], W=["SC1"])
                            if t > 0:
                                S.call("pe", "matmul", SC[1][0:n, hh * 128:(hh + 1) * 128],
                                       lhsT=rqkT[:, hh, 0:n], rhs=Sbf[:, hh, :], start=False, stop=True,
                                       R=["rqkT", "Sbf"], W=["SC1"])
                        for hh in range(4):
                            S.call("pe", "matmul", P7[:, hh * 128:(hh + 1) * 128],
                                   lhsT=kp[0:n, hh * 128:(hh + 1) * 128], rhs=rvb[0:n, hh * 128:(hh + 1) * 128],
                                   start=True, stop=True, R=["kp", "rvb"], W=["P7"])

                    def F7():
                        S.call("act", "activation", out=tmpC[0:n, :], in_=OTp[0:n, :], func=AF.Silu,
                               R=["OT"], W=["tmpC"])
                        S.call("dve", "tensor_tensor", out=tmpC[0:n, :], in0=tmpC[0:n, :], in1=gnB[0:n, :],
                               op=ALU.mult, R=["tmpC", "gnB"], W=["tmpC"])

                    def B0():
                        S4v = S32[:].rearrange("p h e -> p (h e)")
                        if t == 0:
                            S.call("dve", "tensor_copy", out=S4v, in_=P7[:, :], R=["P7"], W=["S32"])
                        else:
                            S.call("dve", "tensor_tensor", out=S4v, in0=S4v, in1=P7[:, :], op=ALU.add,
                                   R=["P7", "S32"], W=["S32"])
                        gsel = 0 if t == 0 else 1
                        S.call("dve", "tensor_tensor", out=S32[:], in0=S32[:],
                               in1=gdecn[:, gsel, :].unsqueeze(2).broadcast_to([128, 4, 128]), op=ALU.mult,
                               R=["S32", "gdecn"], W=["S32"])
                        S.call("act", "copy", out=Sbf[:], in_=S32[:], R=["S32"], W=["Sbf"])

                    def B1():
                        S.call("dve", "tensor_reduce", out=st4[0:n, 0, :], in_=o4, axis=AX.X, op=ALU.add,
                               R=["SC1"], W=["st4"])
                        S.call("act", "activation", out=P6[0:n, :], in_=SC[1][0:n, :], func=AF.Square,
                               R=["SC1"], W=["P6"])

                    def B2():
                        S.call("dve", "tensor_reduce", out=st4[0:n, 1, :],
                               in_=P6[0:n, :].rearrange("p (h e) -> p h e", h=4), axis=AX.X, op=ALU.add,
                               R=["P6"], W=["st4"])
                        S.call("dve", "tensor_scalar", out=st4[0:n, 2, :], in0=st4[0:n, 0, :],
                               scalar1=1.0 / 128, scalar2=None, op0=ALU.mult, R=["st4"], W=["st4"])
                        S.call("dve", "tensor_tensor", out=st4[0:n, 3, :], in0=st4[0:n, 2, :], in1=st4[0:n, 2, :],
                               op=ALU.mult, R=["st4"], W=["st4"])
                        S.call("dve", "scalar_tensor_tensor", out=st4[0:n, 4, :], in0=st4[0:n, 1, :],
                               scalar=1.0 / 128, in1=st4[0:n, 3, :], op0=ALU.mult, op1=ALU.subtract,
                               R=["st4"], W=["st4"])
                        S.call("act", "activation", out=st4[0:n, 5, :], in_=st4[0:n, 4, :], func=AF.Sqrt,
                               bias=epsT[0:n, 0:1], scale=1.0, R=["st4", "epsT"], W=["st4"])

                    def B3():
                        S.call("dve", "reciprocal", out=st4[0:n, 5, :], in_=st4[0:n, 5, :], R=["st4"], W=["st4"])
                        S.call("dve", "scalar_tensor_tensor", out=st4[0:n, 3, :], in0=st4[0:n, 2, :], scalar=-1.0,
                               in1=st4[0:n, 5, :], op0=ALU.mult, op1=ALU.mult, R=["st4"], W=["st4"])
                        for hh in range(4):
                            S.call("act", "activation", out=P6[0:n, hh * 128:(hh + 1) * 128],
                                   in_=SC[1][0:n, hh * 128:(hh + 1) * 128], func=AF.Identity,
                                   scale=st4[0:n, 5, hh:hh + 1], bias=st4[0:n, 3, hh:hh + 1],
                                   R=["SC1", "st4"], W=["P6"])

                    def B4():
                        S.call("dve", "tensor_tensor", out=ROB[rob][0:n, :], in0=P6[0:n, :], in1=tmpC[0:n, :],
                               op=ALU.mult, R=["P6", "tmpC"], W=[ROK[rob]])

                    def B5():
                        for c in range(4):
                            S.call("pe", "transpose", out=PTR[:, c, 0:n], in_=ROB[rob][0:n, c * 128:(c + 1) * 128],
                                   identity=identb[0:n, 0:n], R=[ROK[rob], "identb"], W=["PTR"])
                        S.call("act", "copy", out=mixR[:, :, col:col + n], in_=PTR[:, 0:4, 0:n],
                               R=["PTR"], W=["mixR"])

                    return dict(F0=F0, F0b=F0b, F1=F1, F3=F3, F4=F4, F5=F5, F6=F6, F7=F7,
                                B0=B0, B1=B1, B2=B2, B3=B3, B4=B4, B5=B5)

                prev = None
                for t in tiles:
                    cur = ret_steps(t)
                    cur["F0"]()
                    if prev:
                        prev["B0"]()
                        prev["B1"]()
                    cur["F0b"]()
                    cur["F1"]()
                    if prev:
                        prev["B2"]()
                    cur["F3"]()
                    cur["F4"]()
                    if prev:
                        prev["B3"]()
                    cur["F5"]()
                    if prev:
                        prev["B4"]()
                    cur["F6"]()
                    cur["F7"]()
                    if prev:
                        prev["B5"]()
                    prev = cur
                prev["B0"]()
                tail_steps = [prev[k] for k in ("B1", "B2", "B3", "B4", "B5")]
                if gi == 0:
                    while tail_steps:
                        tail_steps.pop(0)()

                SCB = [SC[0], P[0], P[1]]
                SCK = ["SC0", "P0", "P1"]
                OTB = [OTp, P7]
                OTK = ["OT", "P7"]
                LOOK = 2
                n1_sched = {}
                att_ctr = [0]
                if gi + 1 < len(GROUPS):
                    nxt = GROUPS[gi + 1]
                    S.call("dve", "memset", ssg[:, :], 0.0, W=["ssg"])
                    for jj, t in enumerate(nxt):
                        S.call("act", "activation", out=xn[0][:, :], in_=H[:, t, :], func=AF.Square,
                               accum_out=ssg[:, jj:jj + 1], R=[f"H{t}"], W=["ssg", "xn0"])
                    S.call("act", "activation", out=rstdg[:, :], in_=ssg[:, :], func=AF.Sqrt,
                           scale=1.0 / D, bias=epsT[:, 0:1], R=["ssg", "epsT"], W=["rstdg"])
                    S.call("dve", "reciprocal", out=rstdg[:, :], in_=rstdg[:, :], R=["rstdg"], W=["rstdg"])

                    def n1_A(jj, t):
                        xi = jj % 2
                        S.call("dve", "scalar_tensor_tensor", out=xn[xi][:, :], in0=H[:, t, :],
                               scalar=rstdg[:, jj:jj + 1], in1=gB[:, :], op0=ALU.mult, op1=ALU.mult,
                               R=[f"H{t}", "rstdg", "gB"], W=[f"xn{xi}"])

                    def n1_B(jj, t):
                        xi = jj % 2
                        for c in range(8):
                            S.call("pe", "transpose", out=PTR[:, c, :], in_=xn[xi][:, c * 128:(c + 1) * 128],
                                   identity=identb[:, :], R=[f"xn{xi}", "identb"], W=["PTR"])
                        S.call("dve", "tensor_copy", out=xnT[:, :, 128 * jj:128 * jj + 128], in_=PTR[:, :, :],
                               R=["PTR"], W=["xnT"])

                    plan = [(0, "A", 0), (1, "A", 1), (5, "B", 0), (6, "A", 2), (10, "B", 1), (11, "A", 3),
                            (15, "B", 2), (20, "B", 3)]
                    for (at, kind, jj) in plan:
                        fn = n1_A if kind == "A" else n1_B
                        n1_sched.setdefault(at, []).append((fn, jj, nxt[jj]))

                def n1_tick():
                    for (fn, jj, tt) in n1_sched.pop(att_ctr[0], []):
                        fn(jj, tt)
                    att_ctr[0] += 1

                for (a, b) in segs:
                    w = b - a
                    pa0 = gpos0 + a
                    ktiles = [t for t in range(NT) if POS0[t] < pa0 + w]
                    items = [(h, kt) for h in range(8) for kt in ktiles]
                    geo = {}
                    for kt in ktiles:
                        nk = NTOK[kt]
                        kp0 = POS0[kt]
                        if kp0 < pa0:
                            units = [(a, b, False)]
                            lo = a
                        else:
                            kc = a + (kp0 - pa0)
                            units = [(kc, kc + nk, True)]
                            if kc + nk < b:
                                units.append((kc + nk, b, False))
                            lo = kc
                        geo[kt] = (nk, kp0, units, lo)

                    def emit_scores(i):
                        h, kt = items[i]
                        nk, kp0, units, lo = geo[kt]
                        pr0 = 0 if h % 2 == 0 else 64
                        pair = h // 2
                        u = i % 3
                        for (ua, ub, diag) in units:
                            S.call("pe", "matmul",
                                   SCB[u][0:nk, ua - a:ub - a], lhsT=Kc[:, pair, kp0:kp0 + nk],
                                   rhs=qT[:, h, ua:ub], start=True, stop=False,
                                   R=["Kc", "qT"], W=[SCK[u]])
                            S.call("pe", "matmul",
                                   SCB[u][0:nk, ua - a:ub - a], lhsT=selb[:, h, 0:nk],
                                   rhs=crowT[:, ua:ub], start=False, stop=(not diag),
                                   R=["selb", "crowT"], W=[SCK[u]])
                            if diag:
                                S.call("pe", "matmul",
                                       SCB[u][0:nk, ua - a:ub - a], lhsT=identb[0:nk, 0:nk],
                                       rhs=maskneg[0:nk, 0:nk], start=False, stop=True,
                                       R=["identb", "maskneg"], W=[SCK[u]])

                    def emit_rest(i):
                        h, kt = items[i]
                        nk, kp0, units, lo = geo[kt]
                        u = i % 3
                        pu = i % 2
                        ob = h % 2
                        S.call("act", "activation",
                               out=PTb[pu][0:nk, lo - a:b - a], in_=SCB[u][0:nk, lo - a:b - a], func=AF.Exp,
                               bias=negc[0:nk, kt, h:h + 1], scale=1.0,
                               R=[SCK[u], "negc"], W=[f"PT{pu}"])
                        S.call("pe", "matmul",
                               OTB[ob][:, lo - a:b - a],
                               lhsT=A[0:nk, vc_off + (kt * 8 + h) * 65:vc_off + (kt * 8 + h) * 65 + 128],
                               rhs=PTb[pu][0:nk, lo - a:b - a],
                               start=(kt == ktiles[0]), stop=(kt == ktiles[-1]), skip_group_check=True,
                               R=[f"PT{pu}", "Vc", "Vones"], W=[OTK[ob]])

                    def emit_norm(h):
                        ob = h % 2
                        S.call("dve", "reciprocal", out=tmpC[64:65, 0:w], in_=OTB[ob][64:65, 0:w],
                               R=[OTK[ob]], W=["tmpC"])
                        S.call("pe", "matmul", P6[0:64, 0:w], lhsT=onesf[64:65, 0:64], rhs=tmpC[64:65, 0:w],
                               start=True, stop=True, R=["tmpC", "onesf"], W=["P6"])
                        S.call("dve", "tensor_copy", out=tmpA[0:64, 0:w], in_=OTB[ob][0:64, 0:w],
                               R=[OTK[ob]], W=["tmpA"])
                        S.call("dve", "tensor_tensor",
                               out=mixF[0:64, h, a:b], in0=tmpA[0:64, 0:w], in1=P6[0:64, 0:w], op=ALU.mult,
                               R=["tmpA", "P6"], W=["mixF"])

                    pending = []
                    nit = len(items)
                    for i in range(nit + LOOK):
                        if i < nit:
                            emit_scores(i)
                        j = i - LOOK
                        if j >= 0:
                            emit_rest(j)
                            if items[j][1] == ktiles[-1]:
                                pending.append((j + min(3, len(ktiles)), items[j][0]))
                        while pending and pending[0][0] <= j:
                            emit_norm(pending.pop(0)[1])
                        if tail_steps and i % 2 == 1:
                            tail_steps.pop(0)()
                        n1_tick()
                    for _, hh in pending:
                        emit_norm(hh)
                while tail_steps:
                    tail_steps.pop(0)()
                for at in sorted(n1_sched):
                    for (fn, jj, tt) in n1_sched[at]:
                        fn(jj, tt)
                n1_sched.clear()

                sfa = wload(w_out_d[li, 0:256, :].rearrange("(h p) n -> p h n", p=64), 0,
                            lambda r: r[0:64, 0:4096].rearrange("p (c n) -> p c n", c=4))
                sfb = wload(w_out_d[li, 256:512, :].rearrange("(h p) n -> p h n", p=64), 0,
                            lambda r: r[0:64, 0:4096].rearrange("p (c n) -> p c n", c=4))
                srt = wload(w_out_d[li, 512:1024, :].rearrange("(c p) n -> p c n", p=128), 0,
                            lambda r: r[:, 0:4096].rearrange("p (c n) -> p c n", c=4))
                wfa = ring[sfa][:, 0:4096].rearrange("p (c n) -> p c n", c=4)
                wfb = ring[sfb][:, 0:4096].rearrange("p (c n) -> p c n", c=4)
                wrt = ring[srt][:, 0:4096].rearrange("p (c n) -> p c n", c=4)
                for t in tiles:
                    n = NTOK[t]
                    col = gcols[t]
                    for half in range(2):
                        yb = cnt["y"] % 3
                        cnt["y"] += 1
                        ybk = ["OT", "P6", "P7"][yb]
                        hs = slice(half * 512, (half + 1) * 512)
                        for h in range(8):
                            wsrc = wfa if h < 4 else wfb
                            sk = sfa if h < 4 else sfb
                            S.call("pe", "matmul",
                                YB[yb][0:n, :], lhsT=mixF[:, h, col:col + n], rhs=wsrc[:, h % 4, hs],
                                start=(h == 0), stop=False,
                   R=["mixF", f"ring{sk}"], W=[ybk])
                        for c in range(4):
                            S.call("pe", "matmul",
                                YB[yb][0:n, :], lhsT=mixR[:, c, col:col + n], rhs=wrt[:, c, hs],
                                start=False, stop=(c == 3),
                   R=["mixR", f"ring{srt}"], W=[ybk])
                        S.call("dve", "tensor_tensor",
                            out=H[0:n, t, hs], in0=H[0:n, t, hs], in1=YB[yb][0:n, :], op=ALU.add,
                   R=[ybk, f"H{t}"], W=[f"H{t}"])
                    S.call("dve", "memset", rstd2[0:n, t:t + 1], 0.0, W=[f"rs2_{t}"])
                    S.call("act", "activation", out=xn[1][0:n, :], in_=H[0:n, t, :], func=AF.Square,
                           accum_out=rstd2[0:n, t:t + 1], R=[f"H{t}"], W=[f"rs2_{t}", "xn1"])
                    S.call("act", "activation", out=rstd2[0:n, t:t + 1], in_=rstd2[0:n, t:t + 1], func=AF.Sqrt,
                           scale=1.0 / D, bias=epsT[0:n, 0:1], R=[f"rs2_{t}", "epsT"], W=[f"rs2_{t}"])
                    S.call("dve", "reciprocal", out=rstd2[0:n, t:t + 1], in_=rstd2[0:n, t:t + 1],
                           R=[f"rs2_{t}"], W=[f"rs2_{t}"])

        def phase2(s, li):
            S.dma("sp", "c_gB", gB[:], ffn_norm_d[li:li + 1, :].broadcast_to([128, D]), writes=["gB"])
            segs = [(0, 16), (16, 528), (528, 1040), (1040, 1552), (1552, 2064)]
            seg_tiles = [[0], [1, 2, 3, 4], [5, 6, 7, 8], [9, 10, 11, 12], [13, 14, 15, 16]]

            def norm_seg(si):
                for t in seg_tiles[si]:
                    norm_to_T(t, xn2, "xn2_", hnT, POS0[t], f"hnT{si}", pre=True)

            norm_seg(0)
            norm_seg(1)
            first_fg = True
            f0 = 0
            while f0 < DFF:
                fw = min(512, DFF - f0)
                nfc = fw // 128
                sg = wload(w_gate_d[li, :, f0:f0 + fw].rearrange("(c p) n -> p c n", p=128), fw,
                           lambda r, fw=fw: r[:, 0:8 * fw].rearrange("p (c n) -> p c n", c=8))
                su = wload(w_up_d[li, :, f0:f0 + fw].rearrange("(c p) n -> p c n", p=128), fw,
                           lambda r, fw=fw: r[:, 0:8 * fw].rearrange("p (c n) -> p c n", c=8))
                sd = wload(w_down_d[li, f0:f0 + fw, :].rearrange("(c p) n -> p c n", p=128), fw,
                           lambda r, nfc=nfc: r[:, 0:nfc * 1024].rearrange("p (c n) -> p c n", c=nfc))
                wg = wview8(sg, fw)
                wu = wview8(su, fw)
                wd = ring[sd][:, 0:nfc * 1024].rearrange("p (c n) -> p c n", c=nfc)
                down_pending = None
                for si, (a, b) in enumerate(segs):
                    w = b - a
                    ab = cnt["u"] % 2
                    cnt["u"] += 1
                    if first_fg and si + 2 < len(segs):
                        norm_seg(si + 2)
                    for fc in range(nfc):
                        for k in range(8):
                            S.call("pe", "matmul",
                                P[0][:, 0:w], lhsT=wg[:, k, fc * 128:(fc + 1) * 128], rhs=hnT[:, k, a:b],
                                start=(k == 0), stop=(k == 7),
                   R=[f"hnT{si}", f"ring{sg}"], W=["P0"])
                        for k in range(8):
                            S.call("pe", "matmul",
                                P[1][:, 0:w], lhsT=wu[:, k, fc * 128:(fc + 1) * 128], rhs=hnT[:, k, a:b],
                                start=(k == 0), stop=(k == 7),
                   R=[f"hnT{si}", f"ring{su}"], W=["P1"])
                        S.call("act", "activation", out=tmpA[:, 0:w], in_=P[0][:, 0:w], func=AF.Silu,
                   R=["P0"], W=["tmpA"])
                        S.call("dve", "tensor_tensor",
                            out=actT[ab][:, fc, 0:w], in0=tmpA[:, 0:w], in1=P[1][:, 0:w], op=ALU.mult,
                   R=["tmpA", "P1"], W=[f"actT{ab}"])
                    def emit_down(si=si, a=a, ab=ab):
                        for t in seg_tiles[si]:
                            n = NTOK[t]
                            tc0 = POS0[t] - a
                            for half in range(2):
                                yb = cnt["y"] % 3
                                cnt["y"] += 1
                                ybk = ["OT", "P6", "P7"][yb]
                                hs = slice(half * 512, (half + 1) * 512)
                                for fc in range(nfc):
                                    S.call("pe", "matmul",
                                           YB[yb][0:n, :], lhsT=actT[ab][:, fc, tc0:tc0 + n], rhs=wd[:, fc, hs],
                                           start=(fc == 0), stop=(fc == nfc - 1),
                                           R=[f"actT{ab}", f"ring{sd}"], W=[ybk])
                                S.call("dve", "tensor_tensor",
                                       out=H[0:n, t, hs], in0=H[0:n, t, hs], in1=YB[yb][0:n, :], op=ALU.add,
                                       R=[ybk, f"H{t}"], W=[f"H{t}"])

                    if down_pending is not None:
                        down_pending()
                    down_pending = emit_down
                if down_pending is not None:
                    down_pending()
                    down_pending = None
                f0 += fw
                first_fg = False

        preloaded = False
        for s in range(2):
            if preloaded:
                pass
            elif first:
                S.dma("sp", "ldx", H[:, 1:NT, :], x_d[s].rearrange("(t p) d -> p t d", p=128),
                      writes=[f"H{t}" for t in range(1, NT)])
                S.dma("sp", "ldx", H[0:16, 0, :], meta_d, writes=["H0"])
            else:
                S.dma("sp", "ldx", H[:, 1:NT, :], hin_d[s, 16:L, :].rearrange("(t p) d -> p t d", p=128),
                      writes=[f"H{t}" for t in range(1, NT)])
                S.dma("sp", "ldx", H[0:16, 0, :], hin_d[s, 0:16, :], writes=["H0"])
            for li in layer_ids:
                phase1(s, li, False)
                S.barrier()
                phase2(s, li)
                S.barrier()
            if last:
                S.dma("sp", "c_gB", gB[:], final_norm_d[0:1, :].broadcast_to([128, D]), writes=["gB"])
                for t in range(1, NT):
                    p = cnt["n"] % 2
                    cnt["n"] += 1
                    norm_stats(t, p, xn[p], f"xn{p}")
                    ob = [tmpA, tmpB][p]
                    obk = ["tmpA", "tmpB"][p]
                    for half in range(2):
                        hs = slice(half * 512, (half + 1) * 512)
                        S.call("dve", "scalar_tensor_tensor",
                            out=ob[:, :], in0=H[:, t, hs], scalar=rstd[p][:, 0:1], in1=gB[:, hs],
                            op0=ALU.mult, op1=ALU.mult,
                   R=[f"H{t}", f"rstd{p}", "gB"], W=[obk])
                        S.dma("sp", "st", out_d[s, (t - 1) * 128:t * 128, hs], ob[:, :], reads=[obk])
                if first and s == 0:
                    S.dma("pool", "ldx2", H[0:16, 0, :], meta_d, writes=["H0"])
                    for t in range(1, NT):
                        S.dma("pool", "ldx2", H[:, t, :], x_d[1, (t - 1) * 128:t * 128, :], writes=[f"H{t}"])
                    preloaded = True
            else:
                S.dma("sp", "st", out_d[s, 16:L, :].rearrange("(t p) d -> p t d", p=128), H[:, 1:NT, :],
                      reads=[f"H{t}" for t in range(1, NT)])
                S.dma("sp", "st", out_d[s, 0:16, :], H[0:16, 0, :], reads=["H0"])
            S.barrier()
        S.final_wait("sp", ["st"])
        n_ops = {e: len(S.ops[e]) for e in S.ENG}
        print("ops per engine", n_ops)
        S.emit(st)
    return nc


_CACHE = {}


def _get_prog(key):
    if key not in _CACHE:
        _CACHE[key] = build(*key)
    return _CACHE[key]


def kernel(x, meta_tokens, attn_norm, w_in, b_fgate, ret_gn, w_out, ffn_norm, w_gate, w_up, w_down, final_norm):
    f = lambda a: np.ascontiguousarray(np.asarray(a, dtype=np.float32))
    consts = make_consts()
    shared = {
        "attn_norm": f(attn_norm), "w_in": f(w_in), "b_fgate": f(b_fgate), "ret_gn": f(ret_gn),
        "w_out": f(w_out), "ffn_norm": f(ffn_norm), "w_gate": f(w_gate), "w_up": f(w_up),
        "w_down": f(w_down), "final_norm": f(final_norm).reshape(1, D),
    }
    shared.update(consts)
    x = f(x)
    meta = f(meta_tokens)
    cores = list(range(NCORES))
    if MODE == "fused":
        nc = _get_prog(((0, 1), True, True))
        in_maps = [dict(shared, x=x[2 * c:2 * c + 2], meta=meta) for c in cores]
        res = run_bass_kernel_spmd(nc, in_maps, core_ids=cores)
        return np.concatenate([r["out"] for r in res.results], axis=0)
    else:
        nc0 = _get_prog(((0,), True, False))
        in_maps = [dict(shared, x=x[2 * c:2 * c + 2], meta=meta) for c in cores]
        res0 = run_bass_kernel_spmd(nc0, in_maps, core_ids=cores)
        nc1 = _get_prog(((1,), False, True))
        in_maps = [dict(shared, hin=res0.results[c]["hout"]) for c in cores]
        res1 = run_bass_kernel_spmd(nc1, in_maps, core_ids=cores)
        return np.concatenate([r["out"] for r in res1.results], axis=0)
```
